# Optimizing a Trainium2 kernel written in Bass

```python
import math
import jax, jax.numpy as jnp
from jax import lax
import numpy as np

D_MODEL = 1024
BATCH = 16
SEQ = 2048
DEPTH = 2
DEC_BATCH = 8
DEC_SEQ = 64
PAST_LEN = 1024

CHUNK = 64
N_AB = (DEPTH + 1) // 2
N_CD = DEPTH // 2
EPS = 1e-6
NEG_INF = -1e30
H_A = 4
DK_A = 128
DV_A = 128
ROPE_BASE = 10000.0
H_B = 4
DH_B = 64
DV_B = 2 * DH_B
T5_BUCKETS = 32
T5_MAX_DIST = 128
Q_BLOCK = 128
H_C = 8
DH_C = 64
BAND_CHUNKS = 8
C_WINDOW = BAND_CHUNKS * CHUNK
REL_CLIP = 128
H_D = 8
P_D = 64
D_INNER = H_D * P_D
G_D = 2
N_D = 128
CONV_D = 4
CONV_DIM_D = D_INNER + 2 * G_D * N_D
D_FF = 2816
CONV_F = 3
AB_SPLITS = (H_A * DK_A, H_A * DK_A, H_A * DV_A, H_A * DV_A, H_B * 2 * DH_B, H_B * 2 * DH_B, H_B * DV_B)
D_IN_AB = sum(AB_SPLITS)
D_OUT_AB = H_A * DV_A + H_B * DV_B
CD_SPLITS = (H_C * DH_C, H_C * DH_C, H_C * DH_C, D_INNER, CONV_DIM_D, H_D)
D_IN_CD = sum(CD_SPLITS)
D_OUT_CD = H_C * DH_C + D_INNER

kernel_name = 'streaming_hybrid_retention_diffattn_band_ssd'


def split_cols(y, sizes):
    out, start = [], 0
    for s in sizes:
        out.append(y[..., start:start + s])
        start += s
    return out


def to_chunks(t, size=CHUNK):
    b, l = t.shape[:2]
    return jnp.moveaxis(t.reshape((b, l // size, size) + t.shape[2:]), 1, 0)


def from_chunks(t):
    n, b, s = t.shape[:3]
    return jnp.moveaxis(t, 0, 1).reshape((b, n * s) + t.shape[3:])


def rmsnorm(x, g):
    xf = x.astype(jnp.float32)
    y = xf * lax.rsqrt(jnp.mean(xf * xf, axis=-1, keepdims=True) + EPS)
    return y.astype(x.dtype) * g


def head_groupnorm(x, g):
    xf = x.astype(jnp.float32)
    mu = jnp.mean(xf, axis=-1, keepdims=True)
    var = jnp.mean(jnp.square(xf - mu), axis=-1, keepdims=True)
    y = ((xf - mu) * lax.rsqrt(var + EPS)).astype(x.dtype)
    return y.reshape(x.shape[:-2] + (-1,)) * g


def rotary(x, pos):
    half = x.shape[-1] // 2
    inv = jnp.power(ROPE_BASE, -jnp.arange(half, dtype=jnp.float32) / half)
    ang = pos.astype(jnp.float32)[:, None] * inv[None, :]
    cos = jnp.cos(ang)[:, None, :]
    sin = jnp.sin(ang)[:, None, :]
    x1 = x[..., :half].astype(jnp.float32)
    x2 = x[..., half:].astype(jnp.float32)
    return jnp.concatenate([x1 * cos - x2 * sin, x1 * sin + x2 * cos], axis=-1).astype(x.dtype)


def causal_dwconv(x, past, w, b):
    k = w.shape[0]
    seq = x.shape[1]
    xp = jnp.concatenate([past.astype(x.dtype), x], axis=1)
    y = xp[:, 0:seq] * w[0]
    for i in range(1, k):
        y = y + xp[:, i:i + seq] * w[i]
    return y + b, xp[:, seq:]


def t5_bias(q_pos, k_pos, table):
    rel = k_pos[None, :] - q_pos[:, None]
    half = T5_BUCKETS // 2
    max_exact = half // 2
    base = jnp.where(rel > 0, half, 0)
    n = jnp.abs(rel)
    nf = jnp.maximum(n, 1).astype(jnp.float32)
    large = max_exact + (jnp.log(nf / max_exact) / math.log(T5_MAX_DIST / max_exact) * (half - max_exact)).astype(jnp.int32)
    large = jnp.minimum(large, half - 1)
    bucket = base + jnp.where(n < max_exact, n, large)
    return jnp.moveaxis(table[bucket], -1, 0)


def retention_chunk(state, q, k, v, log_gamma):
    seq = q.shape[1]
    idx = jnp.arange(seq, dtype=jnp.float32)
    diff = idx[:, None] - idx[None, :]
    decay = jnp.where(diff[None] >= 0, jnp.exp(jnp.maximum(diff, 0.0)[None] * log_gamma[:, None, None]), 0.0)
    s = jnp.einsum('blhd,bmhd->bhlm', q, k) * decay[None]
    intra = jnp.einsum('bhlm,bmhe->blhe', s, v)
    q_dec = jnp.exp((idx + 1.0)[:, None] * log_gamma[None, :])
    cross = jnp.einsum('blhd,bhde->blhe', q * q_dec[None, :, :, None], state)
    k_dec = jnp.exp((seq - 1.0 - idx)[:, None] * log_gamma[None, :])
    new_state = jnp.exp(seq * log_gamma)[None, :, None, None] * state + jnp.einsum('blhd,blhe->bhde', k * k_dec[None, :, :, None], v)
    return new_state, intra + cross


def ssd_chunk(state, x, dt, da, bm, cm):
    seq = x.shape[1]
    cum = jnp.cumsum(da, axis=1)
    causal = jnp.tril(jnp.ones((seq, seq), dtype=bool))[None, :, :, None]
    seg = cum[:, :, None, :] - cum[:, None, :, :]
    decay = jnp.exp(jnp.where(causal, seg, -jnp.inf))
    cb = jnp.einsum('blhn,bmhn->blmh', cm, bm)
    y_in = jnp.einsum('blmh,bmhp->blhp', cb * decay * dt[:, None, :, :], x)
    y_x = jnp.einsum('blhn,bhpn->blhp', cm, state) * jnp.exp(cum)[..., None]
    last = cum[:, -1:, :]
    w = jnp.exp(last - cum) * dt
    new_state = jnp.exp(last[:, 0])[:, :, None, None] * state + jnp.einsum('blh,blhp,blhn->bhpn', w, x, bm)
    return new_state, y_in + y_x


def diff_attention(q, k, v, q_pos, k_pos, t5_table, lam):
    bias = t5_bias(q_pos, k_pos, t5_table).astype(jnp.float32)
    mask = (k_pos[None, :] // CHUNK) <= (q_pos[:, None] // CHUNK)
    s = jnp.einsum('bqhid,bkhid->bhiqk', q, k).astype(jnp.float32) * (DH_B ** -0.5) + bias[None, :, None]
    p = jax.nn.softmax(jnp.where(mask, s, NEG_INF), axis=-1)
    attn = p[:, :, 0] - lam * p[:, :, 1]
    return jnp.einsum('bhqk,bkhe->bqhe', attn.astype(v.dtype), v)


def band_attention(q, k, v, q_pos, k_pos, rel_table):
    rel = q_pos[:, None] - k_pos[None, :]
    bias = jnp.moveaxis(rel_table[jnp.clip(rel, -REL_CLIP, REL_CLIP) + REL_CLIP], -1, 0).astype(jnp.float32)
    dc = q_pos[:, None] // CHUNK - k_pos[None, :] // CHUNK
    mask = (k_pos[None, :] >= 0) & (dc >= 0) & (dc <= BAND_CHUNKS)
    s = jnp.einsum('bqhd,bkhd->bhqk', q, k).astype(jnp.float32) * (DH_C ** -0.5) + bias[None]
    p = jax.nn.softmax(jnp.where(mask, s, NEG_INF), axis=-1)
    return jnp.einsum('bhqk,bkhd->bqhd', p.astype(v.dtype), v)


def mixer_ab(h, pos, layer, w_in, w_out, ret_gn, lam_q, lam_k, diff_gn, t5_table, ret_past, k_past, v_past):
    bsz, seq, _ = h.shape
    qa, ka, va, ga, qb, kb, vb = split_cols(h @ w_in, AB_SPLITS)
    qa = rotary(qa.reshape(bsz, seq, H_A, DK_A), pos)
    ka = rotary(ka.reshape(bsz, seq, H_A, DK_A), pos) * (DK_A ** -0.5)
    va = va.reshape(bsz, seq, H_A, DV_A)
    log_gamma = jnp.log1p(-jnp.exp2(-5.0 - jnp.arange(H_A, dtype=jnp.float32)))
    if ret_past is None:
        s0 = jnp.zeros((bsz, H_A, DK_A, DV_A), jnp.float32)
        s_new, o_a = lax.scan(lambda s, t: retention_chunk(s, t[0], t[1], t[2], log_gamma), s0,
                              (to_chunks(qa), to_chunks(ka), to_chunks(va)))
        o_a = from_chunks(o_a)
    else:
        s_new, o_a = retention_chunk(ret_past.astype(jnp.float32), qa, ka, va, log_gamma)
    o_a = head_groupnorm(o_a.astype(h.dtype), ret_gn) * jax.nn.silu(ga)
    qb = qb.reshape(bsz, seq, H_B, 2, DH_B)
    kb = kb.reshape(bsz, seq, H_B, 2, DH_B)
    vb = vb.reshape(bsz, seq, H_B, DV_B)
    lam_init = 0.8 - 0.6 * math.exp(-0.3 * layer)
    lq = lam_q.astype(jnp.float32)
    lk = lam_k.astype(jnp.float32)
    lam = jnp.exp(jnp.sum(lq[0] * lk[0])) - jnp.exp(jnp.sum(lq[1] * lk[1])) + lam_init
    if k_past is None:
        o_b = lax.map(lambda t: diff_attention(t[0], kb, vb, t[1], pos, t5_table, lam),
                      (to_chunks(qb, Q_BLOCK), pos.reshape(-1, Q_BLOCK)))
        o_b = from_chunks(o_b)
    else:
        past = k_past.shape[1]
        k_all = jnp.concatenate([k_past.reshape(bsz, past, H_B, 2, DH_B).astype(kb.dtype), kb], axis=1)
        v_all = jnp.concatenate([v_past.astype(vb.dtype), vb], axis=1)
        k_pos = jnp.arange(past + seq, dtype=jnp.int32)
        o_b = diff_attention(qb, k_all, v_all, pos, k_pos, t5_table, lam)
    o_b = rmsnorm(o_b, diff_gn) * (1.0 - lam_init)
    mixed = jnp.concatenate([o_a, o_b.reshape(bsz, seq, H_B * DV_B)], axis=-1)
    return mixed @ w_out, s_new, kb.reshape(bsz, seq, H_B, 2 * DH_B), vb


def mixer_cd(h, pos, w_in, w_out, rel_table, conv_w, conv_b, dt_bias, a_log, d_skip, norm_g_d, k_past, v_past, conv_past, ssm_past):
    bsz, seq, _ = h.shape
    qc, kc, vc, z, xbc, dt = split_cols(h @ w_in, CD_SPLITS)
    qc = qc.reshape(bsz, seq, H_C, DH_C)
    kc = kc.reshape(bsz, seq, H_C, DH_C)
    vc = vc.reshape(bsz, seq, H_C, DH_C)
    if k_past is None:
        band = C_WINDOW + CHUNK
        kpad = jnp.pad(kc, ((0, 0), (C_WINDOW, 0), (0, 0), (0, 0)))
        vpad = jnp.pad(vc, ((0, 0), (C_WINDOW, 0), (0, 0), (0, 0)))

        def one_chunk(t):
            q_blk, start = t
            k_blk = lax.dynamic_slice_in_dim(kpad, start, band, axis=1)
            v_blk = lax.dynamic_slice_in_dim(vpad, start, band, axis=1)
            k_pos = start - C_WINDOW + jnp.arange(band, dtype=jnp.int32)
            q_pos = start + jnp.arange(CHUNK, dtype=jnp.int32)
            return band_attention(q_blk, k_blk, v_blk, q_pos, k_pos, rel_table)

        starts = jnp.arange(seq // CHUNK, dtype=jnp.int32) * CHUNK
        o_c = from_chunks(lax.map(one_chunk, (to_chunks(qc), starts)))
        keep = min(C_WINDOW, seq)
        new_k, new_v = kc[:, seq - keep:], vc[:, seq - keep:]
    else:
        w = k_past.shape[1]
        k_all = jnp.concatenate([k_past.astype(kc.dtype), kc], axis=1)
        v_all = jnp.concatenate([v_past.astype(vc.dtype), vc], axis=1)
        k_pos = PAST_LEN - w + jnp.arange(w + seq, dtype=jnp.int32)
        o_c = band_attention(qc, k_all, v_all, pos, k_pos, rel_table)
        new_k, new_v = kc, vc
    if conv_past is None:
        conv_past = jnp.zeros((bsz, CONV_D - 1, CONV_DIM_D), h.dtype)
    xbc, new_conv = causal_dwconv(xbc, conv_past, conv_w, conv_b)
    xbc = jax.nn.silu(xbc)
    xs, bm, cm = split_cols(xbc, (D_INNER, G_D * N_D, G_D * N_D))
    xs = xs.reshape(bsz, seq, H_D, P_D)
    rep = H_D // G_D
    bm = jnp.repeat(bm.reshape(bsz, seq, G_D, N_D), rep, axis=2)
    cm = jnp.repeat(cm.reshape(bsz, seq, G_D, N_D), rep, axis=2)
    dt = jax.nn.softplus(dt.astype(jnp.float32) + dt_bias.astype(jnp.float32))
    da = dt * (-jnp.exp(a_log.astype(jnp.float32)))
    if ssm_past is None:
        h0 = jnp.zeros((bsz, H_D, P_D, N_D), jnp.float32)
        ssm_new, y_d = lax.scan(lambda s, t: ssd_chunk(s, t[0], t[1], t[2], t[3], t[4]), h0,
                                (to_chunks(xs), to_chunks(dt), to_chunks(da), to_chunks(bm), to_chunks(cm)))
        y_d = from_chunks(y_d)
    else:
        ssm_new, y_d = ssd_chunk(ssm_past.astype(jnp.float32), xs, dt, da, bm, cm)
    y_d = (y_d + d_skip.astype(jnp.float32)[:, None] * xs).astype(h.dtype).reshape(bsz, seq, D_INNER)
    y_d = y_d * jax.nn.silu(z)
    y_d = rmsnorm(y_d.reshape(bsz, seq, G_D, D_INNER // G_D), norm_g_d.reshape(G_D, D_INNER // G_D)).reshape(bsz, seq, D_INNER)
    mixed = jnp.concatenate([o_c.reshape(bsz, seq, H_C * DH_C), y_d], axis=-1)
    return mixed @ w_out, new_k, new_v, new_conv, ssm_new


def conv_ffn(h, w_up, conv_w, conv_b, w_down, past):
    bsz = h.shape[0]
    u = h @ w_up
    a, g = u[..., :D_FF], u[..., D_FF:]
    if past is None:
        past = jnp.zeros((bsz, CONV_F - 1, D_FF), h.dtype)
    g, new_past = causal_dwconv(g, past, conv_w, conv_b)
    return (a * jax.nn.gelu(g)) @ w_down, new_past


def setup_inputs(seed: int = 0) -> dict:
    key = jax.random.key(seed)
    ks = iter(jax.random.split(key, 48))

    def nrm(shape, scale):
        return jax.random.normal(next(ks), shape, jnp.float32) * scale

    c_cache = min(C_WINDOW, PAST_LEN)
    dt0 = jnp.exp(jax.random.uniform(next(ks), (N_CD, H_D), jnp.float32, math.log(1e-3), math.log(1e-1)))
    return {
        'x_prompt': nrm((BATCH, SEQ, D_MODEL), 1.0),
        'x_sample': nrm((DEC_BATCH, DEC_SEQ, D_MODEL), 1.0),
        'cache_ret_state': nrm((N_AB, DEC_BATCH, H_A, DK_A, DV_A), 0.1),
        'cache_b_k': nrm((N_AB, DEC_BATCH, PAST_LEN, H_B, 2 * DH_B), 1.0),
        'cache_b_v': nrm((N_AB, DEC_BATCH, PAST_LEN, H_B, DV_B), 1.0),
        'cache_c_k': nrm((N_CD, DEC_BATCH, c_cache, H_C, DH_C), 1.0),
        'cache_c_v': nrm((N_CD, DEC_BATCH, c_cache, H_C, DH_C), 1.0),
        'state_d_conv': nrm((N_CD, DEC_BATCH, CONV_D - 1, CONV_DIM_D), 1.0),
        'state_d_ssm': nrm((N_CD, DEC_BATCH, H_D, P_D, N_D), 0.1),
        'state_ffn_conv': nrm((DEPTH, DEC_BATCH, CONV_F - 1, D_FF), 1.0),
        'c_prompt': nrm((BATCH, D_MODEL), 1.0),
        'c_sample': nrm((DEC_BATCH, D_MODEL), 1.0),
        'w_mod': nrm((DEPTH, D_MODEL, 6 * D_MODEL), 0.5 * D_MODEL ** -0.5),
        'b_mod': nrm((DEPTH, 6 * D_MODEL), 0.02),
        'norm_g': 1.0 + nrm((DEPTH, 2, D_MODEL), 0.1),
        'final_g': 1.0 + nrm((D_MODEL,), 0.1),
        't5_table': nrm((T5_BUCKETS, H_B), 0.5),
        'w_in_ab': nrm((N_AB, D_MODEL, D_IN_AB), D_MODEL ** -0.5),
        'w_out_ab': nrm((N_AB, D_OUT_AB, D_MODEL), D_OUT_AB ** -0.5),
        'ret_gn': 1.0 + nrm((N_AB, H_A * DV_A), 0.1),
        'lam_q': nrm((N_AB, 2, DH_B), 0.1),
        'lam_k': nrm((N_AB, 2, DH_B), 0.1),
        'diff_gn': 1.0 + nrm((N_AB, DV_B), 0.1),
        'w_in_cd': nrm((N_CD, D_MODEL, D_IN_CD), D_MODEL ** -0.5),
        'w_out_cd': nrm((N_CD, D_OUT_CD, D_MODEL), D_OUT_CD ** -0.5),
        'rel_table': nrm((N_CD, 2 * REL_CLIP + 1, H_C), 0.5),
        'd_conv_w': nrm((N_CD, CONV_D, CONV_DIM_D), 0.5),
        'd_conv_b': nrm((N_CD, CONV_DIM_D), 0.02),
        'd_dt_bias': dt0 + jnp.log(-jnp.expm1(-dt0)),
        'd_a_log': jnp.log(jax.random.uniform(next(ks), (N_CD, H_D), jnp.float32, 1.0, 16.0)),
        'd_skip': 1.0 + nrm((N_CD, H_D), 0.1),
        'd_norm_g': 1.0 + nrm((N_CD, D_INNER), 0.1),
        'w_up': nrm((DEPTH, D_MODEL, 2 * D_FF), D_MODEL ** -0.5),
        'ffn_conv_w': nrm((DEPTH, CONV_F, D_FF), 0.5),
        'ffn_conv_b': nrm((DEPTH, D_FF), 0.02),
        'w_down': nrm((DEPTH, D_FF, D_MODEL), D_FF ** -0.5),
    }


def reference(x_prompt, x_sample, cache_ret_state, cache_b_k, cache_b_v, cache_c_k, cache_c_v,
              state_d_conv, state_d_ssm, state_ffn_conv, c_prompt, c_sample,
              w_mod, b_mod, norm_g, final_g, t5_table,
              w_in_ab, w_out_ab, ret_gn, lam_q, lam_k, diff_gn,
              w_in_cd, w_out_cd, rel_table, d_conv_w, d_conv_b, d_dt_bias, d_a_log, d_skip, d_norm_g,
              w_up, ffn_conv_w, ffn_conv_b, w_down):

    def trunk(x, c, pos, sample):
        s_ret, s_bk, s_bv, s_ck, s_cv, s_dconv, s_dssm, s_ffn = ([] for _ in range(8))
        c_act = jax.nn.silu(c)
        for l in range(DEPTH):
            i = l // 2
            mod = c_act @ w_mod[l] + b_mod[l]
            sh1, sc1, g1, sh2, sc2, g2 = [m[:, None, :] for m in jnp.split(mod, 6, axis=-1)]
            h = rmsnorm(x, norm_g[l, 0]) * (1.0 + sc1) + sh1
            if l % 2 == 0:
                out, st, nk, nv = mixer_ab(
                    h, pos, l, w_in_ab[i], w_out_ab[i], ret_gn[i], lam_q[i], lam_k[i], diff_gn[i], t5_table,
                    cache_ret_state[i] if sample else None,
                    cache_b_k[i] if sample else None,
                    cache_b_v[i] if sample else None)
                s_ret.append(st)
                s_bk.append(nk)
                s_bv.append(nv)
            else:
                out, nk, nv, nconv, nssm = mixer_cd(
                    h, pos, w_in_cd[i], w_out_cd[i], rel_table[i], d_conv_w[i], d_conv_b[i], d_dt_bias[i],
                    d_a_log[i], d_skip[i], d_norm_g[i],
                    cache_c_k[i] if sample else None,
                    cache_c_v[i] if sample else None,
                    state_d_conv[i] if sample else None,
                    state_d_ssm[i] if sample else None)
                s_ck.append(nk)
                s_cv.append(nv)
                s_dconv.append(nconv)
                s_dssm.append(nssm)
            x = x + g1 * out
            h = rmsnorm(x, norm_g[l, 1]) * (1.0 + sc2) + sh2
            f, nf = conv_ffn(h, w_up[l], ffn_conv_w[l], ffn_conv_b[l], w_down[l],
                             state_ffn_conv[l] if sample else None)
            x = x + g2 * f
            s_ffn.append(nf)
        y = rmsnorm(x, final_g)
        stk = lambda t: jnp.stack(t).astype(x.dtype)
        return (y, stk(s_ret), stk(s_bk), stk(s_bv), stk(s_ck), stk(s_cv), stk(s_dconv), stk(s_dssm), stk(s_ffn))

    pos_p = jnp.arange(x_prompt.shape[1], dtype=jnp.int32)
    pos_s = PAST_LEN + jnp.arange(x_sample.shape[1], dtype=jnp.int32)
    y_prompt, ret_p, bk_p, bv_p, ck_p, cv_p, dconv_p, dssm_p, ffn_p = trunk(x_prompt, c_prompt, pos_p, False)
    y_sample, ret_s, bk_s, bv_s, ck_s, cv_s, dconv_s, dssm_s, ffn_s = trunk(x_sample, c_sample, pos_s, True)
    return (y_prompt, y_sample, ret_p, ret_s, bk_p, bk_s, bv_p, bv_s, ck_p, ck_s, cv_p, cv_s,
            dconv_p, dconv_s, dssm_p, dssm_s, ffn_p, ffn_s)
```

```python
import math
from contextlib import ExitStack
import numpy as np
import concourse.bass as bass
import concourse.mybir as mybir
from concourse.bass_utils import run_bass_kernel_spmd

F32 = mybir.dt.float32
BF16 = mybir.dt.bfloat16
AF = mybir.ActivationFunctionType
ALU = mybir.AluOpType
AX = mybir.AxisListType
EPOCH = 12000
EPS = 1e-6
D = 1024
DFF = 2816
NEG = -30000.0


class Trk:
    __slots__ = ("lastw", "readers", "name", "ldsem", "ldcnt", "stsem", "stcnt")

    def __init__(self, name=""):
        self.lastw = None
        self.readers = []
        self.name = name
        self.ldsem = None
        self.ldcnt = 0
        self.stsem = None
        self.stcnt = 0


class DSem:
    def __init__(self, sem):
        self.sem = sem
        self.cnt = 0


class Tl:
    def __init__(self, t, name):
        self.t = t
        self.k = Trk(name)

    def __getitem__(self, idx):
        return self.t[idx]


class PTl(Tl):
    pass


class CView(Tl):
    def __init__(self, tl, c0):
        self.t = tl.t
        self.k = tl.k
        self.c0 = c0

    def __getitem__(self, idx):
        p, c, t = idx
        if isinstance(c, slice):
            c = slice((c.start or 0) + self.c0, (c.stop if c.stop is not None else 4) + self.c0)
        else:
            c = c + self.c0
        return self.t[p, c, t]


class Eng:
    def __init__(self, fw, name, h):
        self.name = name
        self.h = h
        self.ep = 0
        self.n = 0
        self.sem = fw.newsem(f"e_{name}_0")
        self.sems = [self.sem]
        self.seen = {}


class FW:
    def __init__(self, nc):
        self.nc = nc
        self.es = None
        self.ts = None
        self.nsem = 0
        self.E = {}
        self.dtoks = {}
        self.free_ds = []
        self.phase_ds = []

    def start(self, es):
        self.es = es
        self.ts = es
        nc = self.nc
        for name, h in (("pe", nc.tensor), ("act", nc.scalar), ("dve", nc.vector),
                        ("pool", nc.gpsimd), ("sp", nc.sync)):
            self.E[name] = Eng(self, name, h)

    def newsem(self, name):
        self.nsem += 1
        return self.es.enter_context(self.nc.semaphore(f"{name}_{self.nsem}"))

    def sb(self, name, shape, dt):
        self.nsem += 1
        name = f"{name}_{self.nsem}"
        return Tl(self.ts.enter_context(self.nc.sbuf_tensor(name, list(shape), dt)), name)

    def ps(self, name, shape, dt=F32):
        self.nsem += 1
        name = f"{name}_{self.nsem}"
        return PTl(self.ts.enter_context(self.nc.psum_tensor(name, list(shape), dt)), name)

    def _wait(self, e, tok):
        if tok is None:
            return
        key, n = tok
        if key[0] == "e" and key[1] == e.name and e.name == "pe":
            return
        if e.seen.get(key, 0) >= n:
            return
        e.seen[key] = n
        if key[0] == "e":
            sem = self.E[key[1]].sems[key[2]]
        else:
            sem = key[1].sem
        e.h.wait_ge(sem, n)

    def _deps(self, e, reads, writes):
        need = {}

        def add(tok):
            if tok is not None:
                need[tok[0]] = max(need.get(tok[0], 0), tok[1])
        for r in reads:
            k = r.k if isinstance(r, Tl) else r
            add(k.lastw)
        for w in writes:
            k = w.k if isinstance(w, Tl) else w
            add(k.lastw)
            for t in k.readers:
                add(t)
        for key, n in need.items():
            self._wait(e, (key, n))

    def _mark(self, tok, reads, writes):
        for r in reads:
            k = r.k if isinstance(r, Tl) else r
            k.readers.append(tok)
            if len(k.readers) > 32:
                k.readers = k.readers[-32:]
        for w in writes:
            k = w.k if isinstance(w, Tl) else w
            k.lastw = tok
            k.readers = []

    def op(self, en, reads, writes, fn):
        self.cnt = getattr(self, 'cnt', 0) + 1
        if self.cnt > LIMIT:
            return None
        e = self.E[en]
        if en != "pe":
            pr = [r for r in reads if isinstance(r, PTl)]
            if pr:
                reads = [r for r in reads if not isinstance(r, PTl)]
                writes = list(writes) + pr
        self._deps(e, reads, writes)
        ins = fn(e.h)
        e.n += 1
        ins.then_inc(e.sem, 1)
        tok = (("e", en, e.ep), e.n)
        self._mark(tok, reads, writes)
        if e.n >= EPOCH:
            e.ep += 1
            e.n = 0
            e.sem = self.newsem(f"e_{en}_{e.ep}")
            e.sems.append(e.sem)
        return tok

    def load(self, q, st, out_ap, in_ap, dk=None, **kw):
        self.cnt = getattr(self, 'cnt', 0) + 1
        if self.cnt > LIMIT:
            return None
        e = self.E[q]
        k = st.k
        if k.ldsem is None:
            k.ldsem = DSem(self.newsem("p")) if q == "pool" else self.get_ds()
        self._deps(e, [dk] if dk is not None else [], [st])
        e.h.dma_start(out=out_ap, in_=in_ap, **kw).then_inc(k.ldsem.sem, 16)
        k.ldsem.cnt += 16
        tok = (("d", k.ldsem), k.ldsem.cnt)
        k.lastw = tok
        k.readers = []
        if dk is not None:
            dk.readers.append(tok)
        self.dtoks[tok[0]] = tok[1]
        return tok

    def store(self, q, st, out_ap, in_ap, dk=None, **kw):
        self.cnt = getattr(self, 'cnt', 0) + 1
        if self.cnt > LIMIT:
            return None
        e = self.E[q]
        k = st.k
        if k.stsem is None:
            k.stsem = self.get_ds()
        self._deps(e, [st], [dk] if dk is not None else [])
        e.h.dma_start(out=out_ap, in_=in_ap, **kw).then_inc(k.stsem.sem, 16)
        k.stsem.cnt += 16
        tok = (("d", k.stsem), k.stsem.cnt)
        k.readers.append(tok)
        if dk is not None:
            dk.lastw = tok
            dk.readers = []
        self.dtoks[tok[0]] = tok[1]
        return tok

    def get_ds(self):
        if self.free_ds:
            d = self.free_ds.pop()
        else:
            d = DSem(self.newsem("d"))
        if self.ts is not self.es:
            self.phase_ds.append(d)
        return d

    def release_phase(self):
        self.free_ds.extend(self.phase_ds)
        self.phase_ds = []

    def barrier(self, only=None):
        if os.environ.get('K_SPFIN') and getattr(self, 'cnt', 0) > LIMIT:
            only = "sp"
        for en, e in self.E.items():
            if only is not None and en != only:
                continue
            for key, n in list(self.dtoks.items()):
                self._wait(e, (key, n))
            for sn, src in self.E.items():
                if sn != en and src.n > 0:
                    self._wait(e, (("e", sn, src.ep), src.n))
                if sn != en and src.ep > 0 and src.n == 0:
                    self._wait(e, (("e", sn, src.ep - 1), EPOCH))


class Rot:
    def __init__(self, tiles):
        self.tiles = tiles
        self.i = 0

    def next(self):
        t = self.tiles[self.i % len(self.tiles)]
        self.i += 1
        return t


def t5_bucket(rel):
    half = 16
    max_exact = 8
    base = np.where(rel > 0, half, 0)
    n = np.abs(rel)
    nf = np.maximum(n, 1).astype(np.float32)
    large = max_exact + (np.log(nf / max_exact) / math.log(128 / max_exact) * (half - max_exact)).astype(np.int32)
    large = np.minimum(large, half - 1)
    return base + np.where(n < max_exact, n, large)


def host_consts():
    c = {}
    c["ident"] = np.eye(128, dtype=np.float32)
    J = np.zeros((128, 128), np.float32)
    for j in range(128):
        J[j, 127 - j] = 1.0
    c["exch"] = J
    lg = np.log1p(-np.exp2(-5.0 - np.arange(4, dtype=np.float64)))
    inv = np.power(10000.0, -np.arange(64, dtype=np.float32) / 64).astype(np.float32)
    rope = np.zeros((17, 128, 4, 4, 64), np.float32)
    for ti in range(17):
        if ti < 16:
            pos = ti * 128 + np.arange(128)
        else:
            pos = 1024 + np.arange(128)
        lc = np.arange(128, dtype=np.float64)
        ang = pos.astype(np.float32)[:, None] * inv[None, :]
        cs, sn = np.cos(ang), np.sin(ang)
        for h in range(4):
            fq = np.exp((lc + 1.0) * lg[h])[:, None]
            fk = np.exp(-(lc + 1.0) * lg[h])[:, None] * (128.0 ** -0.5)
            rope[ti, :, 0, h] = cs * fq
            rope[ti, :, 1, h] = sn * fq
            rope[ti, :, 2, h] = cs * fk
            rope[ti, :, 3, h] = sn * fk
    c["rope"] = rope
    m = np.zeros((128, 128), np.float32)
    for mm in range(128):
        m[mm, mm:] = 1.0
    c["retmask"] = m
    c["trigt"] = np.ascontiguousarray(1.0 - m)
    gd = np.zeros((2, 128, 512), np.float32)
    for h in range(4):
        gd[0, :, h * 128:(h + 1) * 128] = np.exp(128 * lg[h])
        gd[1, :, h * 128:(h + 1) * 128] = np.exp(64 * lg[h])
    c["gdec"] = gd
    oh = np.zeros((32, 384), np.float32)
    rel = np.arange(384) - 255
    b = t5_bucket(rel)
    oh[b, np.arange(384)] = 1.0
    c["t5oh"] = oh
    m0 = np.zeros((128, 128), np.float32)
    m0[64:, :64] = NEG
    m4 = np.zeros((128, 128), np.float32)
    m4[:64, 64:] = NEG
    c["mneg"] = np.stack([m0, m4])
    return c


import os
PH = os.environ.get('K_PH', '0AFCGZ')
MAXT = int(os.environ.get('K_MAXT', '99'))
LIMIT = int(os.environ.get('K_LIMIT', '100000000'))
NCORES = int(os.environ.get('K_NCORES', '8'))
SEQSEL = os.environ.get('K_SEQS', '012')


def build():
    nc = bass.Bass("TRN2", target_bir_lowering=False)

    def din(n, s):
        return nc.dram_tensor(n, list(s), F32, kind="ExternalInput").ap()

    def dout(n, s):
        return nc.dram_tensor(n, list(s), F32, kind="ExternalOutput").ap()

    def dscr(n, s):
        return nc.dram_tensor(n, list(s), F32, kind="Internal").ap()

    I = {}
    for n, s in (("xp", [4096, D]), ("xsm", [64, D]), ("cc", [3, D]), ("ret0", [4, 128, 128]),
                 ("bk_c", [1024, 512]), ("bv_c", [1024, 512]), ("ck_c", [512, 512]), ("cv_c", [512, 512]),
                 ("dconv_c", [3, 1024]), ("dssm_c", [8, 64, 128]), ("ffnc", [2, 2, DFF]),
                 ("w_mod", [2, D, 6 * D]), ("b_mod", [1, 12 * D]), ("norm_g", [4, D]), ("final_g", [1, D]),
                 ("t5_table", [32, 4]), ("w_in_ab", [D, 3584]), ("w_out_ab", [D, D]), ("ret_gn", [1, 512]),
                 ("lam_q", [1, 128]), ("lam_k", [1, 128]), ("diff_gn", [1, 128]),
                 ("w_in_cd", [D, 3080]), ("w_out_cd", [D, D]), ("rel_table", [257, 8]),
                 ("d_conv_w", [4, D]), ("d_conv_b", [1, D]), ("d_dt_bias", [1, 8]), ("d_a_log", [1, 8]),
                 ("d_skip", [1, 8]), ("d_norm_g", [1, 512]), ("w_up", [2, D, 2 * DFF]),
                 ("ffn_cw", [6, DFF]), ("ffn_cb", [2, DFF]), ("w_down", [2, DFF, D]),
                 ("ident", [128, 128]), ("exch", [128, 128]), ("rope", [17, 128, 4, 4, 64]),
                 ("retmask", [128, 128]), ("trigt", [128, 128]), ("gdec", [2, 128, 512]), ("t5oh", [32, 384]), ("mneg", [2, 128, 128])):
        I[n] = din(n, s)
    O = {}
    for n, s in (("y_p", [4096, D]), ("y_s", [64, D]), ("ret_p", [2, 4, 128, 128]), ("ret_s", [4, 128, 128]),
                 ("bk_p", [4096, 512]), ("bk_s", [64, 512]), ("bv_p", [4096, 512]), ("bv_s", [64, 512]),
                 ("ck_p", [2, 512, 512]), ("ck_s", [64, 512]), ("cv_p", [2, 512, 512]), ("cv_s", [64, 512]),
                 ("dconv_p", [2, 3, 1024]), ("dconv_s", [3, 1024]), ("dssm_p", [2, 8, 64, 128]),
                 ("dssm_s", [8, 64, 128]), ("ffn_p", [2, 2, 2, DFF]), ("ffn_s", [2, 2, DFF])):
        O[n] = dout(n, s)
    xs = dscr("xs", [4160, D])
    modd = dscr("modd", [3, 2, 6 * D])
    s5d = dscr("s5d", [4, 384])
    sreld = dscr("sreld", [8, 384])

    SEQS = [(2048, 0, False, 0), (2048, 0, False, 2048), (64, 16, True, 4096)]
    xk = {}

    def xtrk(r0):
        if r0 not in xk:
            xk[r0] = Trk(f"xs{r0}")
        return xk[r0]

    fw = FW(nc)
    with ExitStack() as es:
        fw.start(es)
        A = fw.op
        idf = fw.sb("idf", [128, 128], F32)
        idb = fw.sb("idb", [128, 128], BF16)
        modT = [fw.sb(f"modT{l}", [128, 48, 3], F32) for l in range(2)]
        GS = [[fw.sb(f"GS{l}{j}", [128, 8, 3], F32) for j in range(2)] for l in range(2)]
        fcw = fw.sb("fcw", [128, 22, 8], F32)
        lam = fw.sb("lam", [128, 4], F32)
        fw.load("sp", idf, idf[:], I["ident"])
        fw.load("pool", idb, idb[:], I["ident"])
        with ExitStack() as pst:
            fw.ts = pst
            ptb = Rot([fw.ps(f"ptb{i}", [128, 1024], BF16) for i in range(2)])
            pacc = Rot([fw.ps(f"pacc{i}", [128, 512], F32) for i in range(2)])
            psf = Rot([fw.ps(f"psf{i}", [128, 512], F32) for i in range(4)])

            def rows_to_fm(rows, R, n, out, c0=0):
                pp = psf.next()
                def f(h):
                    for c in range(n):
                        ins = h.matmul(pp[:, c * R:(c + 1) * R], rows[0:R, (c0 + c) * 128:(c0 + c + 1) * 128],
                                       idf[0:R, 0:R], start=True, stop=True)
                    return ins
                A("pe", [rows, idf], [pp], f)
                A("dve", [pp], [out], lambda h: h.tensor_copy(
                    out=out[:, 0:n, :], in_=pp[:, 0:n * R].rearrange("p (c r) -> p c r", r=R)))

            with ExitStack() as ph:
                fw.ts = ph
                cr = fw.sb("cr", [3, D], F32)
                ce = fw.sb("ce", [3, D], F32)
                cT = fw.sb("cT", [128, 8, 3], F32)
                mr = fw.sb("mr", [3, 2, 6 * D], F32)
                bmr = fw.sb("bmr", [1, 12 * D], F32)
                one = fw.sb("one", [1, 4], F32)
                ngr = fw.sb("ngr", [4, D], F32)
                ngT = fw.sb("ngT", [128, 8, 4], F32)
                fcr = fw.sb("fcr", [8, DFF], F32)
                wrot = Rot([fw.sb(f"wmb{i}", [128, 8, 512], F32) for i in range(2)])
                fw.load("sp", cr, cr[:], I["cc"])
                fw.load("sp", bmr, bmr[:], I["b_mod"])
                fw.load("sp", ngr, ngr[:], I["norm_g"])
                fw.load("sp", fcr, fcr[0:6, :], I["ffn_cw"])
                fw.load("sp", fcr, fcr[6:8, :], I["ffn_cb"])
                A("dve", [], [one], lambda h: h.memset(one[:], 1.0))
                A("act", [cr], [ce], lambda h: h.activation(out=ce[:], in_=cr[:], func=AF.Exp, scale=-1.0))
                A("dve", [ce], [ce], lambda h: h.tensor_scalar_add(out=ce[:], in0=ce[:], scalar1=1.0))
                A("dve", [ce], [ce], lambda h: h.reciprocal(out=ce[:], in_=ce[:]))
                A("dve", [ce, cr], [cr], lambda h: h.tensor_mul(out=cr[:], in0=cr[:], in1=ce[:]))
                rows_to_fm(cr, 3, 8, cT)
                rows_to_fm(ngr, 4, 8, ngT)
                rows_to_fm(fcr, 8, 22, fcw)
                for l in range(2):
                    for nb in range(12):
                        wb = wrot.next()
                        src = I["w_mod"][l, :, nb * 512:(nb + 1) * 512].rearrange("(kc p) n -> p kc n", p=128)
                        fw.load("sp", wb, wb[:, 0:4, :], src[:, 0:4, :])
                        fw.load("sp", wb, wb[:, 4:8, :], src[:, 4:8, :])
                        pp = psf.next()
                        def f(h, wb=wb, pp=pp, l=l, nb=nb):
                            for kc in range(8):
                                h.matmul(pp[0:3, :], cT[:, kc, :], wb[:, kc, :], start=(kc == 0), stop=False)
                            return h.matmul(pp[0:3, :], one[0:1, 0:3],
                                            bmr[0:1, l * 6 * D + nb * 512: l * 6 * D + (nb + 1) * 512],
                                            start=False, stop=True)
                        A("pe", [wb, cT, one, bmr], [pp], f)
                        A("act", [pp], [mr], lambda h, pp=pp, l=l, nb=nb: h.activation(
                            out=mr[0:3, l, nb * 512:(nb + 1) * 512], in_=pp[0:3, :], func=AF.Copy))
                mk = Trk("modd")
                fw.store("sp", mr, modd, mr[:], dk=mk)
                for l in range(2):
                    for half in range(2):
                        pp = psf.next()
                        def f(h, pp=pp, l=l, half=half):
                            for c in range(24):
                                cc_ = half * 24 + c
                                ins = h.matmul(pp[:, c * 3:(c + 1) * 3], mr[0:3, l, cc_ * 128:(cc_ + 1) * 128],
                                               idf[0:3, 0:3], start=True, stop=True)
                            return ins
                        A("pe", [mr, idf], [pp], f)
                        A("dve", [pp], [modT[l]], lambda h, pp=pp, l=l, half=half: h.tensor_copy(
                            out=modT[l][:, half * 24:(half + 1) * 24, :],
                            in_=pp[:, 0:72].rearrange("p (c r) -> p c r", r=3)))
                    for j in range(2):
                        sc0 = 8 + 24 * j
                        A("dve", [modT[l]], [GS[l][j]], lambda h, l=l, j=j, sc0=sc0: h.tensor_scalar_add(
                            out=GS[l][j][:], in0=modT[l][:, sc0:sc0 + 8, :], scalar1=1.0))
                        A("dve", [GS[l][j], ngT], [GS[l][j]], lambda h, l=l, j=j: h.tensor_tensor(
                            out=GS[l][j][:], in0=GS[l][j][:],
                            in1=ngT[:, :, l * 2 + j:l * 2 + j + 1].to_broadcast([128, 8, 3]), op=ALU.mult))
                lq = fw.sb("lq", [128, 128], F32)
                lk = fw.sb("lk", [128, 128], F32)
                fw.load("sp", lq, lq[:], I["lam_q"].partition_broadcast(128))
                fw.load("sp", lk, lk[:], I["lam_k"].partition_broadcast(128))
                A("dve", [lq, lk], [lq], lambda h: h.tensor_mul(out=lq[:], in0=lq[:], in1=lk[:]))
                A("dve", [lq], [lam], lambda h: h.tensor_reduce(
                    out=lam[:, 1:3], in_=lq[:].rearrange("p (a b) -> p a b", a=2), axis=AX.X, op=ALU.add))
                A("act", [lam], [lam], lambda h: h.activation(out=lam[:, 1:3], in_=lam[:, 1:3], func=AF.Exp))
                lam_init0 = 0.8 - 0.6 * math.exp(-0.3 * 0)
                A("dve", [lam], [lam], lambda h: h.tensor_sub(out=lam[:, 0:1], in0=lam[:, 2:3], in1=lam[:, 1:2]))
                A("dve", [lam], [lam], lambda h: h.tensor_scalar_add(out=lam[:, 0:1], in0=lam[:, 0:1],
                                                                     scalar1=-lam_init0))
                t5t = fw.sb("t5t", [32, 4], F32)
                t5o = fw.sb("t5o", [32, 384], F32)
                t5r = fw.sb("t5r", [4, 384], F32)
                fw.load("sp", t5t, t5t[:], I["t5_table"])
                fw.load("sp", t5o, t5o[:], I["t5oh"])
                pp = psf.next()
                A("pe", [t5t, t5o], [pp], lambda h, pp=pp: h.matmul(pp[0:4, 0:384], t5t[:], t5o[:],
                                                                      start=True, stop=True))
                A("act", [pp], [t5r], lambda h, pp=pp: h.activation(out=t5r[:], in_=pp[0:4, 0:384], func=AF.Copy))
                s5k = Trk("s5d")
                fw.store("sp", t5r, s5d, t5r[:], dk=s5k)
                rlt = fw.sb("rlt", [128, 3, 8], F32)
                rlr = fw.sb("rlr", [8, 384], F32)
                fw.load("sp", rlt, rlt[:, 0:2, :], I["rel_table"][0:256, :].rearrange("(c p) h -> p c h", p=128))
                fw.load("sp", rlt, rlt[0:1, 2, :], I["rel_table"][256:257, :])
                pp = psf.next()
                def f(h, pp=pp):
                    h.matmul(pp[0:8, 0:128], rlt[:, 0, :], idf[:, :], start=True, stop=True)
                    h.matmul(pp[0:8, 128:256], rlt[:, 1, :], idf[:, :], start=True, stop=True)
                    return h.matmul(pp[0:8, 256:257], rlt[0:1, 2, :], idf[0:1, 0:1], start=True, stop=True)
                A("pe", [rlt, idf], [pp], f)
                A("act", [pp], [rlr], lambda h, pp=pp: h.activation(out=rlr[:, 0:257], in_=pp[0:8, 0:257],
                                                                     func=AF.Copy))
                A("dve", [rlr], [rlr], lambda h: h.tensor_copy(out=rlr[:, 257:384],
                                                               in_=rlr[:, 256:257].to_broadcast([8, 127])))
                srk = Trk("sreld")
                fw.store("sp", rlr, sreld, rlr[:], dk=srk)
                fw.barrier()
            fw.release_phase()

            def bias_tiles(ph_name, dst, nh, rows_ap, rk, offs, first, cst, mn):
                hk = Rot([fw.sb(f"hk{ph_name}{i}", [128, 128], F32) for i in range(2)])
                ex = fw.sb(f"ex{ph_name}", [128, 128], F32)
                fw.load("sp", ex, ex[:], I["exch"])
                for h_ in range(nh):
                    for d_, off in enumerate(offs):
                        hkt = hk.next()
                        src = bass.AP(rows_ap.tensor, h_ * 384 + off, [[1, 128], [1, 128]])
                        fw.load("sp", hkt, hkt[:], src, dk=rk)
                        pp = psf.next()
                        if first:
                            A("pe", [hkt, ex], [pp], lambda h, pp=pp, hkt=hkt: h.matmul(
                                pp[:, 0:128], hkt[:], ex[:], start=True, stop=True))
                        else:
                            A("pe", [hkt, ex], [pp], lambda h, pp=pp, hkt=hkt: h.matmul(
                                pp[:, 0:128], ex[:], hkt[:], start=True, stop=True))
                        m_ = mn[d_]
                        if m_ is None:
                            A("dve", [pp, cst], [dst], lambda h, pp=pp, h_=h_, d_=d_: h.tensor_scalar(
                                out=dst[:, h_, d_, :], in0=pp[:, 0:128], scalar1=cst[:, h_:h_ + 1], scalar2=None,
                                op0=ALU.subtract))
                        else:
                            A("dve", [pp, cst, m_], [dst], lambda h, pp=pp, h_=h_, d_=d_, m_=m_: h.scalar_tensor_tensor(
                                out=dst[:, h_, d_, :], in0=pp[:, 0:128], scalar=cst[:, h_:h_ + 1], in1=m_[:],
                                op0=ALU.subtract, op1=ALU.add))

            def rmsnorm_to_hT(xt, nt, hT, gs, sh, s, scr, st4):
                A("act", [xt], [scr, st4], lambda h: h.activation(out=scr[:nt, :], in_=xt[:nt, :], func=AF.Square,
                                                                 accum_out=st4[:nt, 0:1]))
                A("act", [st4], [st4], lambda h: h.activation(out=st4[:nt, 1:2], in_=st4[:nt, 0:1], func=AF.Sqrt,
                                                              scale=1.0 / D, bias=EPS))
                A("dve", [st4], [st4], lambda h: h.reciprocal(out=st4[:nt, 2:3], in_=st4[:nt, 1:2]))
                A("dve", [xt, st4], [scr], lambda h: h.tensor_scalar(out=scr[:nt, :], in0=xt[:nt, :],
                                                                    scalar1=st4[:nt, 2:3], scalar2=None, op0=ALU.mult))
                pt = ptb.next()
                def f(h):
                    for c in range(8):
                        ins = h.transpose(out=pt[:, c * 128:c * 128 + nt], in_=scr[:nt, c * 128:(c + 1) * 128],
                                          identity=idb[:nt, :nt])
                    return ins
                A("pe", [scr, idb], [pt], f)
                for c in range(8):
                    A("act", [pt, gs, sh], [hT], lambda h, c=c: h.activation(
                        out=hT[:, c, :nt], in_=pt[:, c * 128:c * 128 + nt], func=AF.Identity,
                        scale=gs[:, c, s:s + 1], bias=sh[:, c, s:s + 1]))

            def transpose_blocks(src, nt, nblk, dst_fn, rd_extra=()):
                pt = ptb.next()
                def f(h):
                    for c in range(nblk):
                        ins = h.transpose(out=pt[:, c * 128:c * 128 + nt], in_=src[:nt, c * 128:(c + 1) * 128],
                                          identity=idb[:nt, :nt])
                    return ins
                A("pe", [src, idb], [pt], f)
                return pt

            with ExitStack() as ph:
              if 'A' in PH:
                fw.ts = ph
                wIn = fw.sb("wIn", [128, 8, 2560], BF16)
                w32 = fw.sb("w32", [128, 8, 1024], F32)
                wOut = fw.sb("wOut", [128, 8, D], BF16)
                for kc in range(8):
                    fw.load("sp", w32, w32[:, kc, :], I["w_in_ab"][kc * 128:(kc + 1) * 128, 0:1024])
                for kc in range(8):
                    fw.load("pool", wIn, wIn[:, kc, :], I["w_in_ab"][kc * 128:(kc + 1) * 128, 1024:3584])
                for kc in range(8):
                    fw.load("pool", wOut, wOut[:, kc, :], I["w_out_ab"][kc * 128:(kc + 1) * 128, :])
                kbT = fw.sb("kbT", [128, 4, 2176], BF16)
                Vc = fw.sb("Vc", [128, 17, 4, 130], BF16)
                bT5 = fw.sb("bT5", [128, 4, 2, 128], BF16)
                c15 = fw.sb("c15", [128, 4], F32)
                mn0 = fw.sb("mn0", [128, 128], F32)
                rmask = fw.sb("rmask", [128, 128], F32)
                gdec = fw.sb("gdecs", [128, 2, 512], F32)
                rgn = fw.sb("rgn", [128, 512], F32)
                dgn = fw.sb("dgn", [128, 128], F32)
                G1 = fw.sb("G1", [128, D], F32)
                fw.load("sp", c15, c15[:], I["t5_table"][15:16, :].partition_broadcast(128))
                fw.load("sp", mn0, mn0[:], I["mneg"][0])
                fw.load("sp", rmask, rmask[:], I["retmask"])
                fw.load("sp", gdec, gdec[:], I["gdec"].rearrange("a p n -> p a n"))
                fw.load("sp", rgn, rgn[:], I["ret_gn"].partition_broadcast(128))
                fw.load("sp", dgn, dgn[:], I["diff_gn"].partition_broadcast(128))
                A("dve", [dgn], [dgn], lambda h: h.tensor_scalar(out=dgn[:], in0=dgn[:], scalar1=1.0 - lam_init0,
                                                                scalar2=None, op0=ALU.mult))
                A("dve", [], [Vc], lambda h: h.memset(Vc[:, :, :, 128:130], 1.0))
                bias_tiles("t5", bT5, 4, s5d, s5k, [128, 0], True, c15, [mn0, None])

                xrot = Rot([fw.sb(f"xa{i}", [128, D], F32) for i in range(2)])
                rrot = Rot([fw.sb(f"rp{i}", [128, 4, 4, 64], F32) for i in range(1)])
                st4 = fw.sb("st4", [128, 4], F32)
                hT = fw.sb("hT", [128, 8, 128], BF16)
                hT32 = fw.sb("hT32", [128, 8, 128], F32)
                qaT32 = CView(hT32, 0)
                kaT32 = CView(hT32, 4)
                ev = fw.sb("ev", [128, 512], F32)
                rt = [fw.sb(f"rt{i}", [128, 4, 64], F32) for i in range(4)]
                qrot = fw.sb("qrot", [128, 512], BF16)
                krot = fw.sb("krot", [128, 512], BF16)
                va = fw.sb("va", [128, 512], BF16)
                sg = fw.sb("sg", [128, 512], F32)
                sge = fw.sb("sge", [128, 512], F32)
                qb = fw.sb("qb", [128, 512], BF16)
                kbf = Rot([fw.sb(f"kbf{i}", [128, 512], F32) for i in range(1)])
                vbf = Rot([fw.sb(f"vbf{i}", [128, 512], F32) for i in range(1)])
                kb16 = fw.sb("kb16", [128, 512], BF16)
                qaT = fw.sb("qaT", [128, 4, 128], BF16)
                qbT = fw.sb("qbT", [128, 4, 128], BF16)
                PT = fw.sb("PT", [128, 4, 128], BF16)
                stf = fw.sb("stf", [128, 512], F32)
                stb = fw.sb("stb", [128, 512], BF16)
                gst = fw.sb("gst", [128, 16], F32)
                gtmp = fw.sb("gtmp", [128, 512], F32)
                gsq = fw.sb("gsq", [128, 512], F32)
                mix = fw.sb("mix", [128, D], BF16)
                pTt = Rot([fw.sb(f"pTt{i}", [128, 4, 128], BF16) for i in range(2)])
                ob = fw.sb("ob", [128, 4, 128], F32)
                rc = fw.sb("rc", [128, 4], F32)
                t1 = fw.sb("t1", [128, 128], F32)
                mT = fw.sb("mT", [128, 8, 128], BF16)
                xo = Rot([fw.sb(f"xo{i}", [128, D], F32) for i in range(1)])
                ot = fw.sb("ot", [128, D], F32)
                xn32 = ot
                q32 = gtmp
                k32 = gsq
                scr = mix

                for s, (ntok, rp0, samp, row0) in enumerate(SEQS):
                    if str(s) not in SEQSEL:
                        continue
                    ntl = min(MAXT, (ntok + 127) // 128)
                    fw.load("sp", G1, G1[:], modd[s:s + 1, 0, 16 * 128:24 * 128].partition_broadcast(128), dk=mk)
                    if samp:
                        fw.load("sp", stf, stf[:].rearrange("p (h e) -> p h e", h=4),
                                I["ret0"].rearrange("h d e -> d h e"))
                        A("dve", [], [kbT], lambda h: h.memset(kbT[:, :, 1088:1152], 0.0))
                        A("dve", [], [Vc], lambda h: h.memset(Vc[64:128, 8, :, 0:128], 0.0))
                        for j in range(8):
                            kf = kbf.next()
                            fw.load("sp", kf, kf[:], I["bk_c"][j * 128:(j + 1) * 128, :])
                            A("dve", [kf], [kb16], lambda h, kf=kf: h.tensor_copy(out=kb16[:], in_=kf[:]))
                            pt = transpose_blocks(kb16, 128, 4, None)
                            A("act", [pt], [kbT], lambda h, pt=pt, j=j: h.activation(
                                out=kbT[:, :, j * 128:(j + 1) * 128],
                                in_=pt[:, 0:512].rearrange("p (c t) -> p c t", c=4), func=AF.Copy))
                            vf = vbf.next()
                            fw.load("sp", vf, vf[:], I["bv_c"][j * 128:(j + 1) * 128, :])
                            A("dve", [vf], [Vc], lambda h, vf=vf, j=j: h.tensor_copy(
                                out=Vc[:, j, :, 0:128], in_=vf[:].rearrange("p (h e) -> p h e", h=4)))
                    else:
                        A("dve", [], [stf], lambda h: h.memset(stf[:], 0.0))
                    A("act", [stf], [stb], lambda h: h.activation(out=stb[:], in_=stf[:], func=AF.Copy))
                    for ti in range(ntl):
                        nt = min(128, ntok - ti * 128)
                        r0 = row0 + ti * 128
                        qi = 8 if samp else ti
                        gdi = 1 if samp else 0
                        xt = xrot.next()
                        src = I["xsm"] if samp else I["xp"]
                        sr0 = 0 if samp else r0
                        fw.load("sp", xt, xt[:nt, :], src[sr0:sr0 + nt, :])
                        rp = rrot.next()
                        fw.load("sp", rp, rp[:], I["rope"][rp0 + ti])
                        A("act", [xt], [scr, st4], lambda h: h.activation(out=scr[:nt, :], in_=xt[:nt, :], func=AF.Square,
                                                                         accum_out=st4[:nt, 0:1]))
                        A("act", [st4], [st4], lambda h: h.activation(out=st4[:nt, 1:2], in_=st4[:nt, 0:1], func=AF.Sqrt,
                                                                      scale=1.0 / D, bias=EPS))
                        A("dve", [st4], [st4], lambda h: h.reciprocal(out=st4[:nt, 2:3], in_=st4[:nt, 1:2]))
                        A("dve", [xt, st4], [xn32], lambda h: h.tensor_scalar(out=xn32[:nt, :], in0=xt[:nt, :],
                                                                             scalar1=st4[:nt, 2:3], scalar2=None, op0=ALU.mult))
                        for half in range(2):
                            pp = psf.next()
                            def f(h, pp=pp, half=half):
                                for cc in range(4):
                                    c = half * 4 + cc
                                    ins = h.matmul(pp[:, cc * 128:cc * 128 + nt], xn32[:nt, c * 128:(c + 1) * 128],
                                                   idf[:nt, :nt], start=True, stop=True)
                                return ins
                            A("pe", [xn32, idf], [pp], f)
                            for cc in range(4):
                                c = half * 4 + cc
                                A("act", [pp, GS[0][0], modT[0]], [hT32], lambda h, pp=pp, c=c, cc=cc: h.activation(
                                    out=hT32[:, c, :nt], in_=pp[:, cc * 128:cc * 128 + nt], func=AF.Identity,
                                    scale=GS[0][0][:, c, s:s + 1], bias=modT[0][:, c, s:s + 1]))
                        A("dve", [hT32], [hT], lambda h: h.tensor_copy(out=hT[:, :, :nt], in_=hT32[:, :, :nt]))
                        def proj(blk):
                            pp = psf.next()
                            def f(h):
                                for c in range(8):
                                    if blk < 2:
                                        ins = h.matmul(pp[:nt, :], hT32[:, c, :nt], w32[:, c, blk * 512:(blk + 1) * 512],
                                                       start=(c == 0), stop=(c == 7))
                                    else:
                                        ins = h.matmul(pp[:nt, :], hT[:, c, :nt], wIn[:, c, (blk - 2) * 512:(blk - 1) * 512],
                                                       start=(c == 0), stop=(c == 7))
                                return ins
                            A("pe", [hT, hT32, wIn, w32], [pp], f)
                            return pp

                        def rotary(pp, ci, out, eng):
                            A("act", [pp], [ev], lambda h: h.activation(out=ev[:nt, :], in_=pp[:nt, :], func=AF.Copy))
                            x1 = ev[:nt, :].rearrange("p (h e) -> p h e", h=4)[:, :, 0:64]
                            x2 = ev[:nt, :].rearrange("p (h e) -> p h e", h=4)[:, :, 64:128]
                            cs_ = rp[:nt, ci]
                            sn_ = rp[:nt, ci + 1]
                            out32 = q32 if ci == 0 else k32
                            ov = out32[:nt, :].rearrange("p (h e) -> p h e", h=4)
                            A(eng, [ev, rp], [rt[0]], lambda h: h.tensor_tensor(out=rt[0][:nt], in0=x1, in1=cs_, op=ALU.mult))
                            A(eng, [ev, rp], [rt[1]], lambda h: h.tensor_tensor(out=rt[1][:nt], in0=x2, in1=sn_, op=ALU.mult))
                            A(eng, [ev, rp], [rt[2]], lambda h: h.tensor_tensor(out=rt[2][:nt], in0=x1, in1=sn_, op=ALU.mult))
                            A(eng, [ev, rp], [rt[3]], lambda h: h.tensor_tensor(out=rt[3][:nt], in0=x2, in1=cs_, op=ALU.mult))
                            A(eng, [rt[0], rt[1]], [out32], lambda h: h.tensor_tensor(
                                out=ov[:, :, 0:64], in0=rt[0][:nt], in1=rt[1][:nt], op=ALU.subtract))
                            A(eng, [rt[2], rt[3]], [out32], lambda h: h.tensor_tensor(
                                out=ov[:, :, 64:128], in0=rt[2][:nt], in1=rt[3][:nt], op=ALU.add))
                            A("dve", [out32], [out], lambda h: h.tensor_copy(out=out[:nt, :], in_=out32[:nt, :]))

                        pp = proj(0)
                        rotary(pp, 0, qrot, "dve")
                        pp = proj(1)
                        rotary(pp, 2, krot, "dve")
                        pp = proj(2)
                        A("act", [pp], [va], lambda h, pp=pp: h.activation(out=va[:nt, :], in_=pp[:nt, :], func=AF.Copy))
                        pp = proj(3)
                        A("act", [pp], [sge], lambda h, pp=pp: h.activation(out=sge[:nt, :], in_=pp[:nt, :], func=AF.Sigmoid))
                        A("dve", [pp, sge], [sg], lambda h, pp=pp: h.tensor_tensor(out=sg[:nt, :], in0=pp[:nt, :],
                                                                                 in1=sge[:nt, :], op=ALU.mult))
                        pp = proj(4)
                        A("act", [pp], [qb], lambda h, pp=pp: h.activation(out=qb[:nt, :], in_=pp[:nt, :], func=AF.Copy,
                                                                         scale=0.125))
                        pp = proj(5)
                        kf = kbf.next()
                        A("act", [pp], [kf], lambda h, pp=pp: h.activation(out=kf[:nt, :], in_=pp[:nt, :], func=AF.Copy))
                        A("dve", [pp], [kb16], lambda h, pp=pp: h.tensor_copy(out=kb16[:nt, :], in_=pp[:nt, :]))
                        fw.store("sp", kf, (O["bk_s"] if samp else O["bk_p"])[sr0:sr0 + nt, :], kf[:nt, :])
                        pp = proj(6)
                        vf = vbf.next()
                        A("act", [pp], [vf], lambda h, pp=pp: h.activation(out=vf[:nt, :], in_=pp[:nt, :], func=AF.Copy))
                        A("dve", [pp], [Vc], lambda h, pp=pp: h.tensor_copy(
                            out=Vc[:nt, qi, :, 0:128], in_=pp[:nt, :].rearrange("p (h e) -> p h e", h=4)))
                        fw.store("sp", vf, (O["bv_s"] if samp else O["bv_p"])[sr0:sr0 + nt, :], vf[:nt, :])
                        for src32, dst32 in ((q32, qaT32), (k32, kaT32)):
                            pp = psf.next()
                            def f(h, pp=pp, src32=src32):
                                for hh in range(4):
                                    ins = h.matmul(pp[:, hh * 128:hh * 128 + nt], src32[:nt, hh * 128:(hh + 1) * 128],
                                                   idf[:nt, :nt], start=True, stop=True)
                                return ins
                            A("pe", [src32, idf], [pp], f)
                            A("act", [pp], [dst32], lambda h, pp=pp, dst32=dst32: h.activation(
                                out=dst32[:, :, :nt], in_=pp[:, :].rearrange("p (c t) -> p c t", c=4)[:, :, :nt],
                                func=AF.Copy))
                        for srcT, dstT in ((qrot, qaT), (qb, qbT)):
                            pt = transpose_blocks(srcT, nt, 4, None)
                            A("act", [pt], [dstT], lambda h, pt=pt, dstT=dstT: h.activation(
                                out=dstT[:, :, :nt], in_=pt[:, 0:512].rearrange("p (c t) -> p c t", c=4)[:, :, :nt],
                                func=AF.Copy))
                        pt = transpose_blocks(kb16, nt, 4, None)
                        A("act", [pt], [kbT], lambda h, pt=pt: h.activation(
                            out=kbT[:, :, qi * 128:qi * 128 + nt],
                            in_=pt[:, 0:512].rearrange("p (c t) -> p c t", c=4)[:, :, :nt], func=AF.Copy))
                        pS = psf.next()
                        def f(h):
                            for hh in range(4):
                                ins = h.matmul(pS[:nt, hh * 128:hh * 128 + nt], kaT32[:, hh, :nt], qaT32[:, hh, :nt],
                                               start=True, stop=True)
                            return ins
                        A("pe", [kaT32, qaT32], [pS], f)
                        A("dve", [pS, rmask], [PT], lambda h: h.tensor_tensor(
                            out=PT[:nt, :, :nt], in0=pS[:nt, :].rearrange("p (h t) -> p h t", h=4)[:, :, :nt],
                            in1=rmask[:nt, :nt].unsqueeze(1).to_broadcast([nt, 4, nt]), op=ALU.mult))
                        pD = psf.next()
                        def f(h):
                            for hh in range(4):
                                ins = h.matmul(pD[:, hh * 128:(hh + 1) * 128], krot[:nt, hh * 128:(hh + 1) * 128],
                                               va[:nt, hh * 128:(hh + 1) * 128], start=True, stop=True)
                            return ins
                        A("pe", [krot, va], [pD], f)
                        pO = psf.next()
                        def f(h):
                            for hh in range(4):
                                h.matmul(pO[:nt, hh * 128:(hh + 1) * 128], PT[:nt, hh, :nt], va[:nt, hh * 128:(hh + 1) * 128],
                                         start=True, stop=False)
                                ins = h.matmul(pO[:nt, hh * 128:(hh + 1) * 128], qaT[:, hh, :nt],
                                               stb[:, hh * 128:(hh + 1) * 128], start=False, stop=True)
                            return ins
                        A("pe", [PT, va, qaT, stb], [pO], f)
                        A("dve", [pD, stf], [stf], lambda h: h.tensor_tensor(out=stf[:], in0=pD[:], in1=stf[:], op=ALU.add))
                        A("dve", [stf, gdec], [stf], lambda h: h.tensor_tensor(out=stf[:], in0=stf[:], in1=gdec[:, gdi, :],
                                                                               op=ALU.mult))
                        A("act", [stf], [stb], lambda h: h.activation(out=stb[:], in_=stf[:], func=AF.Copy))
                        pOv = pO[:nt, :].rearrange("p (h e) -> p h e", h=4)
                        A("dve", [pO], [gst], lambda h: h.tensor_reduce(out=gst[:nt, 0:4], in_=pOv, axis=AX.X, op=ALU.add))
                        A("act", [pO], [gsq], lambda h: h.activation(out=gsq[:nt, :], in_=pO[:nt, :], func=AF.Square))
                        A("dve", [gsq], [gst], lambda h: h.tensor_reduce(
                            out=gst[:nt, 4:8], in_=gsq[:nt, :].rearrange("p (h e) -> p h e", h=4), axis=AX.X, op=ALU.add))
                        A("dve", [gst], [gst], lambda h: h.tensor_scalar(out=gst[:nt, 0:4], in0=gst[:nt, 0:4],
                                                                        scalar1=1.0 / 128, scalar2=None, op0=ALU.mult))
                        A("dve", [gst], [gst], lambda h: h.tensor_tensor(out=gst[:nt, 8:12], in0=gst[:nt, 0:4],
                                                                        in1=gst[:nt, 0:4], op=ALU.mult))
                        A("dve", [gst], [gst], lambda h: h.scalar_tensor_tensor(
                            out=gst[:nt, 4:8], in0=gst[:nt, 4:8], scalar=1.0 / 128, in1=gst[:nt, 8:12],
                            op0=ALU.mult, op1=ALU.subtract))
                        A("act", [gst], [gst], lambda h: h.activation(out=gst[:nt, 4:8], in_=gst[:nt, 4:8], func=AF.Sqrt,
                                                                      bias=EPS))
                        A("dve", [gst], [gst], lambda h: h.reciprocal(out=gst[:nt, 4:8], in_=gst[:nt, 4:8]))
                        gv = gtmp[:nt, :].rearrange("p (h e) -> p h e", h=4)
                        A("dve", [pO, gst], [gtmp], lambda h: h.tensor_tensor(
                            out=gv, in0=pOv, in1=gst[:nt, 0:4].unsqueeze(2).to_broadcast([nt, 4, 128]), op=ALU.subtract))
                        A("dve", [gtmp, gst], [gtmp], lambda h: h.tensor_tensor(
                            out=gv, in0=gv, in1=gst[:nt, 4:8].unsqueeze(2).to_broadcast([nt, 4, 128]), op=ALU.mult))
                        A("dve", [gtmp, rgn], [gtmp], lambda h: h.tensor_tensor(out=gtmp[:nt, :], in0=gtmp[:nt, :],
                                                                                in1=rgn[:nt, :], op=ALU.mult))
                        A("dve", [gtmp, sg], [mix], lambda h: h.tensor_tensor(out=mix[:nt, 0:512], in0=gtmp[:nt, :],
                                                                             in1=sg[:nt, :], op=ALU.mult))
                        groups = []
                        j = 0
                        while j <= qi:
                            if samp and j == 8:
                                groups.append([8])
                                j += 1
                            else:
                                hi = min(j + 4, (8 if samp else qi + 1))
                                groups.append(list(range(j, hi)))
                                j = hi
                        pas = {}
                        def qk_a(hh, i_, g):
                            nk = 128
                            pS = psf.next()
                            def f(h):
                                for jj, j_ in enumerate(g):
                                    dl = j_ - qi
                                    near = dl >= -1
                                    ins = h.matmul(pS[:nk, jj * 128:jj * 128 + nt],
                                                   kbT[i_ * 64:(i_ + 1) * 64, hh, j_ * 128:j_ * 128 + nk],
                                                   qbT[i_ * 64:(i_ + 1) * 64, hh, :nt], start=True, stop=not near)
                                    if near:
                                        ins = h.matmul(pS[:nk, jj * 128:jj * 128 + nt], idb[:nk, :nk],
                                                       bT5[:nk, hh, -dl, :nt], start=False, stop=True)
                                return ins
                            A("pe", [kbT, qbT, idb, bT5], [pS], f)
                            pt_ = pTt.next()
                            ng = len(g)
                            A("act", [pS, c15], [pt_], lambda h: h.activation(
                                out=pt_[:nk, 0:ng, :nt],
                                in_=pS[:nk, 0:ng * 128].rearrange("p (g t) -> p g t", g=ng)[:, :, :nt],
                                func=AF.Exp, bias=c15[:nk, hh:hh + 1]))
                            return pt_

                        def pv_a(hh, i_, g, pt_):
                            nk = 128
                            if hh not in pas:
                                pas[hh] = pacc.next()
                            pa = pas[hh]
                            def f(h):
                                for jj, j_ in enumerate(g):
                                    ins = h.matmul(pa[:nt, i_ * 256:i_ * 256 + 129], pt_[:nk, jj, :nt],
                                                   Vc[:nk, j_, hh, 0:129], start=(j_ == 0), stop=(j_ == qi))
                                return ins
                            A("pe", [pt_, Vc], [pa], f)
                            if i_ == 1 and g is groups[-1]:
                                pav = pa[:nt, 0:512].rearrange("p (i e) -> p i e", i=2)
                                A("dve", [pa], [rc], lambda h: h.reciprocal(out=rc[:nt, 0:2], in_=pav[:, :, 128]))
                                A("dve", [rc, lam], [rc], lambda h: h.tensor_tensor(out=rc[:nt, 2:3], in0=rc[:nt, 1:2],
                                                                                   in1=lam[:nt, 0:1], op=ALU.mult))
                                A("dve", [pa, rc], [t1], lambda h: h.tensor_scalar(
                                    out=t1[:nt, :], in0=pa[:nt, 256:384], scalar1=rc[:nt, 2:3], scalar2=None, op0=ALU.mult))
                                A("dve", [pa, rc, t1], [ob], lambda h: h.scalar_tensor_tensor(
                                    out=ob[:nt, hh, :], in0=pa[:nt, 0:128], scalar=rc[:nt, 0:1], in1=t1[:nt, :],
                                    op0=ALU.mult, op1=ALU.add))

                        items = [(hh, i_, g) for hh in range(4) for i_ in range(2) for g in groups]
                        prev = None
                        for it in items:
                            cur = qk_a(*it)
                            if prev is not None:
                                pv_a(*prev)
                            prev = it + (cur,)
                        pv_a(*prev)
                        obf = ob[:nt].rearrange("p h e -> p (h e)")
                        A("act", [ob], [gsq], lambda h: h.activation(out=gsq[:nt, :], in_=obf, func=AF.Square))
                        A("dve", [gsq], [gst], lambda h: h.tensor_reduce(
                            out=gst[:nt, 12:16], in_=gsq[:nt, :].rearrange("p (h e) -> p h e", h=4), axis=AX.X, op=ALU.add))
                        A("act", [gst], [gst], lambda h: h.activation(out=gst[:nt, 12:16], in_=gst[:nt, 12:16], func=AF.Sqrt,
                                                                      scale=1.0 / 128, bias=EPS))
                        A("dve", [gst], [gst], lambda h: h.reciprocal(out=gst[:nt, 12:16], in_=gst[:nt, 12:16]))
                        A("dve", [ob, gst], [gtmp], lambda h: h.tensor_tensor(
                            out=gv, in0=ob[:nt], in1=gst[:nt, 12:16].unsqueeze(2).to_broadcast([nt, 4, 128]), op=ALU.mult))
                        A("dve", [gtmp, dgn], [mix], lambda h: h.tensor_tensor(
                            out=mix[:nt, 512:1024].rearrange("p (h e) -> p h e", h=4), in0=gv,
                            in1=dgn[:nt, :].unsqueeze(1).to_broadcast([nt, 4, 128]), op=ALU.mult))
                        pt = transpose_blocks(mix, nt, 8, None)
                        A("act", [pt], [mT], lambda h, pt=pt: h.activation(
                            out=mT[:, :, :nt], in_=pt[:, :].rearrange("p (c t) -> p c t", c=8)[:, :, :nt], func=AF.Copy))
                        xn_ = xo.next()
                        for blk in range(2):
                            pp = psf.next()
                            def f(h, pp=pp, blk=blk):
                                for c in range(8):
                                    ins = h.matmul(pp[:nt, :], mT[:, c, :nt], wOut[:, c, blk * 512:(blk + 1) * 512],
                                                   start=(c == 0), stop=(c == 7))
                                return ins
                            A("pe", [mT, wOut], [pp], f)
                            A("dve", [pp, G1], [ot], lambda h, pp=pp, blk=blk: h.tensor_tensor(
                                out=ot[:nt, blk * 512:(blk + 1) * 512], in0=pp[:nt, :],
                                in1=G1[:nt, blk * 512:(blk + 1) * 512], op=ALU.mult))
                        A("dve", [ot, xt], [xn_], lambda h, xn_=xn_, xt=xt: h.tensor_tensor(
                            out=xn_[:nt, :], in0=ot[:nt, :], in1=xt[:nt, :], op=ALU.add))
                        fw.store("sp", xn_, xs[r0:r0 + nt, :], xn_[:nt, :], dk=xtrk(r0))
                    dst = O["ret_s"] if samp else O["ret_p"][s]
                    fw.store("sp", stf, dst.rearrange("h d e -> d h e"), stf[:].rearrange("p (h e) -> p h e", h=4))
                fw.barrier()
            fw.release_phase()

            def ffn_phase(l):
                with ExitStack() as ph:
                    fw.ts = ph
                    wUp = fw.sb("wUp", [128, 8, 2 * DFF], BF16)
                    wDn = fw.sb("wDn", [128, 22, D], BF16)
                    for kc in range(8):
                        fw.load("pool", wUp, wUp[:, kc, :], I["w_up"][l, kc * 128:(kc + 1) * 128, :])
                    for kc in range(22):
                        fw.load("pool", wDn, wDn[:, kc, :], I["w_down"][l, kc * 128:(kc + 1) * 128, :])
                    G2 = fw.sb("G2", [128, D], F32)
                    xrot = Rot([fw.sb(f"xf{i}", [128, D], F32) for i in range(3)])
                    scr = fw.sb("scrf", [128, D], BF16)
                    st4 = fw.sb("st4f", [128, 4], F32)
                    hT = fw.sb("hTf", [128, 8, 128], BF16)
                    gbs = [fw.sb(f"gb{i}", [128, 22, 130], F32) for i in range(2)]
                    cv = fw.sb("cv", [128, 6, 128], F32)
                    tA = fw.sb("tA", [128, 6, 128], F32)
                    tB = fw.sb("tB", [128, 6, 128], F32)
                    abs_ = [fw.sb(f"ab{i}", [128, 22, 128], BF16) for i in range(2)]
                    act = fw.sb("actT", [128, 22, 128], BF16)
                    xo = Rot([fw.sb(f"xof{i}", [128, D], F32) for i in range(1)])
                    fo = fw.sb("fo", [128, 22, 2], F32)
                    K0 = math.sqrt(2.0 / math.pi)
                    QB = [(0, 6), (6, 12), (12, 18), (18, 22)]
                    for s, (ntok, rp0, samp, row0) in enumerate(SEQS):
                        if str(s) not in SEQSEL:
                            continue
                        ntl = min(MAXT, (ntok + 127) // 128)
                        fw.load("sp", G2, G2[:], modd[s:s + 1, l, 40 * 128:48 * 128].partition_broadcast(128), dk=mk)
                        gb0 = gbs[0]
                        if samp:
                            for c in range(22):
                                fw.load("sp", gb0, gb0[:, c, 0:2],
                                        I["ffnc"][l, :, c * 128:(c + 1) * 128].rearrange("t p -> p t"),
                                        allow_slow_non_contiguous=True)
                        else:
                            A("dve", [], [gb0], lambda h: h.memset(gb0[:, :, 0:2], 0.0))
                        xts = {}

                        def ntk(ti):
                            return min(128, ntok - ti * 128)

                        def ldx(ti):
                            xt = xrot.next()
                            xts[ti] = xt
                            r0 = row0 + ti * 128
                            fw.load("sp", xt, xt[:ntk(ti), :], xs[r0:r0 + ntk(ti), :], dk=xtrk(r0))

                        def up_pairs(ti, cps):
                            nt = ntk(ti)
                            gb, ab = gbs[ti % 2], abs_[ti % 2]
                            for cp in cps:
                                pg = psf.next()
                                def f(h, pg=pg, cp=cp):
                                    for q_ in range(4):
                                        c = 2 * cp + (q_ % 2)
                                        col = (DFF if q_ >= 2 else 0) + c * 128
                                        for kc in range(8):
                                            ins = h.matmul(pg[:, q_ * 128:q_ * 128 + nt], wUp[:, kc, col:col + 128],
                                                           hT[:, kc, :nt], start=(kc == 0), stop=(kc == 7))
                                    return ins
                                A("pe", [wUp, hT], [pg], f)
                                A("act", [pg], [gb], lambda h, pg=pg, cp=cp: h.activation(
                                    out=gb[:, 2 * cp:2 * cp + 2, 2:2 + nt],
                                    in_=pg[:, 256:512].rearrange("p (c t) -> p c t", c=2)[:, :, :nt], func=AF.Copy))
                                A("dve", [pg], [ab], lambda h, pg=pg, cp=cp: h.tensor_copy(
                                    out=ab[:, 2 * cp:2 * cp + 2, :nt],
                                    in_=pg[:, 0:256].rearrange("p (c t) -> p c t", c=2)[:, :, :nt]))

                        def carry(ti):
                            nt = ntk(ti)
                            gb, gbn = gbs[ti % 2], gbs[(ti + 1) % 2]
                            A("pool", [gb], [fo], lambda h: h.tensor_copy(out=fo[:, :, :], in_=gb[:, :, nt:nt + 2]))
                            A("pool", [fo], [gbn], lambda h: h.tensor_copy(out=gbn[:, :, 0:2], in_=fo[:, :, :]))

                        def ew(ti, q):
                            nt = ntk(ti)
                            gb, ab = gbs[ti % 2], abs_[ti % 2]
                            c0, c1 = QB[q]
                            n = c1 - c0
                            def wb(col):
                                return fcw[:, c0:c1, col:col + 1].to_broadcast([128, n, nt])
                            cvv, tAv, tBv = cv[:, 0:n, :nt], tA[:, 0:n, :nt], tB[:, 0:n, :nt]
                            A("dve", [gb, fcw], [cv], lambda h: h.tensor_tensor(out=cvv, in0=gb[:, c0:c1, 2:2 + nt],
                                                                              in1=wb(l * 3 + 2), op=ALU.mult))
                            A("dve", [gb, fcw], [tA], lambda h: h.tensor_tensor(out=tAv, in0=gb[:, c0:c1, 1:1 + nt],
                                                                              in1=wb(l * 3 + 1), op=ALU.mult))
                            A("dve", [cv, tA], [cv], lambda h: h.tensor_tensor(out=cvv, in0=cvv, in1=tAv, op=ALU.add))
                            A("dve", [gb, fcw], [tA], lambda h: h.tensor_tensor(out=tAv, in0=gb[:, c0:c1, 0:nt],
                                                                              in1=wb(l * 3 + 0), op=ALU.mult))
                            A("dve", [cv, tA], [cv], lambda h: h.tensor_tensor(out=cvv, in0=cvv, in1=tAv, op=ALU.add))
                            A("dve", [cv, fcw], [cv], lambda h: h.tensor_tensor(out=cvv, in0=cvv, in1=wb(6 + l), op=ALU.add))
                            A("act", [cv], [tA], lambda h: h.activation(out=tAv, in_=cvv, func=AF.Square))
                            A("act", [tA], [tA], lambda h: h.activation(out=tAv, in_=tAv, func=AF.Identity, scale=0.044715,
                                                                        bias=1.0))
                            A("dve", [tA, cv], [tA], lambda h: h.tensor_tensor(out=tAv, in0=tAv, in1=cvv, op=ALU.mult))
                            A("act", [tA], [tB], lambda h: h.activation(out=tBv, in_=tAv, func=AF.Sigmoid, scale=2.0 * K0))
                            A("dve", [tB, cv], [tB], lambda h: h.tensor_tensor(out=tBv, in0=tBv, in1=cvv, op=ALU.mult))
                            A("dve", [ab, tB], [act], lambda h: h.tensor_tensor(out=act[:, c0:c1, :nt], in0=ab[:, c0:c1, :nt],
                                                                               in1=tBv, op=ALU.mult))

                        def down(ti):
                            nt = ntk(ti)
                            r0 = row0 + ti * 128
                            xt = xts.pop(ti)
                            xn_ = xo.next()
                            for blk in range(2):
                                pp = psf.next()
                                def f(h, pp=pp, blk=blk):
                                    for c in range(22):
                                        ins = h.matmul(pp[:nt, :], act[:, c, :nt], wDn[:, c, blk * 512:(blk + 1) * 512],
                                                       start=(c == 0), stop=(c == 21))
                                    return ins
                                A("pe", [act, wDn], [pp], f)
                                A("dve", [pp, G2], [xn_], lambda h, pp=pp, blk=blk: h.tensor_tensor(
                                    out=xn_[:nt, blk * 512:(blk + 1) * 512], in0=pp[:nt, :],
                                    in1=G2[:nt, blk * 512:(blk + 1) * 512], op=ALU.mult))
                            A("dve", [xn_, xt], [xn_], lambda h, xn_=xn_, xt=xt: h.tensor_tensor(
                                out=xn_[:nt, :], in0=xn_[:nt, :], in1=xt[:nt, :], op=ALU.add))
                            fw.store("sp", xn_, xs[r0:r0 + nt, :], xn_[:nt, :], dk=xtrk(r0))

                        PAIRS = [(0, 1, 2), (3, 4, 5), (6, 7, 8), (9, 10)]
                        ldx(0)
                        if ntl > 1:
                            ldx(1)
                        _rms_ffn(xts[0], ntk(0), hT, l, s, scr, st4)
                        for q in range(4):
                            up_pairs(0, PAIRS[q])
                        carry(0)
                        for ti in range(ntl):
                            if ti + 2 < ntl:
                                ldx(ti + 2)
                            if ti + 1 < ntl:
                                _rms_ffn(xts[ti + 1], ntk(ti + 1), hT, l, s, scr, st4)
                            for q in range(4):
                                ew(ti, q)
                                if ti + 1 < ntl:
                                    up_pairs(ti + 1, PAIRS[q])
                            if ti + 1 < ntl:
                                carry(ti + 1)
                            down(ti)
                        dst = O["ffn_s"][l] if samp else O["ffn_p"][l, s]
                        for c in range(22):
                            fw.store("sp", fo, dst[:, c * 128:(c + 1) * 128].rearrange("t p -> p t"), fo[:, c, :],
                                     allow_slow_non_contiguous=True)
                    fw.barrier()
                fw.release_phase()

            def _rms_ffn(xt, nt, hT, l, s, scr, st4):
                class _V:
                    def __init__(self, t, c0):
                        self.t, self.c0, self.k = t, c0, t.k
                    def __getitem__(self, idx):
                        p, c, ss = idx
                        return self.t[p, self.c0 + c, ss]
                sh = _V(modT[l], 24)
                A("act", [xt], [scr, st4], lambda h: h.activation(out=scr[:nt, :], in_=xt[:nt, :], func=AF.Square,
                                                                 accum_out=st4[:nt, 0:1]))
                A("act", [st4], [st4], lambda h: h.activation(out=st4[:nt, 1:2], in_=st4[:nt, 0:1], func=AF.Sqrt,
                                                              scale=1.0 / D, bias=EPS))
                A("dve", [st4], [st4], lambda h: h.reciprocal(out=st4[:nt, 2:3], in_=st4[:nt, 1:2]))
                A("dve", [xt, st4], [scr], lambda h: h.tensor_scalar(out=scr[:nt, :], in0=xt[:nt, :],
                                                                    scalar1=st4[:nt, 2:3], scalar2=None, op0=ALU.mult))
                pt = ptb.next()
                def f(h):
                    for c in range(8):
                        ins = h.transpose(out=pt[:, c * 128:c * 128 + nt], in_=scr[:nt, c * 128:(c + 1) * 128],
                                          identity=idb[:nt, :nt])
                    return ins
                A("pe", [scr, idb], [pt], f)
                for c in range(8):
                    A("act", [pt, GS[l][1], modT[l]], [hT], lambda h, c=c: h.activation(
                        out=hT[:, c, :nt], in_=pt[:, c * 128:c * 128 + nt], func=AF.Identity,
                        scale=GS[l][1][:, c, s:s + 1], bias=modT[l][:, 24 + c, s:s + 1]))


            def cd_phase():
              with ExitStack() as ph:
                fw.ts = ph
                wIn = fw.sb("wIn2", [128, 8, 3200], BF16)
                wOut = fw.sb("wOut2", [128, 8, D], BF16)
                for kc in range(8):
                    fw.load("pool", wIn, wIn[:, kc, 0:3080], I["w_in_cd"][kc * 128:(kc + 1) * 128, :])
                for kc in range(8):
                    fw.load("pool", wOut, wOut[:, kc, :], I["w_out_cd"][kc * 128:(kc + 1) * 128, :])
                kcT = fw.sb("kcT", [128, 4, 2176], BF16)
                Vc = fw.sb("Vc2", [128, 8, 8, 96], BF16)
                bRel = fw.sb("bRel", [128, 8, 2, 128], BF16)
                bM4 = fw.sb("bM4", [128, 128], BF16)
                crel = fw.sb("crel", [128, 8], F32)
                mn0 = fw.sb("mn0c", [128, 128], F32)
                mn4 = fw.sb("mn4c", [128, 128], F32)
                rmask = fw.sb("rmaskc", [128, 128], F32)
                trigt = fw.sb("trigt", [128, 128], F32)
                ones = fw.sb("onesc", [128, 128], F32)
                dcr = fw.sb("dcr", [5, D], F32)
                dcw = fw.sb("dcw", [128, 8, 5], F32)
                dng = fw.sb("dng", [128, 512], F32)
                dsk = fw.sb("dsk", [128, 8], F32)
                dtb = fw.sb("dtb", [128, 8], F32)
                Aneg = fw.sb("Aneg", [128, 8], F32)
                G1 = fw.sb("G1c", [128, D], F32)
                fw.load("sp", crel, crel[:], I["rel_table"][256:257, :].partition_broadcast(128))
                fw.load("sp", mn0, mn0[:], I["mneg"][0])
                fw.load("sp", mn4, mn4[:], I["mneg"][1])
                fw.load("sp", rmask, rmask[:], I["retmask"])
                fw.load("sp", trigt, trigt[:], I["trigt"])
                fw.load("sp", dcr, dcr[0:4, :], I["d_conv_w"])
                fw.load("sp", dcr, dcr[4:5, :], I["d_conv_b"])
                fw.load("sp", dng, dng[:], I["d_norm_g"].partition_broadcast(128))
                fw.load("sp", dsk, dsk[:], I["d_skip"].partition_broadcast(128))
                fw.load("sp", dtb, dtb[:], I["d_dt_bias"].partition_broadcast(128))
                fw.load("sp", Aneg, Aneg[:], I["d_a_log"].partition_broadcast(128))
                A("act", [Aneg], [Aneg], lambda h: h.activation(out=Aneg[:], in_=Aneg[:], func=AF.Exp))
                A("dve", [Aneg], [Aneg], lambda h: h.tensor_scalar(out=Aneg[:], in0=Aneg[:], scalar1=-1.0, scalar2=None,
                                                                  op0=ALU.mult))
                A("dve", [], [ones], lambda h: h.memset(ones[:], 1.0))
                A("dve", [mn4], [bM4], lambda h: h.tensor_copy(out=bM4[:], in_=mn4[:]))
                A("dve", [], [Vc], lambda h: h.memset(Vc[:, :, :, 64:96], 1.0))
                rows_to_fm(dcr, 5, 8, dcw)
                bias_tiles("rel", bRel, 8, sreld, srk, [1, 129], False, crel, [mn0, None])

                xrot = Rot([fw.sb(f"xc{i}", [128, D], F32) for i in range(2)])
                scr = fw.sb("scrc", [128, D], BF16)
                st4 = fw.sb("st4c", [128, 4], F32)
                hT = fw.sb("hTc", [128, 8, 128], BF16)
                qc = fw.sb("qc", [128, 512], BF16)
                kc16 = fw.sb("kc16", [128, 512], BF16)
                kcf = Rot([fw.sb(f"kcf{i}", [128, 512], F32) for i in range(2)])
                vcf = Rot([fw.sb(f"vcf{i}", [128, 512], F32) for i in range(2)])
                qcT = fw.sb("qcT", [128, 4, 128], BF16)
                zs = fw.sb("zs", [128, 512], F32)
                ze = fw.sb("ze", [128, 512], F32)
                dts = fw.sb("dts", [128, 40], F32)
                xbuf = fw.sb("xbuf", [128, 8, 132], F32)
                xo3 = fw.sb("xo3", [128, 8, 3], F32)
                cvx = fw.sb("cvx", [128, 8, 128], F32)
                sle = fw.sb("sle", [128, 8, 128], F32)
                xbb = fw.sb("xbb", [128, 8, 128], BF16)
                xtok = fw.sb("xtok", [128, 768], BF16)
                xw = fw.sb("xw", [128, 512], BF16)
                Rm = fw.sb("Rm", [128, 8, 128], F32)
                Eh = fw.sb("Eh", [128, 8, 128], F32)
                cbm = fw.sb("cbm", [128, 2, 128], F32)
                WT = fw.sb("WT", [128, 8, 128], BF16)
                stT = fw.sb("stT", [128, 512], F32)
                stTb = fw.sb("stTb", [128, 512], BF16)
                sld = fw.sb("sld", [64, 8, 128], F32)
                yin = fw.sb("yin", [128, 512], F32)
                yt_ = fw.sb("ytc", [128, 512], F32)
                gsq = fw.sb("gsqc", [128, 512], F32)
                gst = fw.sb("gstc", [128, 8], F32)
                mix = fw.sb("mixc", [128, D], BF16)
                pTt = Rot([fw.sb(f"pTc{i}", [128, 4, 128], BF16) for i in range(3)])
                rc = fw.sb("rcc", [128, 4], F32)
                mT = fw.sb("mTc", [128, 8, 128], BF16)
                xo = Rot([fw.sb(f"xoc{i}", [128, D], F32) for i in range(2)])
                ot = fw.sb("otc", [128, D], F32)

                for s, (ntok, rp0, samp, row0) in enumerate(SEQS):
                    if str(s) not in SEQSEL:
                        continue
                    ntl = min(MAXT, (ntok + 127) // 128)
                    fw.load("sp", G1, G1[:], modd[s:s + 1, 1, 16 * 128:24 * 128].partition_broadcast(128), dk=mk)
                    if samp:
                        A("dve", [], [kcT], lambda h: h.memset(kcT[:, :, 1088:1152], 0.0))
                        A("dve", [], [Vc], lambda h: h.memset(Vc[64:128, 0, :, 0:64], 0.0))
                        for j in range(4):
                            kf = kcf.next()
                            fw.load("sp", kf, kf[:], I["ck_c"][j * 128:(j + 1) * 128, :])
                            A("dve", [kf], [kc16], lambda h, kf=kf: h.tensor_copy(out=kc16[:], in_=kf[:]))
                            pt = transpose_blocks(kc16, 128, 4, None)
                            A("act", [pt], [kcT], lambda h, pt=pt, j=j: h.activation(
                                out=kcT[:, :, (4 + j) * 128:(5 + j) * 128],
                                in_=pt[:, 0:512].rearrange("p (c t) -> p c t", c=4), func=AF.Copy))
                            vf = vcf.next()
                            fw.load("sp", vf, vf[:], I["cv_c"][j * 128:(j + 1) * 128, :])
                            A("dve", [vf], [Vc], lambda h, vf=vf, j=j: h.tensor_copy(
                                out=Vc[:, (4 + j) % 8, :, 0:64], in_=vf[:].rearrange("p (h e) -> p h e", h=8)))
                        for c in range(8):
                            fw.load("sp", xbuf, xbuf[:, c, 0:3],
                                    I["dconv_c"][:, c * 128:(c + 1) * 128].rearrange("t p -> p t"),
                                    allow_slow_non_contiguous=True)
                        fw.load("sp", sld, sld[:], I["dssm_c"].rearrange("h p n -> p h n"))
                        pp = psf.next()
                        def f(h, pp=pp):
                            for hh in range(8):
                                ins = h.matmul(pp[:, hh * 64:(hh + 1) * 64], sld[:, hh, :], idf[0:64, 0:64],
                                               start=True, stop=True)
                            return ins
                        A("pe", [sld, idf], [pp], f)
                        A("dve", [pp], [stT], lambda h, pp=pp: h.tensor_copy(out=stT[:], in_=pp[:]))
                    else:
                        A("dve", [], [stT], lambda h: h.memset(stT[:], 0.0))
                        A("dve", [], [xbuf], lambda h: h.memset(xbuf[:, :, 0:3], 0.0))
                    A("act", [stT], [stTb], lambda h: h.activation(out=stTb[:], in_=stT[:], func=AF.Copy))
                    for ti in range(ntl):
                        nt = min(128, ntok - ti * 128)
                        r0 = row0 + ti * 128
                        qi = 8 if samp else ti
                        xt = xrot.next()
                        fw.load("sp", xt, xt[:nt, :], xs[r0:r0 + nt, :], dk=xtrk(r0))
                        rmsnorm_to_hT(xt, nt, hT, GS[1][0], modT[1], s, scr, st4)

                        def proj(c0, n, nrow=None):
                            pp = psf.next()
                            def f(h):
                                for c in range(8):
                                    ins = h.matmul(pp[:nt, 0:n], hT[:, c, :nt], wIn[:, c, c0:c0 + n],
                                                   start=(c == 0), stop=(c == 7))
                                return ins
                            A("pe", [hT, wIn], [pp], f)
                            return pp
                        pp = proj(0, 512)
                        A("act", [pp], [qc], lambda h, pp=pp: h.activation(out=qc[:nt, :], in_=pp[:nt, :], func=AF.Copy,
                                                                         scale=0.125))
                        pp = proj(512, 512)
                        kf = kcf.next()
                        A("act", [pp], [kf], lambda h, pp=pp, kf=kf: h.activation(out=kf[:nt, :], in_=pp[:nt, :], func=AF.Copy))
                        A("dve", [kf], [kc16], lambda h, kf=kf: h.tensor_copy(out=kc16[:nt, :], in_=kf[:nt, :]))
                        pp = proj(1024, 512)
                        vf = vcf.next()
                        A("act", [pp], [vf], lambda h, pp=pp, vf=vf: h.activation(out=vf[:nt, :], in_=pp[:nt, :], func=AF.Copy))
                        A("dve", [pp], [Vc], lambda h, pp=pp: h.tensor_copy(
                            out=Vc[:nt, qi % 8, :, 0:64], in_=pp[:nt, :].rearrange("p (h e) -> p h e", h=8)))
                        if samp:
                            fw.store("sp", kf, O["ck_s"][0:nt, :], kf[:nt, :])
                            fw.store("sp", vf, O["cv_s"][0:nt, :], vf[:nt, :])
                        elif ti >= 12:
                            fw.store("sp", kf, O["ck_p"][s, (ti - 12) * 128:(ti - 11) * 128, :], kf[:nt, :])
                            fw.store("sp", vf, O["cv_p"][s, (ti - 12) * 128:(ti - 11) * 128, :], vf[:nt, :])
                        pp = proj(1536, 512)
                        A("act", [pp], [ze], lambda h, pp=pp: h.activation(out=ze[:nt, :], in_=pp[:nt, :], func=AF.Sigmoid))
                        A("dve", [pp, ze], [zs], lambda h, pp=pp: h.tensor_tensor(out=zs[:nt, :], in0=pp[:nt, :], in1=ze[:nt, :],
                                                                                 op=ALU.mult))
                        pp = proj(3072, 8)
                        A("dve", [pp, dtb], [dts], lambda h, pp=pp: h.tensor_tensor(out=dts[:nt, 0:8], in0=pp[:nt, 0:8],
                                                                                   in1=dtb[:nt, :], op=ALU.add))
                        A("act", [dts], [dts], lambda h: h.activation(out=dts[:nt, 0:8], in_=dts[:nt, 0:8], func=AF.Exp))
                        A("act", [dts], [dts], lambda h: h.activation(out=dts[:nt, 0:8], in_=dts[:nt, 0:8], func=AF.Ln, bias=1.0))
                        A("dve", [dts, Aneg], [dts], lambda h: h.tensor_tensor(out=dts[:nt, 8:16], in0=dts[:nt, 0:8],
                                                                              in1=Aneg[:nt, :], op=ALU.mult))
                        for half in range(2):
                            pp = psf.next()
                            def f(h, pp=pp, half=half):
                                for cc in range(4):
                                    c = half * 4 + cc
                                    for kc in range(8):
                                        ins = h.matmul(pp[:, cc * 128:cc * 128 + nt],
                                                       wIn[:, kc, 2048 + c * 128:2048 + (c + 1) * 128], hT[:, kc, :nt],
                                                       start=(kc == 0), stop=(kc == 7))
                                return ins
                            A("pe", [wIn, hT], [pp], f)
                            A("act", [pp], [xbuf], lambda h, pp=pp, half=half: h.activation(
                                out=xbuf[:, half * 4:(half + 1) * 4, 3:3 + nt],
                                in_=pp[:, :].rearrange("p (c t) -> p c t", c=4)[:, :, :nt], func=AF.Copy))
                        def wb(col):
                            return dcw[:, :, col:col + 1].to_broadcast([128, 8, nt])
                        cvv, tAv = cvx[:, :, :nt], sle[:, :, :nt]
                        A("dve", [xbuf, dcw], [cvx], lambda h: h.tensor_tensor(out=cvv, in0=xbuf[:, :, 3:3 + nt], in1=wb(3),
                                                                            op=ALU.mult))
                        for tp in range(3):
                            A("dve", [xbuf, dcw], [sle], lambda h, tp=tp: h.tensor_tensor(out=tAv, in0=xbuf[:, :, tp:tp + nt],
                                                                                          in1=wb(tp), op=ALU.mult))
                            A("dve", [cvx, sle], [cvx], lambda h: h.tensor_tensor(out=cvv, in0=cvv, in1=tAv, op=ALU.add))
                        A("pool", [cvx, dcw], [cvx], lambda h: h.tensor_tensor(out=cvv, in0=cvv, in1=wb(4), op=ALU.add))
                        A("pool", [xbuf], [xo3], lambda h: h.tensor_copy(out=xo3[:], in_=xbuf[:, :, nt:nt + 3]))
                        A("pool", [xo3], [xbuf], lambda h: h.tensor_copy(out=xbuf[:, :, 0:3], in_=xo3[:]))
                        A("act", [cvx], [sle], lambda h: h.activation(out=sle[:, :, :nt], in_=cvx[:, :, :nt], func=AF.Sigmoid))
                        A("dve", [sle, cvx], [xbb], lambda h: h.tensor_tensor(out=xbb[:, :, :nt], in0=sle[:, :, :nt],
                                                                             in1=cvx[:, :, :nt], op=ALU.mult))
                        pt = ptb.next()
                        def f(h, pt=pt):
                            for c in range(6):
                                ins = h.transpose(out=pt[:nt, c * 128:(c + 1) * 128], in_=xbb[:, c, :nt], identity=idb[:, :])
                            return ins
                        A("pe", [xbb, idb], [pt], f)
                        A("act", [pt], [xtok], lambda h, pt=pt: h.activation(out=xtok[:nt, :], in_=pt[:nt, 0:768], func=AF.Copy))
                        pt = transpose_blocks(qc, nt, 4, None)
                        A("act", [pt], [qcT], lambda h, pt=pt: h.activation(
                            out=qcT[:, :, :nt], in_=pt[:, 0:512].rearrange("p (c t) -> p c t", c=4)[:, :, :nt], func=AF.Copy))
                        pt = transpose_blocks(kc16, nt, 4, None)
                        A("act", [pt], [kcT], lambda h, pt=pt: h.activation(
                            out=kcT[:, :, qi * 128:qi * 128 + nt],
                            in_=pt[:, 0:512].rearrange("p (c t) -> p c t", c=4)[:, :, :nt], func=AF.Copy))
                        pc = psf.next()
                        def f(h, pc=pc):
                            h.matmul(pc[:nt, 0:8], rmask[:nt, :nt], dts[:nt, 8:16], start=True, stop=True)
                            h.matmul(pc[:nt, 8:16], trigt[:nt, :nt], dts[:nt, 8:16], start=True, stop=True)
                            return h.matmul(pc[:, 16:24], ones[:nt, :], dts[:nt, 8:16], start=True, stop=True)
                        A("pe", [rmask, trigt, ones, dts], [pc], f)
                        A("act", [pc], [dts], lambda h, pc=pc: h.activation(out=dts[:nt, 16:32], in_=pc[:nt, 0:16], func=AF.Exp))
                        A("act", [pc], [dts], lambda h, pc=pc: h.activation(out=dts[:, 32:40], in_=pc[:, 16:24], func=AF.Exp))
                        A("dve", [dts], [dts], lambda h: h.tensor_tensor(out=dts[:nt, 24:32], in0=dts[:nt, 24:32],
                                                                        in1=dts[:nt, 0:8], op=ALU.mult))
                        pcb = psf.next()
                        def f(h, pcb=pcb):
                            for g in range(2):
                                ins = h.matmul(pcb[:nt, g * 128:g * 128 + nt], xbb[:, 4 + g, :nt], xbb[:, 6 + g, :nt],
                                               start=True, stop=True)
                            return ins
                        A("pe", [xbb], [pcb], f)
                        A("dve", [pcb, rmask], [cbm], lambda h, pcb=pcb: h.tensor_tensor(
                            out=cbm[:nt, :, :nt], in0=pcb[:nt, 0:256].rearrange("p (g t) -> p g t", g=2)[:, :, :nt],
                            in1=rmask[:nt, :nt].unsqueeze(1).to_broadcast([nt, 2, nt]), op=ALU.mult))
                        px = pacc.next()
                        def f(h, px=px):
                            for hh in range(8):
                                ins = h.matmul(px[:nt, hh * 64:(hh + 1) * 64], xbb[:, 6 + hh // 4, :nt],
                                               stTb[:, hh * 64:(hh + 1) * 64], start=True, stop=True)
                            return ins
                        A("pe", [xbb, stTb], [px], f)
                        A("dve", [rmask, dts], [Rm], lambda h: h.tensor_tensor(
                            out=Rm[:nt, :, :nt], in0=rmask[:nt, :nt].unsqueeze(1).to_broadcast([nt, 8, nt]),
                            in1=dts[:nt, 8:16].unsqueeze(2).to_broadcast([nt, 8, nt]), op=ALU.mult))
                        for half in range(2):
                            pg = psf.next()
                            def f(h, pg=pg, half=half):
                                for hh in range(4):
                                    ins = h.matmul(pg[:nt, hh * 128:hh * 128 + nt], trigt[:nt, :nt], Rm[:nt, half * 4 + hh, :nt],
                                                   start=True, stop=True)
                                return ins
                            A("pe", [trigt, Rm], [pg], f)
                            A("act", [pg], [Eh], lambda h, pg=pg, half=half: h.activation(
                                out=Eh[:nt, half * 4:(half + 1) * 4, :nt],
                                in_=pg[:nt, :].rearrange("p (c t) -> p c t", c=4)[:, :, :nt], func=AF.Exp))
                        A("dve", [Eh, dts], [Eh], lambda h: h.tensor_tensor(
                            out=Eh[:nt, :, :nt], in0=Eh[:nt, :, :nt],
                            in1=dts[:nt, 0:8].unsqueeze(2).to_broadcast([nt, 8, nt]), op=ALU.mult))
                        for g in range(2):
                            A("dve", [Eh, cbm], [WT], lambda h, g=g: h.tensor_tensor(
                                out=WT[:nt, g * 4:(g + 1) * 4, :nt], in0=Eh[:nt, g * 4:(g + 1) * 4, :nt],
                                in1=cbm[:nt, g:g + 1, :nt].to_broadcast([nt, 4, nt]), op=ALU.mult))
                        py = pacc.next()
                        def f(h, py=py):
                            for hh in range(8):
                                ins = h.matmul(py[:nt, hh * 64:(hh + 1) * 64], WT[:nt, hh, :nt], xtok[:nt, hh * 64:(hh + 1) * 64],
                                               start=True, stop=True)
                            return ins
                        A("pe", [WT, xtok], [py], f)
                        A("act", [py], [yin], lambda h, py=py: h.activation(out=yin[:nt, :], in_=py[:nt, :], func=AF.Copy))
                        A("dve", [px, dts], [yt_], lambda h, px=px: h.tensor_tensor(
                            out=yt_[:nt, :].rearrange("p (h e) -> p h e", h=8),
                            in0=px[:nt, :].rearrange("p (h e) -> p h e", h=8),
                            in1=dts[:nt, 16:24].unsqueeze(2).to_broadcast([nt, 8, 64]), op=ALU.mult))
                        A("dve", [yt_, yin], [yt_], lambda h: h.tensor_tensor(out=yt_[:nt, :], in0=yt_[:nt, :], in1=yin[:nt, :],
                                                                              op=ALU.add))
                        A("dve", [xtok, dsk], [yin], lambda h: h.tensor_tensor(
                            out=yin[:nt, :].rearrange("p (h e) -> p h e", h=8),
                            in0=xtok[:nt, 0:512].rearrange("p (h e) -> p h e", h=8),
                            in1=dsk[:nt, :].unsqueeze(2).to_broadcast([nt, 8, 64]), op=ALU.mult))
                        A("dve", [yt_, yin], [yt_], lambda h: h.tensor_tensor(out=yt_[:nt, :], in0=yt_[:nt, :], in1=yin[:nt, :],
                                                                              op=ALU.add))
                        A("dve", [xtok, dts], [xw], lambda h: h.tensor_tensor(
                            out=xw[:nt, :].rearrange("p (h e) -> p h e", h=8),
                            in0=xtok[:nt, 0:512].rearrange("p (h e) -> p h e", h=8),
                            in1=dts[:nt, 24:32].unsqueeze(2).to_broadcast([nt, 8, 64]), op=ALU.mult))
                        pd = psf.next()
                        def f(h, pd=pd):
                            for hh in range(8):
                                g = hh // 4
                                ins = h.matmul(pd[:, hh * 64:(hh + 1) * 64], xtok[:nt, 512 + g * 128:512 + (g + 1) * 128],
                                               xw[:nt, hh * 64:(hh + 1) * 64], start=True, stop=True)
                            return ins
                        A("pe", [xtok, xw], [pd], f)
                        A("dve", [stT, dts], [stT], lambda h: h.tensor_tensor(
                            out=stT[:].rearrange("p (h e) -> p h e", h=8), in0=stT[:].rearrange("p (h e) -> p h e", h=8),
                            in1=dts[:, 32:40].unsqueeze(2).to_broadcast([128, 8, 64]), op=ALU.mult))
                        A("dve", [pd, stT], [stT], lambda h, pd=pd: h.tensor_tensor(out=stT[:], in0=pd[:], in1=stT[:], op=ALU.add))
                        A("act", [stT], [stTb], lambda h: h.activation(out=stTb[:], in_=stT[:], func=AF.Copy))
                        A("dve", [yt_, zs], [yt_], lambda h: h.tensor_tensor(out=yt_[:nt, :], in0=yt_[:nt, :], in1=zs[:nt, :],
                                                                            op=ALU.mult))
                        A("act", [yt_], [gsq], lambda h: h.activation(out=gsq[:nt, :], in_=yt_[:nt, :], func=AF.Square))
                        A("dve", [gsq], [gst], lambda h: h.tensor_reduce(
                            out=gst[:nt, 0:2], in_=gsq[:nt, :].rearrange("p (g e) -> p g e", g=2), axis=AX.X, op=ALU.add))
                        A("act", [gst], [gst], lambda h: h.activation(out=gst[:nt, 0:2], in_=gst[:nt, 0:2], func=AF.Sqrt,
                                                                      scale=1.0 / 256, bias=EPS))
                        A("dve", [gst], [gst], lambda h: h.reciprocal(out=gst[:nt, 0:2], in_=gst[:nt, 0:2]))
                        A("dve", [yt_, gst], [yt_], lambda h: h.tensor_tensor(
                            out=yt_[:nt, :].rearrange("p (g e) -> p g e", g=2), in0=yt_[:nt, :].rearrange("p (g e) -> p g e", g=2),
                            in1=gst[:nt, 0:2].unsqueeze(2).to_broadcast([nt, 2, 256]), op=ALU.mult))
                        A("dve", [yt_, dng], [mix], lambda h: h.tensor_tensor(out=mix[:nt, 512:1024], in0=yt_[:nt, :],
                                                                              in1=dng[:nt, :], op=ALU.mult))
                        jlo = max(0, qi - 4)
                        if samp:
                            groups = [[4, 5, 6, 7], [8]]
                        else:
                            js = list(range(jlo, qi + 1))
                            groups = [js[:4], js[4:]] if len(js) > 4 else [js]
                        pas = {}
                        def qk_c(c4, i_, g):
                            u = 2 * c4 + i_
                            pS = psf.next()
                            def f(h):
                                for jj, j_ in enumerate(g):
                                    dl = j_ - qi
                                    extra = dl >= -1 or dl == -4
                                    ins = h.matmul(pS[:, jj * 128:jj * 128 + nt],
                                                   kcT[i_ * 64:(i_ + 1) * 64, c4, j_ * 128:(j_ + 1) * 128],
                                                   qcT[i_ * 64:(i_ + 1) * 64, c4, :nt], start=True, stop=not extra)
                                    if dl >= -1:
                                        ins = h.matmul(pS[:, jj * 128:jj * 128 + nt], idb[:, :],
                                                       bRel[:, u, -dl, :nt], start=False, stop=True)
                                    elif dl == -4:
                                        ins = h.matmul(pS[:, jj * 128:jj * 128 + nt], idb[:, :],
                                                       bM4[:, :nt], start=False, stop=True)
                                return ins
                            A("pe", [kcT, qcT, idb, bRel, bM4], [pS], f)
                            pt_ = pTt.next()
                            ng = len(g)
                            A("act", [pS, crel], [pt_], lambda h: h.activation(
                                out=pt_[:, 0:ng, :nt],
                                in_=pS[:, 0:ng * 128].rearrange("p (g t) -> p g t", g=ng)[:, :, :nt],
                                func=AF.Exp, bias=crel[:, u:u + 1]))
                            return pt_

                        def pv_c(c4, i_, g, pt_):
                            u = 2 * c4 + i_
                            if c4 not in pas:
                                pas[c4] = pacc.next()
                            pa = pas[c4]
                            def f(h):
                                for jj, j_ in enumerate(g):
                                    ins = h.matmul(pa[:nt, i_ * 256:i_ * 256 + 65], pt_[:, jj, :nt],
                                                   Vc[:, j_ % 8, u, 0:65], start=(j_ == groups[0][0]), stop=(j_ == qi))
                                return ins
                            A("pe", [pt_, Vc], [pa], f)
                            if i_ == 1 and g is groups[-1]:
                                pav = pa[:nt, 0:512].rearrange("p (i e) -> p i e", i=2)
                                A("dve", [pa], [rc], lambda h: h.reciprocal(out=rc[:nt, 0:2], in_=pav[:, :, 64]))
                                for ii in range(2):
                                    uu = 2 * c4 + ii
                                    A("dve", [pa, rc], [mix], lambda h, ii=ii, uu=uu: h.tensor_scalar(
                                        out=mix[:nt, uu * 64:(uu + 1) * 64], in0=pa[:nt, ii * 256:ii * 256 + 64],
                                        scalar1=rc[:nt, ii:ii + 1], scalar2=None, op0=ALU.mult))

                        items = [(c4, i_, g) for c4 in range(4) for i_ in range(2) for g in groups]
                        prev = None
                        for it in items:
                            cur = qk_c(*it)
                            if prev is not None:
                                pv_c(*prev)
                            prev = it + (cur,)
                        pv_c(*prev)
                        pt = transpose_blocks(mix, nt, 8, None)
                        A("act", [pt], [mT], lambda h, pt=pt: h.activation(
                            out=mT[:, :, :nt], in_=pt[:, :].rearrange("p (c t) -> p c t", c=8)[:, :, :nt], func=AF.Copy))
                        xn_ = xo.next()
                        for blk in range(2):
                            pp = psf.next()
                            def f(h, pp=pp, blk=blk):
                                for c in range(8):
                                    ins = h.matmul(pp[:nt, :], mT[:, c, :nt], wOut[:, c, blk * 512:(blk + 1) * 512],
                                                   start=(c == 0), stop=(c == 7))
                                return ins
                            A("pe", [mT, wOut], [pp], f)
                            A("dve", [pp, G1], [ot], lambda h, pp=pp, blk=blk: h.tensor_tensor(
                                out=ot[:nt, blk * 512:(blk + 1) * 512], in0=pp[:nt, :],
                                in1=G1[:nt, blk * 512:(blk + 1) * 512], op=ALU.mult))
                        A("dve", [ot, xt], [xn_], lambda h, xn_=xn_, xt=xt: h.tensor_tensor(
                            out=xn_[:nt, :], in0=ot[:nt, :], in1=xt[:nt, :], op=ALU.add))
                        fw.store("sp", xn_, xs[r0:r0 + nt, :], xn_[:nt, :], dk=xtrk(r0))
                    dst = O["dconv_s"] if samp else O["dconv_p"][s]
                    for c in range(8):
                        fw.store("sp", xo3, dst[:, c * 128:(c + 1) * 128].rearrange("t p -> p t"), xo3[:, c, :],
                                 allow_slow_non_contiguous=True)
                    dst = O["dssm_s"] if samp else O["dssm_p"][s]
                    for half in range(2):
                        pp = psf.next()
                        def f(h, pp=pp, half=half):
                            for hh in range(4):
                                ins = h.matmul(pp[0:64, hh * 128:(hh + 1) * 128],
                                               stT[:, (half * 4 + hh) * 64:(half * 4 + hh + 1) * 64], idf[:, :],
                                               start=True, stop=True)
                            return ins
                        A("pe", [stT, idf], [pp], f)
                        A("act", [pp], [sld], lambda h, pp=pp, half=half: h.activation(
                            out=sld[:, half * 4:(half + 1) * 4, :], in_=pp[0:64, :].rearrange("p (h n) -> p h n", h=4),
                            func=AF.Copy))
                    fw.store("sp", sld, dst.rearrange("h p n -> p h n"), sld[:])
                fw.barrier()
              fw.release_phase()

            if 'F' in PH:
                ffn_phase(0)
            if 'C' in PH:
                cd_phase()
            if 'G' in PH:
                ffn_phase(1)

            with ExitStack() as ph:
              if 'Z' in PH:
                fw.ts = ph
                fg = fw.sb("fg", [128, D], F32)
                fw.load("sp", fg, fg[:], I["final_g"].partition_broadcast(128))
                xrot = Rot([fw.sb(f"xz{i}", [128, D], F32) for i in range(2)])
                yo = Rot([fw.sb(f"yo{i}", [128, D], F32) for i in range(2)])
                scr = fw.sb("scrz", [128, D], F32)
                st4 = fw.sb("st4z", [128, 4], F32)
                for s, (ntok, rp0, samp, row0) in enumerate(SEQS):
                    if str(s) not in SEQSEL:
                        continue
                    ntl = min(MAXT, (ntok + 127) // 128)
                    for ti in range(ntl):
                        nt = min(128, ntok - ti * 128)
                        r0 = row0 + ti * 128
                        xt = xrot.next()
                        fw.load("sp", xt, xt[:nt, :], xs[r0:r0 + nt, :], dk=xtrk(r0))
                        A("act", [xt], [scr, st4], lambda h: h.activation(out=scr[:nt, :], in_=xt[:nt, :], func=AF.Square,
                                                                         accum_out=st4[:nt, 0:1]))
                        A("act", [st4], [st4], lambda h: h.activation(out=st4[:nt, 1:2], in_=st4[:nt, 0:1], func=AF.Sqrt,
                                                                      scale=1.0 / D, bias=EPS))
                        A("dve", [st4], [st4], lambda h: h.reciprocal(out=st4[:nt, 2:3], in_=st4[:nt, 1:2]))
                        yt = yo.next()
                        A("dve", [xt, st4, fg], [yt], lambda h, yt=yt, xt=xt: h.scalar_tensor_tensor(
                            out=yt[:nt, :], in0=xt[:nt, :], scalar=st4[:nt, 2:3], in1=fg[:nt, :], op0=ALU.mult, op1=ALU.mult))
                        dst = O["y_s"] if samp else O["y_p"]
                        sr0 = 0 if samp else r0
                        fw.store("sp", yt, dst[sr0:sr0 + nt, :], yt[:nt, :])
                fw.barrier()
            fw.release_phase()
    print('OPS', fw.cnt, flush=True)
    return nc


_NC = None


def kernel(**inp):
    global _NC
    f = lambda a: np.ascontiguousarray(np.asarray(a, dtype=np.float32))
    consts = host_consts()
    if _NC is None:
        _NC = build()
    nc = _NC
    shared = {
        "w_mod": f(inp["w_mod"]), "b_mod": f(inp["b_mod"]).reshape(1, -1), "norm_g": f(inp["norm_g"]).reshape(4, D),
        "final_g": f(inp["final_g"]).reshape(1, D), "t5_table": f(inp["t5_table"]),
        "w_in_ab": f(inp["w_in_ab"][0]), "w_out_ab": f(inp["w_out_ab"][0]), "ret_gn": f(inp["ret_gn"]).reshape(1, 512),
        "lam_q": f(inp["lam_q"]).reshape(1, 128), "lam_k": f(inp["lam_k"]).reshape(1, 128),
        "diff_gn": f(inp["diff_gn"]).reshape(1, 128), "w_in_cd": f(inp["w_in_cd"][0]), "w_out_cd": f(inp["w_out_cd"][0]),
        "rel_table": f(inp["rel_table"][0]), "d_conv_w": f(inp["d_conv_w"][0]), "d_conv_b": f(inp["d_conv_b"]).reshape(1, D),
        "d_dt_bias": f(inp["d_dt_bias"]).reshape(1, 8), "d_a_log": f(inp["d_a_log"]).reshape(1, 8),
        "d_skip": f(inp["d_skip"]).reshape(1, 8), "d_norm_g": f(inp["d_norm_g"]).reshape(1, 512),
        "w_up": f(inp["w_up"]), "ffn_cw": f(inp["ffn_conv_w"]).reshape(6, DFF), "ffn_cb": f(inp["ffn_conv_b"]),
        "w_down": f(inp["w_down"]),
    }
    shared.update(consts)
    in_maps = []
    for i in range(8):
        m = dict(shared)
        m["xp"] = f(inp["x_prompt"][2 * i:2 * i + 2]).reshape(4096, D)
        m["xsm"] = f(inp["x_sample"][i])
        m["cc"] = np.concatenate([f(inp["c_prompt"][2 * i:2 * i + 2]), f(inp["c_sample"][i:i + 1])], 0)
        m["ret0"] = f(inp["cache_ret_state"][0, i])
        m["bk_c"] = f(inp["cache_b_k"][0, i]).reshape(1024, 512)
        m["bv_c"] = f(inp["cache_b_v"][0, i]).reshape(1024, 512)
        m["ck_c"] = f(inp["cache_c_k"][0, i]).reshape(512, 512)
        m["cv_c"] = f(inp["cache_c_v"][0, i]).reshape(512, 512)
        m["dconv_c"] = f(inp["state_d_conv"][0, i])
        m["dssm_c"] = f(inp["state_d_ssm"][0, i])
        m["ffnc"] = f(inp["state_ffn_conv"][:, i])
        in_maps.append(m)
    res = run_bass_kernel_spmd(nc, in_maps[:NCORES], core_ids=list(range(NCORES)))
    R = list(res.results) * (8 // NCORES)
    cat = lambda k: np.concatenate([np.asarray(R[i][k]) for i in range(8)], 0)
    stk = lambda k: np.stack([np.asarray(R[i][k]) for i in range(8)], 0)
    y_p = cat("y_p").reshape(16, 2048, D)
    y_s = stk("y_s")
    ret_p = cat("ret_p")[None]
    ret_s = stk("ret_s")[None]
    bk_p = cat("bk_p").reshape(1, 16, 2048, 4, 128)
    bk_s = stk("bk_s").reshape(1, 8, 64, 4, 128)
    bv_p = cat("bv_p").reshape(1, 16, 2048, 4, 128)
    bv_s = stk("bv_s").reshape(1, 8, 64, 4, 128)
    ck_p = cat("ck_p").reshape(1, 16, 512, 8, 64)
    ck_s = stk("ck_s").reshape(1, 8, 64, 8, 64)
    cv_p = cat("cv_p").reshape(1, 16, 512, 8, 64)
    cv_s = stk("cv_s").reshape(1, 8, 64, 8, 64)
    dconv_p = cat("dconv_p")[None]
    dconv_s = stk("dconv_s")[None]
    dssm_p = cat("dssm_p")[None]
    dssm_s = stk("dssm_s")[None]
    ffn_p = np.concatenate([np.asarray(R[i]["ffn_p"]) for i in range(8)], 1)
    ffn_s = np.stack([np.asarray(R[i]["ffn_s"]) for i in range(8)], 1)
    return (y_p, y_s, ret_p, ret_s, bk_p, bk_s, bv_p, bv_s, ck_p, ck_s, cv_p, cv_s,
            dconv_p, dconv_s, dssm_p, dssm_s, ffn_p, ffn_s)
```

```python
import math
from contextlib import ExitStack
import numpy as np
import concourse.bass as bass
import concourse.mybir as mybir
from concourse.bass_utils import run_bass_kernel_spmd

F32 = mybir.dt.float32
BF16 = mybir.dt.bfloat16
AF = mybir.ActivationFunctionType
ALU = mybir.AluOpType
AX = mybir.AxisListType
EPOCH = 12000
EPS = 1e-6
D = 1024
DFF = 2816
NEG = -30000.0


class Trk:
    __slots__ = ("lastw", "readers", "name", "ldsem", "ldcnt", "stsem", "stcnt")

    def __init__(self, name=""):
        self.lastw = None
        self.readers = []
        self.name = name
        self.ldsem = None
        self.ldcnt = 0
        self.stsem = None
        self.stcnt = 0


class DSem:
    def __init__(self, sem):
        self.sem = sem
        self.cnt = 0


class Tl:
    def __init__(self, t, name):
        self.t = t
        self.k = Trk(name)

    def __getitem__(self, idx):
        return self.t[idx]


class PTl(Tl):
    pass


class CView(Tl):
    def __init__(self, tl, c0):
        self.t = tl.t
        self.k = tl.k
        self.c0 = c0

    def __getitem__(self, idx):
        p, c, t = idx
        if isinstance(c, slice):
            c = slice((c.start or 0) + self.c0, (c.stop if c.stop is not None else 4) + self.c0)
        else:
            c = c + self.c0
        return self.t[p, c, t]


class Eng:
    def __init__(self, fw, name, h):
        self.name = name
        self.h = h
        self.ep = 0
        self.n = 0
        self.sem = fw.newsem(f"e_{name}_0")
        self.sems = [self.sem]
        self.seen = {}


class FW:
    def __init__(self, nc):
        self.nc = nc
        self.es = None
        self.ts = None
        self.nsem = 0
        self.E = {}
        self.dtoks = {}
        self.free_ds = []
        self.phase_ds = []

    def start(self, es):
        self.es = es
        self.ts = es
        nc = self.nc
        for name, h in (("pe", nc.tensor), ("act", nc.scalar), ("dve", nc.vector),
                        ("pool", nc.gpsimd), ("sp", nc.sync)):
            self.E[name] = Eng(self, name, h)

    def newsem(self, name):
        self.nsem += 1
        return self.es.enter_context(self.nc.semaphore(f"{name}_{self.nsem}"))

    def sb(self, name, shape, dt):
        self.nsem += 1
        name = f"{name}_{self.nsem}"
        return Tl(self.ts.enter_context(self.nc.sbuf_tensor(name, list(shape), dt)), name)

    def ps(self, name, shape, dt=F32):
        self.nsem += 1
        name = f"{name}_{self.nsem}"
        return PTl(self.ts.enter_context(self.nc.psum_tensor(name, list(shape), dt)), name)

    def _wait(self, e, tok):
        if tok is None:
            return
        key, n = tok
        if key[0] == "e" and key[1] == e.name and e.name == "pe":
            return
        if e.seen.get(key, 0) >= n:
            return
        e.seen[key] = n
        if key[0] == "e":
            sem = self.E[key[1]].sems[key[2]]
        else:
            sem = key[1].sem
        e.h.wait_ge(sem, n)

    def _deps(self, e, reads, writes):
        need = {}

        def add(tok):
            if tok is not None:
                need[tok[0]] = max(need.get(tok[0], 0), tok[1])
        for r in reads:
            k = r.k if isinstance(r, Tl) else r
            add(k.lastw)
        for w in writes:
            k = w.k if isinstance(w, Tl) else w
            add(k.lastw)
            for t in k.readers:
                add(t)
        for key, n in need.items():
            self._wait(e, (key, n))

    def _mark(self, tok, reads, writes):
        for r in reads:
            k = r.k if isinstance(r, Tl) else r
            k.readers.append(tok)
            if len(k.readers) > 32:
                k.readers = k.readers[-32:]
        for w in writes:
            k = w.k if isinstance(w, Tl) else w
            k.lastw = tok
            k.readers = []

    def op(self, en, reads, writes, fn):
        self.cnt = getattr(self, 'cnt', 0) + 1
        if self.cnt > LIMIT:
            return None
        e = self.E[en]
        if en != "pe":
            pr = [r for r in reads if isinstance(r, PTl)]
            if pr:
                reads = [r for r in reads if not isinstance(r, PTl)]
                writes = list(writes) + pr
        self._deps(e, reads, writes)
        ins = fn(e.h)
        e.n += 1
        ins.then_inc(e.sem, 1)
        tok = (("e", en, e.ep), e.n)
        self._mark(tok, reads, writes)
        if e.n >= EPOCH:
            e.ep += 1
            e.n = 0
            e.sem = self.newsem(f"e_{en}_{e.ep}")
            e.sems.append(e.sem)
        return tok

    def load(self, q, st, out_ap, in_ap, dk=None, **kw):
        self.cnt = getattr(self, 'cnt', 0) + 1
        if self.cnt > LIMIT:
            return None
        e = self.E[q]
        k = st.k
        if k.ldsem is None:
            k.ldsem = DSem(self.newsem("p")) if q == "pool" else self.get_ds()
        self._deps(e, [dk] if dk is not None else [], [st])
        e.h.dma_start(out=out_ap, in_=in_ap, **kw).then_inc(k.ldsem.sem, 16)
        k.ldsem.cnt += 16
        tok = (("d", k.ldsem), k.ldsem.cnt)
        k.lastw = tok
        k.readers = []
        if dk is not None:
            dk.readers.append(tok)
        self.dtoks[tok[0]] = tok[1]
        return tok

    def store(self, q, st, out_ap, in_ap, dk=None, **kw):
        self.cnt = getattr(self, 'cnt', 0) + 1
        if self.cnt > LIMIT:
            return None
        e = self.E[q]
        k = st.k
        if k.stsem is None:
            k.stsem = self.get_ds()
        self._deps(e, [st], [dk] if dk is not None else [])
        e.h.dma_start(out=out_ap, in_=in_ap, **kw).then_inc(k.stsem.sem, 16)
        k.stsem.cnt += 16
        tok = (("d", k.stsem), k.stsem.cnt)
        k.readers.append(tok)
        if dk is not None:
            dk.lastw = tok
            dk.readers = []
        self.dtoks[tok[0]] = tok[1]
        return tok

    def get_ds(self):
        if self.free_ds:
            d = self.free_ds.pop()
        else:
            d = DSem(self.newsem("d"))
        if self.ts is not self.es:
            self.phase_ds.append(d)
        return d

    def release_phase(self):
        self.free_ds.extend(self.phase_ds)
        self.phase_ds = []

    def barrier(self, only=None):
        if os.environ.get('K_SPFIN') and getattr(self, 'cnt', 0) > LIMIT:
            only = "sp"
        for en, e in self.E.items():
            if only is not None and en != only:
                continue
            for key, n in list(self.dtoks.items()):
                self._wait(e, (key, n))
            for sn, src in self.E.items():
                if sn != en and src.n > 0:
                    self._wait(e, (("e", sn, src.ep), src.n))
                if sn != en and src.ep > 0 and src.n == 0:
                    self._wait(e, (("e", sn, src.ep - 1), EPOCH))


class Rot:
    def __init__(self, tiles):
        self.tiles = tiles
        self.i = 0

    def next(self):
        t = self.tiles[self.i % len(self.tiles)]
        self.i += 1
        return t


def t5_bucket(rel):
    half = 16
    max_exact = 8
    base = np.where(rel > 0, half, 0)
    n = np.abs(rel)
    nf = np.maximum(n, 1).astype(np.float32)
    large = max_exact + (np.log(nf / max_exact) / math.log(128 / max_exact) * (half - max_exact)).astype(np.int32)
    large = np.minimum(large, half - 1)
    return base + np.where(n < max_exact, n, large)


def host_consts():
    c = {}
    c["ident"] = np.eye(128, dtype=np.float32)
    J = np.zeros((128, 128), np.float32)
    for j in range(128):
        J[j, 127 - j] = 1.0
    c["exch"] = J
    lg = np.log1p(-np.exp2(-5.0 - np.arange(4, dtype=np.float64)))
    inv = np.power(10000.0, -np.arange(64, dtype=np.float32) / 64).astype(np.float32)
    rope = np.zeros((17, 128, 4, 4, 64), np.float32)
    for ti in range(17):
        if ti < 16:
            pos = ti * 128 + np.arange(128)
        else:
            pos = 1024 + np.arange(128)
        lc = np.arange(128, dtype=np.float64)
        ang = pos.astype(np.float32)[:, None] * inv[None, :]
        cs, sn = np.cos(ang), np.sin(ang)
        for h in range(4):
            fq = np.exp((lc + 1.0) * lg[h])[:, None]
            fk = np.exp(-(lc + 1.0) * lg[h])[:, None] * (128.0 ** -0.5)
            rope[ti, :, 0, h] = cs * fq
            rope[ti, :, 1, h] = sn * fq
            rope[ti, :, 2, h] = cs * fk
            rope[ti, :, 3, h] = sn * fk
    c["rope"] = rope
    m = np.zeros((128, 128), np.float32)
    for mm in range(128):
        m[mm, mm:] = 1.0
    c["retmask"] = m
    c["trigt"] = np.ascontiguousarray(1.0 - m)
    gd = np.zeros((2, 128, 512), np.float32)
    for h in range(4):
        gd[0, :, h * 128:(h + 1) * 128] = np.exp(128 * lg[h])
        gd[1, :, h * 128:(h + 1) * 128] = np.exp(64 * lg[h])
    c["gdec"] = gd
    oh = np.zeros((32, 384), np.float32)
    rel = np.arange(384) - 255
    b = t5_bucket(rel)
    oh[b, np.arange(384)] = 1.0
    c["t5oh"] = oh
    m0 = np.zeros((128, 128), np.float32)
    m0[64:, :64] = NEG
    m4 = np.zeros((128, 128), np.float32)
    m4[:64, 64:] = NEG
    c["mneg"] = np.stack([m0, m4])
    return c


import os
PH = os.environ.get('K_PH', '0AFCGZ')
MAXT = int(os.environ.get('K_MAXT', '99'))
LIMIT = int(os.environ.get('K_LIMIT', '100000000'))
NCORES = int(os.environ.get('K_NCORES', '8'))
SEQSEL = os.environ.get('K_SEQS', '012')


def build():
    nc = bass.Bass("TRN2", target_bir_lowering=False)

    def din(n, s):
        return nc.dram_tensor(n, list(s), F32, kind="ExternalInput").ap()

    def dout(n, s):
        return nc.dram_tensor(n, list(s), F32, kind="ExternalOutput").ap()

    def dscr(n, s):
        return nc.dram_tensor(n, list(s), F32, kind="Internal").ap()

    I = {}
    for n, s in (("xp", [4096, D]), ("xsm", [64, D]), ("cc", [3, D]), ("ret0", [4, 128, 128]),
                 ("bk_c", [1024, 512]), ("bv_c", [1024, 512]), ("ck_c", [512, 512]), ("cv_c", [512, 512]),
                 ("dconv_c", [3, 1024]), ("dssm_c", [8, 64, 128]), ("ffnc", [2, 2, DFF]),
                 ("w_mod", [2, D, 6 * D]), ("b_mod", [1, 12 * D]), ("norm_g", [4, D]), ("final_g", [1, D]),
                 ("t5_table", [32, 4]), ("w_in_ab", [D, 3584]), ("w_out_ab", [D, D]), ("ret_gn", [1, 512]),
                 ("lam_q", [1, 128]), ("lam_k", [1, 128]), ("diff_gn", [1, 128]),
                 ("w_in_cd", [D, 3080]), ("w_out_cd", [D, D]), ("rel_table", [257, 8]),
                 ("d_conv_w", [4, D]), ("d_conv_b", [1, D]), ("d_dt_bias", [1, 8]), ("d_a_log", [1, 8]),
                 ("d_skip", [1, 8]), ("d_norm_g", [1, 512]), ("w_up", [2, D, 2 * DFF]),
                 ("ffn_cw", [6, DFF]), ("ffn_cb", [2, DFF]), ("w_down", [2, DFF, D]),
                 ("ident", [128, 128]), ("exch", [128, 128]), ("rope", [17, 128, 4, 4, 64]),
                 ("retmask", [128, 128]), ("trigt", [128, 128]), ("gdec", [2, 128, 512]), ("t5oh", [32, 384]), ("mneg", [2, 128, 128])):
        I[n] = din(n, s)
    O = {}
    for n, s in (("y_p", [4096, D]), ("y_s", [64, D]), ("ret_p", [2, 4, 128, 128]), ("ret_s", [4, 128, 128]),
                 ("bk_p", [4096, 512]), ("bk_s", [64, 512]), ("bv_p", [4096, 512]), ("bv_s", [64, 512]),
                 ("ck_p", [2, 512, 512]), ("ck_s", [64, 512]), ("cv_p", [2, 512, 512]), ("cv_s", [64, 512]),
                 ("dconv_p", [2, 3, 1024]), ("dconv_s", [3, 1024]), ("dssm_p", [2, 8, 64, 128]),
                 ("dssm_s", [8, 64, 128]), ("ffn_p", [2, 2, 2, DFF]), ("ffn_s", [2, 2, DFF])):
        O[n] = dout(n, s)
    xs = dscr("xs", [4160, D])
    modd = dscr("modd", [3, 2, 6 * D])
    s5d = dscr("s5d", [4, 384])
    sreld = dscr("sreld", [8, 384])

    SEQS = [(2048, 0, False, 0), (2048, 0, False, 2048), (64, 16, True, 4096)]
    xk = {}

    def xtrk(r0):
        if r0 not in xk:
            xk[r0] = Trk(f"xs{r0}")
        return xk[r0]

    fw = FW(nc)
    with ExitStack() as es:
        fw.start(es)
        A = fw.op
        idf = fw.sb("idf", [128, 128], F32)
        idb = fw.sb("idb", [128, 128], BF16)
        modT = [fw.sb(f"modT{l}", [128, 48, 3], F32) for l in range(2)]
        GS = [[fw.sb(f"GS{l}{j}", [128, 8, 3], F32) for j in range(2)] for l in range(2)]
        fcw = fw.sb("fcw", [128, 22, 8], F32)
        lam = fw.sb("lam", [128, 4], F32)
        fw.load("sp", idf, idf[:], I["ident"])
        fw.load("pool", idb, idb[:], I["ident"])
        with ExitStack() as pst:
            fw.ts = pst
            ptb = Rot([fw.ps(f"ptb{i}", [128, 1024], BF16) for i in range(2)])
            pacc = Rot([fw.ps(f"pacc{i}", [128, 512], F32) for i in range(2)])
            psf = Rot([fw.ps(f"psf{i}", [128, 512], F32) for i in range(4)])

            def rows_to_fm(rows, R, n, out, c0=0):
                pp = psf.next()
                def f(h):
                    for c in range(n):
                        ins = h.matmul(pp[:, c * R:(c + 1) * R], rows[0:R, (c0 + c) * 128:(c0 + c + 1) * 128],
                                       idf[0:R, 0:R], start=True, stop=True)
                    return ins
                A("pe", [rows, idf], [pp], f)
                A("dve", [pp], [out], lambda h: h.tensor_copy(
                    out=out[:, 0:n, :], in_=pp[:, 0:n * R].rearrange("p (c r) -> p c r", r=R)))

            with ExitStack() as ph:
                fw.ts = ph
                cr = fw.sb("cr", [3, D], F32)
                ce = fw.sb("ce", [3, D], F32)
                cT = fw.sb("cT", [128, 8, 3], F32)
                mr = fw.sb("mr", [3, 2, 6 * D], F32)
                bmr = fw.sb("bmr", [1, 12 * D], F32)
                one = fw.sb("one", [1, 4], F32)
                ngr = fw.sb("ngr", [4, D], F32)
                ngT = fw.sb("ngT", [128, 8, 4], F32)
                fcr = fw.sb("fcr", [8, DFF], F32)
                wrot = Rot([fw.sb(f"wmb{i}", [128, 8, 512], F32) for i in range(2)])
                fw.load("sp", cr, cr[:], I["cc"])
                fw.load("sp", bmr, bmr[:], I["b_mod"])
                fw.load("sp", ngr, ngr[:], I["norm_g"])
                fw.load("sp", fcr, fcr[0:6, :], I["ffn_cw"])
                fw.load("sp", fcr, fcr[6:8, :], I["ffn_cb"])
                A("dve", [], [one], lambda h: h.memset(one[:], 1.0))
                A("act", [cr], [ce], lambda h: h.activation(out=ce[:], in_=cr[:], func=AF.Exp, scale=-1.0))
                A("dve", [ce], [ce], lambda h: h.tensor_scalar_add(out=ce[:], in0=ce[:], scalar1=1.0))
                A("dve", [ce], [ce], lambda h: h.reciprocal(out=ce[:], in_=ce[:]))
                A("dve", [ce, cr], [cr], lambda h: h.tensor_mul(out=cr[:], in0=cr[:], in1=ce[:]))
                rows_to_fm(cr, 3, 8, cT)
                rows_to_fm(ngr, 4, 8, ngT)
                rows_to_fm(fcr, 8, 22, fcw)
                for l in range(2):
                    for nb in range(12):
                        wb = wrot.next()
                        src = I["w_mod"][l, :, nb * 512:(nb + 1) * 512].rearrange("(kc p) n -> p kc n", p=128)
                        fw.load("sp", wb, wb[:, 0:4, :], src[:, 0:4, :])
                        fw.load("sp", wb, wb[:, 4:8, :], src[:, 4:8, :])
                        pp = psf.next()
                        def f(h, wb=wb, pp=pp, l=l, nb=nb):
                            for kc in range(8):
                                h.matmul(pp[0:3, :], cT[:, kc, :], wb[:, kc, :], start=(kc == 0), stop=False)
                            return h.matmul(pp[0:3, :], one[0:1, 0:3],
                                            bmr[0:1, l * 6 * D + nb * 512: l * 6 * D + (nb + 1) * 512],
                                            start=False, stop=True)
                        A("pe", [wb, cT, one, bmr], [pp], f)
                        A("act", [pp], [mr], lambda h, pp=pp, l=l, nb=nb: h.activation(
                            out=mr[0:3, l, nb * 512:(nb + 1) * 512], in_=pp[0:3, :], func=AF.Copy))
                mk = Trk("modd")
                fw.store("sp", mr, modd, mr[:], dk=mk)
                for l in range(2):
                    for half in range(2):
                        pp = psf.next()
                        def f(h, pp=pp, l=l, half=half):
                            for c in range(24):
                                cc_ = half * 24 + c
                                ins = h.matmul(pp[:, c * 3:(c + 1) * 3], mr[0:3, l, cc_ * 128:(cc_ + 1) * 128],
                                               idf[0:3, 0:3], start=True, stop=True)
                            return ins
                        A("pe", [mr, idf], [pp], f)
                        A("dve", [pp], [modT[l]], lambda h, pp=pp, l=l, half=half: h.tensor_copy(
                            out=modT[l][:, half * 24:(half + 1) * 24, :],
                            in_=pp[:, 0:72].rearrange("p (c r) -> p c r", r=3)))
                    for j in range(2):
                        sc0 = 8 + 24 * j
                        A("dve", [modT[l]], [GS[l][j]], lambda h, l=l, j=j, sc0=sc0: h.tensor_scalar_add(
                            out=GS[l][j][:], in0=modT[l][:, sc0:sc0 + 8, :], scalar1=1.0))
                        A("dve", [GS[l][j], ngT], [GS[l][j]], lambda h, l=l, j=j: h.tensor_tensor(
                            out=GS[l][j][:], in0=GS[l][j][:],
                            in1=ngT[:, :, l * 2 + j:l * 2 + j + 1].to_broadcast([128, 8, 3]), op=ALU.mult))
                lq = fw.sb("lq", [128, 128], F32)
                lk = fw.sb("lk", [128, 128], F32)
                fw.load("sp", lq, lq[:], I["lam_q"].partition_broadcast(128))
                fw.load("sp", lk, lk[:], I["lam_k"].partition_broadcast(128))
                A("dve", [lq, lk], [lq], lambda h: h.tensor_mul(out=lq[:], in0=lq[:], in1=lk[:]))
                A("dve", [lq], [lam], lambda h: h.tensor_reduce(
                    out=lam[:, 1:3], in_=lq[:].rearrange("p (a b) -> p a b", a=2), axis=AX.X, op=ALU.add))
                A("act", [lam], [lam], lambda h: h.activation(out=lam[:, 1:3], in_=lam[:, 1:3], func=AF.Exp))
                lam_init0 = 0.8 - 0.6 * math.exp(-0.3 * 0)
                A("dve", [lam], [lam], lambda h: h.tensor_sub(out=lam[:, 0:1], in0=lam[:, 2:3], in1=lam[:, 1:2]))
                A("dve", [lam], [lam], lambda h: h.tensor_scalar_add(out=lam[:, 0:1], in0=lam[:, 0:1],
                                                                     scalar1=-lam_init0))
                t5t = fw.sb("t5t", [32, 4], F32)
                t5o = fw.sb("t5o", [32, 384], F32)
                t5r = fw.sb("t5r", [4, 384], F32)
                fw.load("sp", t5t, t5t[:], I["t5_table"])
                fw.load("sp", t5o, t5o[:], I["t5oh"])
                pp = psf.next()
                A("pe", [t5t, t5o], [pp], lambda h, pp=pp: h.matmul(pp[0:4, 0:384], t5t[:], t5o[:],
                                                                      start=True, stop=True))
                A("act", [pp], [t5r], lambda h, pp=pp: h.activation(out=t5r[:], in_=pp[0:4, 0:384], func=AF.Copy))
                s5k = Trk("s5d")
                fw.store("sp", t5r, s5d, t5r[:], dk=s5k)
                rlt = fw.sb("rlt", [128, 3, 8], F32)
                rlr = fw.sb("rlr", [8, 384], F32)
                fw.load("sp", rlt, rlt[:, 0:2, :], I["rel_table"][0:256, :].rearrange("(c p) h -> p c h", p=128))
                fw.load("sp", rlt, rlt[0:1, 2, :], I["rel_table"][256:257, :])
                pp = psf.next()
                def f(h, pp=pp):
                    h.matmul(pp[0:8, 0:128], rlt[:, 0, :], idf[:, :], start=True, stop=True)
                    h.matmul(pp[0:8, 128:256], rlt[:, 1, :], idf[:, :], start=True, stop=True)
                    return h.matmul(pp[0:8, 256:257], rlt[0:1, 2, :], idf[0:1, 0:1], start=True, stop=True)
                A("pe", [rlt, idf], [pp], f)
                A("act", [pp], [rlr], lambda h, pp=pp: h.activation(out=rlr[:, 0:257], in_=pp[0:8, 0:257],
                                                                     func=AF.Copy))
                A("dve", [rlr], [rlr], lambda h: h.tensor_copy(out=rlr[:, 257:384],
                                                               in_=rlr[:, 256:257].to_broadcast([8, 127])))
                srk = Trk("sreld")
                fw.store("sp", rlr, sreld, rlr[:], dk=srk)
                fw.barrier()
            fw.release_phase()

            def bias_tiles(ph_name, dst, nh, rows_ap, rk, offs, first, cst, mn):
                hk = Rot([fw.sb(f"hk{ph_name}{i}", [128, 128], F32) for i in range(2)])
                ex = fw.sb(f"ex{ph_name}", [128, 128], F32)
                fw.load("sp", ex, ex[:], I["exch"])
                for h_ in range(nh):
                    for d_, off in enumerate(offs):
                        hkt = hk.next()
                        src = bass.AP(rows_ap.tensor, h_ * 384 + off, [[1, 128], [1, 128]])
                        fw.load("sp", hkt, hkt[:], src, dk=rk)
                        pp = psf.next()
                        if first:
                            A("pe", [hkt, ex], [pp], lambda h, pp=pp, hkt=hkt: h.matmul(
                                pp[:, 0:128], hkt[:], ex[:], start=True, stop=True))
                        else:
                            A("pe", [hkt, ex], [pp], lambda h, pp=pp, hkt=hkt: h.matmul(
                                pp[:, 0:128], ex[:], hkt[:], start=True, stop=True))
                        m_ = mn[d_]
                        if m_ is None:
                            A("dve", [pp, cst], [dst], lambda h, pp=pp, h_=h_, d_=d_: h.tensor_scalar(
                                out=dst[:, h_, d_, :], in0=pp[:, 0:128], scalar1=cst[:, h_:h_ + 1], scalar2=None,
                                op0=ALU.subtract))
                        else:
                            A("dve", [pp, cst, m_], [dst], lambda h, pp=pp, h_=h_, d_=d_, m_=m_: h.scalar_tensor_tensor(
                                out=dst[:, h_, d_, :], in0=pp[:, 0:128], scalar=cst[:, h_:h_ + 1], in1=m_[:],
                                op0=ALU.subtract, op1=ALU.add))

            def rmsnorm_to_hT(xt, nt, hT, gs, sh, s, scr, st4):
                A("act", [xt], [scr, st4], lambda h: h.activation(out=scr[:nt, :], in_=xt[:nt, :], func=AF.Square,
                                                                 accum_out=st4[:nt, 0:1]))
                A("act", [st4], [st4], lambda h: h.activation(out=st4[:nt, 1:2], in_=st4[:nt, 0:1], func=AF.Sqrt,
                                                              scale=1.0 / D, bias=EPS))
                A("dve", [st4], [st4], lambda h: h.reciprocal(out=st4[:nt, 2:3], in_=st4[:nt, 1:2]))
                A("dve", [xt, st4], [scr], lambda h: h.tensor_scalar(out=scr[:nt, :], in0=xt[:nt, :],
                                                                    scalar1=st4[:nt, 2:3], scalar2=None, op0=ALU.mult))
                pt = ptb.next()
                def f(h):
                    for c in range(8):
                        ins = h.transpose(out=pt[:, c * 128:c * 128 + nt], in_=scr[:nt, c * 128:(c + 1) * 128],
                                          identity=idb[:nt, :nt])
                    return ins
                A("pe", [scr, idb], [pt], f)
                for c in range(8):
                    A("act", [pt, gs, sh], [hT], lambda h, c=c: h.activation(
                        out=hT[:, c, :nt], in_=pt[:, c * 128:c * 128 + nt], func=AF.Identity,
                        scale=gs[:, c, s:s + 1], bias=sh[:, c, s:s + 1]))

            def transpose_blocks(src, nt, nblk, dst_fn, rd_extra=()):
                pt = ptb.next()
                def f(h):
                    for c in range(nblk):
                        ins = h.transpose(out=pt[:, c * 128:c * 128 + nt], in_=src[:nt, c * 128:(c + 1) * 128],
                                          identity=idb[:nt, :nt])
                    return ins
                A("pe", [src, idb], [pt], f)
                return pt

            with ExitStack() as ph:
              if 'A' in PH:
                fw.ts = ph
                wIn = fw.sb("wIn", [128, 8, 2560], BF16)
                w32 = fw.sb("w32", [128, 8, 1024], F32)
                wOut = fw.sb("wOut", [128, 8, D], BF16)
                for kc in range(8):
                    fw.load("sp", w32, w32[:, kc, :], I["w_in_ab"][kc * 128:(kc + 1) * 128, 0:1024])
                for kc in range(8):
                    fw.load("pool", wIn, wIn[:, kc, :], I["w_in_ab"][kc * 128:(kc + 1) * 128, 1024:3584])
                for kc in range(8):
                    fw.load("pool", wOut, wOut[:, kc, :], I["w_out_ab"][kc * 128:(kc + 1) * 128, :])
                kbT = fw.sb("kbT", [128, 4, 2176], BF16)
                Vc = fw.sb("Vc", [128, 17, 4, 130], BF16)
                bT5 = fw.sb("bT5", [128, 4, 2, 128], BF16)
                c15 = fw.sb("c15", [128, 4], F32)
                mn0 = fw.sb("mn0", [128, 128], F32)
                rmask = fw.sb("rmask", [128, 128], F32)
                gdec = fw.sb("gdecs", [128, 2, 512], F32)
                rgn = fw.sb("rgn", [128, 512], F32)
                dgn = fw.sb("dgn", [128, 128], F32)
                G1 = fw.sb("G1", [128, D], F32)
                fw.load("sp", c15, c15[:], I["t5_table"][15:16, :].partition_broadcast(128))
                fw.load("sp", mn0, mn0[:], I["mneg"][0])
                fw.load("sp", rmask, rmask[:], I["retmask"])
                fw.load("sp", gdec, gdec[:], I["gdec"].rearrange("a p n -> p a n"))
                fw.load("sp", rgn, rgn[:], I["ret_gn"].partition_broadcast(128))
                fw.load("sp", dgn, dgn[:], I["diff_gn"].partition_broadcast(128))
                A("dve", [dgn], [dgn], lambda h: h.tensor_scalar(out=dgn[:], in0=dgn[:], scalar1=1.0 - lam_init0,
                                                                scalar2=None, op0=ALU.mult))
                A("dve", [], [Vc], lambda h: h.memset(Vc[:, :, :, 128:130], 1.0))
                bias_tiles("t5", bT5, 4, s5d, s5k, [128, 0], True, c15, [mn0, None])

                xrot = Rot([fw.sb(f"xa{i}", [128, D], F32) for i in range(2)])
                rrot = Rot([fw.sb(f"rp{i}", [128, 4, 4, 64], F32) for i in range(1)])
                st4 = fw.sb("st4", [128, 4], F32)
                hT = fw.sb("hT", [128, 8, 128], BF16)
                hT32 = fw.sb("hT32", [128, 8, 128], F32)
                qaT32 = CView(hT32, 0)
                kaT32 = CView(hT32, 4)
                ev = fw.sb("ev", [128, 512], F32)
                rt = [fw.sb(f"rt{i}", [128, 4, 64], F32) for i in range(4)]
                qrot = fw.sb("qrot", [128, 512], BF16)
                krot = fw.sb("krot", [128, 512], BF16)
                va = fw.sb("va", [128, 512], BF16)
                sg = fw.sb("sg", [128, 512], F32)
                sge = fw.sb("sge", [128, 512], F32)
                qb = fw.sb("qb", [128, 512], BF16)
                kbf = Rot([fw.sb(f"kbf{i}", [128, 512], F32) for i in range(1)])
                vbf = Rot([fw.sb(f"vbf{i}", [128, 512], F32) for i in range(1)])
                kb16 = fw.sb("kb16", [128, 512], BF16)
                qaT = fw.sb("qaT", [128, 4, 128], BF16)
                qbT = fw.sb("qbT", [128, 4, 128], BF16)
                PT = fw.sb("PT", [128, 4, 128], BF16)
                stf = fw.sb("stf", [128, 512], F32)
                stb = fw.sb("stb", [128, 512], BF16)
                gst = fw.sb("gst", [128, 16], F32)
                gtmp = fw.sb("gtmp", [128, 512], F32)
                gsq = fw.sb("gsq", [128, 512], F32)
                mix = fw.sb("mix", [128, D], BF16)
                pTt = Rot([fw.sb(f"pTt{i}", [128, 4, 128], BF16) for i in range(2)])
                ob = fw.sb("ob", [128, 4, 128], F32)
                rc = fw.sb("rc", [128, 4], F32)
                t1 = fw.sb("t1", [128, 128], F32)
                mT = fw.sb("mT", [128, 8, 128], BF16)
                xo = Rot([fw.sb(f"xo{i}", [128, D], F32) for i in range(1)])
                ot = fw.sb("ot", [128, D], F32)
                xn32 = ot
                q32 = gtmp
                k32 = gsq
                scr = mix

                for s, (ntok, rp0, samp, row0) in enumerate(SEQS):
                    if str(s) not in SEQSEL:
                        continue
                    ntl = min(MAXT, (ntok + 127) // 128)
                    fw.load("sp", G1, G1[:], modd[s:s + 1, 0, 16 * 128:24 * 128].partition_broadcast(128), dk=mk)
                    if samp:
                        fw.load("sp", stf, stf[:].rearrange("p (h e) -> p h e", h=4),
                                I["ret0"].rearrange("h d e -> d h e"))
                        A("dve", [], [kbT], lambda h: h.memset(kbT[:, :, 1088:1152], 0.0))
                        A("dve", [], [Vc], lambda h: h.memset(Vc[64:128, 8, :, 0:128], 0.0))
                        for j in range(8):
                            kf = kbf.next()
                            fw.load("sp", kf, kf[:], I["bk_c"][j * 128:(j + 1) * 128, :])
                            A("dve", [kf], [kb16], lambda h, kf=kf: h.tensor_copy(out=kb16[:], in_=kf[:]))
                            pt = transpose_blocks(kb16, 128, 4, None)
                            A("act", [pt], [kbT], lambda h, pt=pt, j=j: h.activation(
                                out=kbT[:, :, j * 128:(j + 1) * 128],
                                in_=pt[:, 0:512].rearrange("p (c t) -> p c t", c=4), func=AF.Copy))
                            vf = vbf.next()
                            fw.load("sp", vf, vf[:], I["bv_c"][j * 128:(j + 1) * 128, :])
                            A("dve", [vf], [Vc], lambda h, vf=vf, j=j: h.tensor_copy(
                                out=Vc[:, j, :, 0:128], in_=vf[:].rearrange("p (h e) -> p h e", h=4)))
                    else:
                        A("dve", [], [stf], lambda h: h.memset(stf[:], 0.0))
                    A("act", [stf], [stb], lambda h: h.activation(out=stb[:], in_=stf[:], func=AF.Copy))
                    for ti in range(ntl):
                        nt = min(128, ntok - ti * 128)
                        r0 = row0 + ti * 128
                        qi = 8 if samp else ti
                        gdi = 1 if samp else 0
                        src = I["xsm"] if samp else I["xp"]
                        sr0 = 0 if samp else r0

                        def ldxa(tj):
                            ntj = min(128, ntok - tj * 128)
                            sj = 0 if samp else row0 + tj * 128
                            xj = xrot.next()
                            fw.load("sp", xj, xj[:ntj, :], src[sj:sj + ntj, :])
                            return xj
                        if ti == 0:
                            xnext = ldxa(0)
                        xt = xnext
                        if ti + 1 < ntl:
                            xnext = ldxa(ti + 1)
                        rp = rrot.next()
                        fw.load("sp", rp, rp[:], I["rope"][rp0 + ti])
                        A("act", [xt], [scr, st4], lambda h: h.activation(out=scr[:nt, :], in_=xt[:nt, :], func=AF.Square,
                                                                         accum_out=st4[:nt, 0:1]))
                        A("act", [st4], [st4], lambda h: h.activation(out=st4[:nt, 1:2], in_=st4[:nt, 0:1], func=AF.Sqrt,
                                                                      scale=1.0 / D, bias=EPS))
                        A("dve", [st4], [st4], lambda h: h.reciprocal(out=st4[:nt, 2:3], in_=st4[:nt, 1:2]))
                        A("dve", [xt, st4], [xn32], lambda h: h.tensor_scalar(out=xn32[:nt, :], in0=xt[:nt, :],
                                                                             scalar1=st4[:nt, 2:3], scalar2=None, op0=ALU.mult))
                        for half in range(2):
                            pp = psf.next()
                            def f(h, pp=pp, half=half):
                                for cc in range(4):
                                    c = half * 4 + cc
                                    ins = h.matmul(pp[:, cc * 128:cc * 128 + nt], xn32[:nt, c * 128:(c + 1) * 128],
                                                   idf[:nt, :nt], start=True, stop=True)
                                return ins
                            A("pe", [xn32, idf], [pp], f)
                            for cc in range(4):
                                c = half * 4 + cc
                                A("act", [pp, GS[0][0], modT[0]], [hT32], lambda h, pp=pp, c=c, cc=cc: h.activation(
                                    out=hT32[:, c, :nt], in_=pp[:, cc * 128:cc * 128 + nt], func=AF.Identity,
                                    scale=GS[0][0][:, c, s:s + 1], bias=modT[0][:, c, s:s + 1]))
                        A("dve", [hT32], [hT], lambda h: h.tensor_copy(out=hT[:, :, :nt], in_=hT32[:, :, :nt]))
                        def proj(blk):
                            pp = psf.next()
                            def f(h):
                                for c in range(8):
                                    if blk < 2:
                                        ins = h.matmul(pp[:nt, :], hT32[:, c, :nt], w32[:, c, blk * 512:(blk + 1) * 512],
                                                       start=(c == 0), stop=(c == 7))
                                    else:
                                        ins = h.matmul(pp[:nt, :], hT[:, c, :nt], wIn[:, c, (blk - 2) * 512:(blk - 1) * 512],
                                                       start=(c == 0), stop=(c == 7))
                                return ins
                            A("pe", [hT, hT32, wIn, w32], [pp], f)
                            return pp

                        def rotary(pp, ci, out, eng):
                            A("act", [pp], [ev], lambda h: h.activation(out=ev[:nt, :], in_=pp[:nt, :], func=AF.Copy))
                            x1 = ev[:nt, :].rearrange("p (h e) -> p h e", h=4)[:, :, 0:64]
                            x2 = ev[:nt, :].rearrange("p (h e) -> p h e", h=4)[:, :, 64:128]
                            cs_ = rp[:nt, ci]
                            sn_ = rp[:nt, ci + 1]
                            out32 = q32 if ci == 0 else k32
                            ov = out32[:nt, :].rearrange("p (h e) -> p h e", h=4)
                            A(eng, [ev, rp], [rt[0]], lambda h: h.tensor_tensor(out=rt[0][:nt], in0=x1, in1=cs_, op=ALU.mult))
                            A(eng, [ev, rp], [rt[1]], lambda h: h.tensor_tensor(out=rt[1][:nt], in0=x2, in1=sn_, op=ALU.mult))
                            A(eng, [ev, rp], [rt[2]], lambda h: h.tensor_tensor(out=rt[2][:nt], in0=x1, in1=sn_, op=ALU.mult))
                            A(eng, [ev, rp], [rt[3]], lambda h: h.tensor_tensor(out=rt[3][:nt], in0=x2, in1=cs_, op=ALU.mult))
                            A(eng, [rt[0], rt[1]], [out32], lambda h: h.tensor_tensor(
                                out=ov[:, :, 0:64], in0=rt[0][:nt], in1=rt[1][:nt], op=ALU.subtract))
                            A(eng, [rt[2], rt[3]], [out32], lambda h: h.tensor_tensor(
                                out=ov[:, :, 64:128], in0=rt[2][:nt], in1=rt[3][:nt], op=ALU.add))
                            A("dve", [out32], [out], lambda h: h.tensor_copy(out=out[:nt, :], in_=out32[:nt, :]))

                        pp = proj(0)
                        rotary(pp, 0, qrot, "dve")
                        pp = proj(1)
                        rotary(pp, 2, krot, "dve")
                        pp = proj(2)
                        A("act", [pp], [va], lambda h, pp=pp: h.activation(out=va[:nt, :], in_=pp[:nt, :], func=AF.Copy))
                        pp = proj(3)
                        A("act", [pp], [sge], lambda h, pp=pp: h.activation(out=sge[:nt, :], in_=pp[:nt, :], func=AF.Sigmoid))
                        A("dve", [pp, sge], [sg], lambda h, pp=pp: h.tensor_tensor(out=sg[:nt, :], in0=pp[:nt, :],
                                                                                 in1=sge[:nt, :], op=ALU.mult))
                        pp = proj(4)
                        A("act", [pp], [qb], lambda h, pp=pp: h.activation(out=qb[:nt, :], in_=pp[:nt, :], func=AF.Copy,
                                                                         scale=0.125))
                        pp = proj(5)
                        kf = kbf.next()
                        A("act", [pp], [kf], lambda h, pp=pp: h.activation(out=kf[:nt, :], in_=pp[:nt, :], func=AF.Copy))
                        A("dve", [pp], [kb16], lambda h, pp=pp: h.tensor_copy(out=kb16[:nt, :], in_=pp[:nt, :]))
                        fw.store("sp", kf, (O["bk_s"] if samp else O["bk_p"])[sr0:sr0 + nt, :], kf[:nt, :])
                        pp = proj(6)
                        vf = vbf.next()
                        A("act", [pp], [vf], lambda h, pp=pp: h.activation(out=vf[:nt, :], in_=pp[:nt, :], func=AF.Copy))
                        A("dve", [pp], [Vc], lambda h, pp=pp: h.tensor_copy(
                            out=Vc[:nt, qi, :, 0:128], in_=pp[:nt, :].rearrange("p (h e) -> p h e", h=4)))
                        fw.store("sp", vf, (O["bv_s"] if samp else O["bv_p"])[sr0:sr0 + nt, :], vf[:nt, :])
                        for src32, dst32 in ((q32, qaT32), (k32, kaT32)):
                            pp = psf.next()
                            def f(h, pp=pp, src32=src32):
                                for hh in range(4):
                                    ins = h.matmul(pp[:, hh * 128:hh * 128 + nt], src32[:nt, hh * 128:(hh + 1) * 128],
                                                   idf[:nt, :nt], start=True, stop=True)
                                return ins
                            A("pe", [src32, idf], [pp], f)
                            A("act", [pp], [dst32], lambda h, pp=pp, dst32=dst32: h.activation(
                                out=dst32[:, :, :nt], in_=pp[:, :].rearrange("p (c t) -> p c t", c=4)[:, :, :nt],
                                func=AF.Copy))
                        for srcT, dstT in ((qrot, qaT), (qb, qbT)):
                            pt = transpose_blocks(srcT, nt, 4, None)
                            A("act", [pt], [dstT], lambda h, pt=pt, dstT=dstT: h.activation(
                                out=dstT[:, :, :nt], in_=pt[:, 0:512].rearrange("p (c t) -> p c t", c=4)[:, :, :nt],
                                func=AF.Copy))
                        pt = transpose_blocks(kb16, nt, 4, None)
                        A("act", [pt], [kbT], lambda h, pt=pt: h.activation(
                            out=kbT[:, :, qi * 128:qi * 128 + nt],
                            in_=pt[:, 0:512].rearrange("p (c t) -> p c t", c=4)[:, :, :nt], func=AF.Copy))
                        pS = psf.next()
                        def f(h):
                            for hh in range(4):
                                ins = h.matmul(pS[:nt, hh * 128:hh * 128 + nt], kaT32[:, hh, :nt], qaT32[:, hh, :nt],
                                               start=True, stop=True)
                            return ins
                        A("pe", [kaT32, qaT32], [pS], f)
                        A("dve", [pS, rmask], [PT], lambda h: h.tensor_tensor(
                            out=PT[:nt, :, :nt], in0=pS[:nt, :].rearrange("p (h t) -> p h t", h=4)[:, :, :nt],
                            in1=rmask[:nt, :nt].unsqueeze(1).to_broadcast([nt, 4, nt]), op=ALU.mult))
                        pD = psf.next()
                        def f(h):
                            for hh in range(4):
                                ins = h.matmul(pD[:, hh * 128:(hh + 1) * 128], krot[:nt, hh * 128:(hh + 1) * 128],
                                               va[:nt, hh * 128:(hh + 1) * 128], start=True, stop=True)
                            return ins
                        A("pe", [krot, va], [pD], f)
                        pO = psf.next()
                        def f(h):
                            for hh in range(4):
                                h.matmul(pO[:nt, hh * 128:(hh + 1) * 128], PT[:nt, hh, :nt], va[:nt, hh * 128:(hh + 1) * 128],
                                         start=True, stop=False)
                                ins = h.matmul(pO[:nt, hh * 128:(hh + 1) * 128], qaT[:, hh, :nt],
                                               stb[:, hh * 128:(hh + 1) * 128], start=False, stop=True)
                            return ins
                        A("pe", [PT, va, qaT, stb], [pO], f)
                        A("dve", [pD, stf], [stf], lambda h: h.tensor_tensor(out=stf[:], in0=pD[:], in1=stf[:], op=ALU.add))
                        A("dve", [stf, gdec], [stf], lambda h: h.tensor_tensor(out=stf[:], in0=stf[:], in1=gdec[:, gdi, :],
                                                                               op=ALU.mult))
                        A("act", [stf], [stb], lambda h: h.activation(out=stb[:], in_=stf[:], func=AF.Copy))
                        pOv = pO[:nt, :].rearrange("p (h e) -> p h e", h=4)
                        A("dve", [pO], [gst], lambda h: h.tensor_reduce(out=gst[:nt, 0:4], in_=pOv, axis=AX.X, op=ALU.add))
                        A("act", [pO], [gsq], lambda h: h.activation(out=gsq[:nt, :], in_=pO[:nt, :], func=AF.Square))
                        A("dve", [gsq], [gst], lambda h: h.tensor_reduce(
                            out=gst[:nt, 4:8], in_=gsq[:nt, :].rearrange("p (h e) -> p h e", h=4), axis=AX.X, op=ALU.add))
                        A("dve", [gst], [gst], lambda h: h.tensor_scalar(out=gst[:nt, 0:4], in0=gst[:nt, 0:4],
                                                                        scalar1=1.0 / 128, scalar2=None, op0=ALU.mult))
                        A("dve", [gst], [gst], lambda h: h.tensor_tensor(out=gst[:nt, 8:12], in0=gst[:nt, 0:4],
                                                                        in1=gst[:nt, 0:4], op=ALU.mult))
                        A("dve", [gst], [gst], lambda h: h.scalar_tensor_tensor(
                            out=gst[:nt, 4:8], in0=gst[:nt, 4:8], scalar=1.0 / 128, in1=gst[:nt, 8:12],
                            op0=ALU.mult, op1=ALU.subtract))
                        A("act", [gst], [gst], lambda h: h.activation(out=gst[:nt, 4:8], in_=gst[:nt, 4:8], func=AF.Sqrt,
                                                                      bias=EPS))
                        A("dve", [gst], [gst], lambda h: h.reciprocal(out=gst[:nt, 4:8], in_=gst[:nt, 4:8]))
                        gv = gtmp[:nt, :].rearrange("p (h e) -> p h e", h=4)
                        A("dve", [pO, gst], [gtmp], lambda h: h.tensor_tensor(
                            out=gv, in0=pOv, in1=gst[:nt, 0:4].unsqueeze(2).to_broadcast([nt, 4, 128]), op=ALU.subtract))
                        A("dve", [gtmp, gst], [gtmp], lambda h: h.tensor_tensor(
                            out=gv, in0=gv, in1=gst[:nt, 4:8].unsqueeze(2).to_broadcast([nt, 4, 128]), op=ALU.mult))
                        A("dve", [gtmp, rgn], [gtmp], lambda h: h.tensor_tensor(out=gtmp[:nt, :], in0=gtmp[:nt, :],
                                                                                in1=rgn[:nt, :], op=ALU.mult))
                        A("dve", [gtmp, sg], [mix], lambda h: h.tensor_tensor(out=mix[:nt, 0:512], in0=gtmp[:nt, :],
                                                                             in1=sg[:nt, :], op=ALU.mult))
                        groups = []
                        j = 0
                        while j <= qi:
                            if samp and j == 8:
                                groups.append([8])
                                j += 1
                            else:
                                hi = min(j + 4, (8 if samp else qi + 1))
                                groups.append(list(range(j, hi)))
                                j = hi
                        pas = {}
                        def qk_a(hh, i_, g):
                            nk = 128
                            pS = psf.next()
                            def f(h):
                                for jj, j_ in enumerate(g):
                                    dl = j_ - qi
                                    near = dl >= -1
                                    ins = h.matmul(pS[:nk, jj * 128:jj * 128 + nt],
                                                   kbT[i_ * 64:(i_ + 1) * 64, hh, j_ * 128:j_ * 128 + nk],
                                                   qbT[i_ * 64:(i_ + 1) * 64, hh, :nt], start=True, stop=not near)
                                    if near:
                                        ins = h.matmul(pS[:nk, jj * 128:jj * 128 + nt], idb[:nk, :nk],
                                                       bT5[:nk, hh, -dl, :nt], start=False, stop=True)
                                return ins
                            A("pe", [kbT, qbT, idb, bT5], [pS], f)
                            pt_ = pTt.next()
                            ng = len(g)
                            A("act", [pS, c15], [pt_], lambda h: h.activation(
                                out=pt_[:nk, 0:ng, :nt],
                                in_=pS[:nk, 0:ng * 128].rearrange("p (g t) -> p g t", g=ng)[:, :, :nt],
                                func=AF.Exp, bias=c15[:nk, hh:hh + 1]))
                            return pt_

                        def pv_a(hh, i_, g, pt_):
                            nk = 128
                            if hh not in pas:
                                pas[hh] = pacc.next()
                            pa = pas[hh]
                            def f(h):
                                for jj, j_ in enumerate(g):
                                    ins = h.matmul(pa[:nt, i_ * 256:i_ * 256 + 129], pt_[:nk, jj, :nt],
                                                   Vc[:nk, j_, hh, 0:129], start=(j_ == 0), stop=(j_ == qi))
                                return ins
                            A("pe", [pt_, Vc], [pa], f)
                            if i_ == 1 and g is groups[-1]:
                                pav = pa[:nt, 0:512].rearrange("p (i e) -> p i e", i=2)
                                A("dve", [pa], [rc], lambda h: h.reciprocal(out=rc[:nt, 0:2], in_=pav[:, :, 128]))
                                A("dve", [rc, lam], [rc], lambda h: h.tensor_tensor(out=rc[:nt, 2:3], in0=rc[:nt, 1:2],
                                                                                   in1=lam[:nt, 0:1], op=ALU.mult))
                                A("dve", [pa, rc], [t1], lambda h: h.tensor_scalar(
                                    out=t1[:nt, :], in0=pa[:nt, 256:384], scalar1=rc[:nt, 2:3], scalar2=None, op0=ALU.mult))
                                A("dve", [pa, rc, t1], [ob], lambda h: h.scalar_tensor_tensor(
                                    out=ob[:nt, hh, :], in0=pa[:nt, 0:128], scalar=rc[:nt, 0:1], in1=t1[:nt, :],
                                    op0=ALU.mult, op1=ALU.add))

                        items = [(hh, i_, g) for hh in range(4) for i_ in range(2) for g in groups]
                        prev = None
                        for it in items:
                            cur = qk_a(*it)
                            if prev is not None:
                                pv_a(*prev)
                            prev = it + (cur,)
                        pv_a(*prev)
                        obf = ob[:nt].rearrange("p h e -> p (h e)")
                        A("act", [ob], [gsq], lambda h: h.activation(out=gsq[:nt, :], in_=obf, func=AF.Square))
                        A("dve", [gsq], [gst], lambda h: h.tensor_reduce(
                            out=gst[:nt, 12:16], in_=gsq[:nt, :].rearrange("p (h e) -> p h e", h=4), axis=AX.X, op=ALU.add))
                        A("act", [gst], [gst], lambda h: h.activation(out=gst[:nt, 12:16], in_=gst[:nt, 12:16], func=AF.Sqrt,
                                                                      scale=1.0 / 128, bias=EPS))
                        A("dve", [gst], [gst], lambda h: h.reciprocal(out=gst[:nt, 12:16], in_=gst[:nt, 12:16]))
                        A("dve", [ob, gst], [gtmp], lambda h: h.tensor_tensor(
                            out=gv, in0=ob[:nt], in1=gst[:nt, 12:16].unsqueeze(2).to_broadcast([nt, 4, 128]), op=ALU.mult))
                        A("dve", [gtmp, dgn], [mix], lambda h: h.tensor_tensor(
                            out=mix[:nt, 512:1024].rearrange("p (h e) -> p h e", h=4), in0=gv,
                            in1=dgn[:nt, :].unsqueeze(1).to_broadcast([nt, 4, 128]), op=ALU.mult))
                        pt = transpose_blocks(mix, nt, 8, None)
                        A("act", [pt], [mT], lambda h, pt=pt: h.activation(
                            out=mT[:, :, :nt], in_=pt[:, :].rearrange("p (c t) -> p c t", c=8)[:, :, :nt], func=AF.Copy))
                        xn_ = xo.next()
                        for blk in range(2):
                            pp = psf.next()
                            def f(h, pp=pp, blk=blk):
                                for c in range(8):
                                    ins = h.matmul(pp[:nt, :], mT[:, c, :nt], wOut[:, c, blk * 512:(blk + 1) * 512],
                                                   start=(c == 0), stop=(c == 7))
                                return ins
                            A("pe", [mT, wOut], [pp], f)
                            A("dve", [pp, G1], [ot], lambda h, pp=pp, blk=blk: h.tensor_tensor(
                                out=ot[:nt, blk * 512:(blk + 1) * 512], in0=pp[:nt, :],
                                in1=G1[:nt, blk * 512:(blk + 1) * 512], op=ALU.mult))
                        A("dve", [ot, xt], [xn_], lambda h, xn_=xn_, xt=xt: h.tensor_tensor(
                            out=xn_[:nt, :], in0=ot[:nt, :], in1=xt[:nt, :], op=ALU.add))
                        fw.store("sp", xn_, xs[r0:r0 + nt, :], xn_[:nt, :], dk=xtrk(r0))
                    dst = O["ret_s"] if samp else O["ret_p"][s]
                    fw.store("sp", stf, dst.rearrange("h d e -> d h e"), stf[:].rearrange("p (h e) -> p h e", h=4))
                fw.barrier()
            fw.release_phase()

            def ffn_phase(l):
                with ExitStack() as ph:
                    fw.ts = ph
                    wUp = fw.sb("wUp", [128, 8, 2 * DFF], BF16)
                    wDn = fw.sb("wDn", [128, 22, D], BF16)
                    for kc in range(8):
                        fw.load("pool", wUp, wUp[:, kc, :], I["w_up"][l, kc * 128:(kc + 1) * 128, :])
                    for kc in range(22):
                        fw.load("pool", wDn, wDn[:, kc, :], I["w_down"][l, kc * 128:(kc + 1) * 128, :])
                    G2 = fw.sb("G2", [128, D], F32)
                    xrot = Rot([fw.sb(f"xf{i}", [128, D], F32) for i in range(3)])
                    scr = fw.sb("scrf", [128, D], BF16)
                    st4 = fw.sb("st4f", [128, 4], F32)
                    hT = fw.sb("hTf", [128, 8, 128], BF16)
                    gbs = [fw.sb(f"gb{i}", [128, 22, 130], F32) for i in range(2)]
                    cv = fw.sb("cv", [128, 6, 128], F32)
                    tA = fw.sb("tA", [128, 6, 128], F32)
                    tB = fw.sb("tB", [128, 6, 128], F32)
                    abs_ = [fw.sb(f"ab{i}", [128, 22, 128], BF16) for i in range(2)]
                    act = fw.sb("actT", [128, 22, 128], BF16)
                    xo = Rot([fw.sb(f"xof{i}", [128, D], F32) for i in range(1)])
                    fo = fw.sb("fo", [128, 22, 2], F32)
                    K0 = math.sqrt(2.0 / math.pi)
                    QB = [(0, 6), (6, 12), (12, 18), (18, 22)]
                    for s, (ntok, rp0, samp, row0) in enumerate(SEQS):
                        if str(s) not in SEQSEL:
                            continue
                        ntl = min(MAXT, (ntok + 127) // 128)
                        fw.load("sp", G2, G2[:], modd[s:s + 1, l, 40 * 128:48 * 128].partition_broadcast(128), dk=mk)
                        gb0 = gbs[0]
                        if samp:
                            for c in range(22):
                                fw.load("sp", gb0, gb0[:, c, 0:2],
                                        I["ffnc"][l, :, c * 128:(c + 1) * 128].rearrange("t p -> p t"),
                                        allow_slow_non_contiguous=True)
                        else:
                            A("dve", [], [gb0], lambda h: h.memset(gb0[:, :, 0:2], 0.0))
                        xts = {}

                        def ntk(ti):
                            return min(128, ntok - ti * 128)

                        def ldx(ti):
                            xt = xrot.next()
                            xts[ti] = xt
                            r0 = row0 + ti * 128
                            fw.load("sp", xt, xt[:ntk(ti), :], xs[r0:r0 + ntk(ti), :], dk=xtrk(r0))

                        def up_pairs(ti, cps):
                            nt = ntk(ti)
                            gb, ab = gbs[ti % 2], abs_[ti % 2]
                            for cp in cps:
                                pg = psf.next()
                                def f(h, pg=pg, cp=cp):
                                    for q_ in range(4):
                                        c = 2 * cp + (q_ % 2)
                                        col = (DFF if q_ >= 2 else 0) + c * 128
                                        for kc in range(8):
                                            ins = h.matmul(pg[:, q_ * 128:q_ * 128 + nt], wUp[:, kc, col:col + 128],
                                                           hT[:, kc, :nt], start=(kc == 0), stop=(kc == 7))
                                    return ins
                                A("pe", [wUp, hT], [pg], f)
                                A("act", [pg], [gb], lambda h, pg=pg, cp=cp: h.activation(
                                    out=gb[:, 2 * cp:2 * cp + 2, 2:2 + nt],
                                    in_=pg[:, 256:512].rearrange("p (c t) -> p c t", c=2)[:, :, :nt], func=AF.Copy))
                                A("dve", [pg], [ab], lambda h, pg=pg, cp=cp: h.tensor_copy(
                                    out=ab[:, 2 * cp:2 * cp + 2, :nt],
                                    in_=pg[:, 0:256].rearrange("p (c t) -> p c t", c=2)[:, :, :nt]))

                        def carry(ti):
                            nt = ntk(ti)
                            gb, gbn = gbs[ti % 2], gbs[(ti + 1) % 2]
                            A("pool", [gb], [fo], lambda h: h.tensor_copy(out=fo[:, :, :], in_=gb[:, :, nt:nt + 2]))
                            A("pool", [fo], [gbn], lambda h: h.tensor_copy(out=gbn[:, :, 0:2], in_=fo[:, :, :]))

                        def ew(ti, q):
                            nt = ntk(ti)
                            gb, ab = gbs[ti % 2], abs_[ti % 2]
                            c0, c1 = QB[q]
                            n = c1 - c0
                            def wb(col):
                                return fcw[:, c0:c1, col:col + 1].to_broadcast([128, n, nt])
                            cvv, tAv, tBv = cv[:, 0:n, :nt], tA[:, 0:n, :nt], tB[:, 0:n, :nt]
                            A("dve", [gb, fcw], [cv], lambda h: h.tensor_tensor(out=cvv, in0=gb[:, c0:c1, 2:2 + nt],
                                                                              in1=wb(l * 3 + 2), op=ALU.mult))
                            A("dve", [gb, fcw], [tA], lambda h: h.tensor_tensor(out=tAv, in0=gb[:, c0:c1, 1:1 + nt],
                                                                              in1=wb(l * 3 + 1), op=ALU.mult))
                            A("dve", [cv, tA], [cv], lambda h: h.tensor_tensor(out=cvv, in0=cvv, in1=tAv, op=ALU.add))
                            A("dve", [gb, fcw], [tA], lambda h: h.tensor_tensor(out=tAv, in0=gb[:, c0:c1, 0:nt],
                                                                              in1=wb(l * 3 + 0), op=ALU.mult))
                            A("dve", [cv, tA], [cv], lambda h: h.tensor_tensor(out=cvv, in0=cvv, in1=tAv, op=ALU.add))
                            A("dve", [cv, fcw], [cv], lambda h: h.tensor_tensor(out=cvv, in0=cvv, in1=wb(6 + l), op=ALU.add))
                            A("act", [cv], [tA], lambda h: h.activation(out=tAv, in_=cvv, func=AF.Square))
                            A("act", [tA], [tA], lambda h: h.activation(out=tAv, in_=tAv, func=AF.Identity, scale=0.044715,
                                                                        bias=1.0))
                            A("dve", [tA, cv], [tA], lambda h: h.tensor_tensor(out=tAv, in0=tAv, in1=cvv, op=ALU.mult))
                            A("act", [tA], [tB], lambda h: h.activation(out=tBv, in_=tAv, func=AF.Sigmoid, scale=2.0 * K0))
                            A("dve", [tB, cv], [tB], lambda h: h.tensor_tensor(out=tBv, in0=tBv, in1=cvv, op=ALU.mult))
                            A("dve", [ab, tB], [act], lambda h: h.tensor_tensor(out=act[:, c0:c1, :nt], in0=ab[:, c0:c1, :nt],
                                                                               in1=tBv, op=ALU.mult))

                        def down(ti):
                            nt = ntk(ti)
                            r0 = row0 + ti * 128
                            xt = xts.pop(ti)
                            xn_ = xo.next()
                            for blk in range(2):
                                pp = psf.next()
                                def f(h, pp=pp, blk=blk):
                                    for c in range(22):
                                        ins = h.matmul(pp[:nt, :], act[:, c, :nt], wDn[:, c, blk * 512:(blk + 1) * 512],
                                                       start=(c == 0), stop=(c == 21))
                                    return ins
                                A("pe", [act, wDn], [pp], f)
                                A("dve", [pp, G2], [xn_], lambda h, pp=pp, blk=blk: h.tensor_tensor(
                                    out=xn_[:nt, blk * 512:(blk + 1) * 512], in0=pp[:nt, :],
                                    in1=G2[:nt, blk * 512:(blk + 1) * 512], op=ALU.mult))
                            A("dve", [xn_, xt], [xn_], lambda h, xn_=xn_, xt=xt: h.tensor_tensor(
                                out=xn_[:nt, :], in0=xn_[:nt, :], in1=xt[:nt, :], op=ALU.add))
                            fw.store("sp", xn_, xs[r0:r0 + nt, :], xn_[:nt, :], dk=xtrk(r0))

                        PAIRS = [(0, 1, 2), (3, 4, 5), (6, 7, 8), (9, 10)]
                        ldx(0)
                        if ntl > 1:
                            ldx(1)
                        _rms_ffn(xts[0], ntk(0), hT, l, s, scr, st4)
                        for q in range(4):
                            up_pairs(0, PAIRS[q])
                        carry(0)
                        for ti in range(ntl):
                            if ti + 2 < ntl:
                                ldx(ti + 2)
                            if ti + 1 < ntl:
                                _rms_ffn(xts[ti + 1], ntk(ti + 1), hT, l, s, scr, st4)
                            for q in range(4):
                                ew(ti, q)
                                if ti + 1 < ntl:
                                    up_pairs(ti + 1, PAIRS[q])
                            if ti + 1 < ntl:
                                carry(ti + 1)
                            down(ti)
                        dst = O["ffn_s"][l] if samp else O["ffn_p"][l, s]
                        for c in range(22):
                            fw.store("sp", fo, dst[:, c * 128:(c + 1) * 128].rearrange("t p -> p t"), fo[:, c, :],
                                     allow_slow_non_contiguous=True)
                    fw.barrier()
                fw.release_phase()

            def _rms_ffn(xt, nt, hT, l, s, scr, st4):
                class _V:
                    def __init__(self, t, c0):
                        self.t, self.c0, self.k = t, c0, t.k
                    def __getitem__(self, idx):
                        p, c, ss = idx
                        return self.t[p, self.c0 + c, ss]
                sh = _V(modT[l], 24)
                A("act", [xt], [scr, st4], lambda h: h.activation(out=scr[:nt, :], in_=xt[:nt, :], func=AF.Square,
                                                                 accum_out=st4[:nt, 0:1]))
                A("act", [st4], [st4], lambda h: h.activation(out=st4[:nt, 1:2], in_=st4[:nt, 0:1], func=AF.Sqrt,
                                                              scale=1.0 / D, bias=EPS))
                A("dve", [st4], [st4], lambda h: h.reciprocal(out=st4[:nt, 2:3], in_=st4[:nt, 1:2]))
                A("dve", [xt, st4], [scr], lambda h: h.tensor_scalar(out=scr[:nt, :], in0=xt[:nt, :],
                                                                    scalar1=st4[:nt, 2:3], scalar2=None, op0=ALU.mult))
                pt = ptb.next()
                def f(h):
                    for c in range(8):
                        ins = h.transpose(out=pt[:, c * 128:c * 128 + nt], in_=scr[:nt, c * 128:(c + 1) * 128],
                                          identity=idb[:nt, :nt])
                    return ins
                A("pe", [scr, idb], [pt], f)
                for c in range(8):
                    A("act", [pt, GS[l][1], modT[l]], [hT], lambda h, c=c: h.activation(
                        out=hT[:, c, :nt], in_=pt[:, c * 128:c * 128 + nt], func=AF.Identity,
                        scale=GS[l][1][:, c, s:s + 1], bias=modT[l][:, 24 + c, s:s + 1]))


            def cd_phase():
              with ExitStack() as ph:
                fw.ts = ph
                wIn = fw.sb("wIn2", [128, 8, 3200], BF16)
                wOut = fw.sb("wOut2", [128, 8, D], BF16)
                for kc in range(8):
                    fw.load("pool", wIn, wIn[:, kc, 0:3080], I["w_in_cd"][kc * 128:(kc + 1) * 128, :])
                for kc in range(8):
                    fw.load("pool", wOut, wOut[:, kc, :], I["w_out_cd"][kc * 128:(kc + 1) * 128, :])
                kcT = fw.sb("kcT", [128, 4, 2176], BF16)
                Vc = fw.sb("Vc2", [128, 8, 8, 96], BF16)
                bRel = fw.sb("bRel", [128, 8, 2, 128], BF16)
                bM4 = fw.sb("bM4", [128, 128], BF16)
                crel = fw.sb("crel", [128, 8], F32)
                mn0 = fw.sb("mn0c", [128, 128], F32)
                mn4 = fw.sb("mn4c", [128, 128], F32)
                rmask = fw.sb("rmaskc", [128, 128], F32)
                trigt = fw.sb("trigt", [128, 128], F32)
                ones = fw.sb("onesc", [128, 128], F32)
                dcr = fw.sb("dcr", [5, D], F32)
                dcw = fw.sb("dcw", [128, 8, 5], F32)
                dng = fw.sb("dng", [128, 512], F32)
                dsk = fw.sb("dsk", [128, 8], F32)
                dtb = fw.sb("dtb", [128, 8], F32)
                Aneg = fw.sb("Aneg", [128, 8], F32)
                G1 = fw.sb("G1c", [128, D], F32)
                fw.load("sp", crel, crel[:], I["rel_table"][256:257, :].partition_broadcast(128))
                fw.load("sp", mn0, mn0[:], I["mneg"][0])
                fw.load("sp", mn4, mn4[:], I["mneg"][1])
                fw.load("sp", rmask, rmask[:], I["retmask"])
                fw.load("sp", trigt, trigt[:], I["trigt"])
                fw.load("sp", dcr, dcr[0:4, :], I["d_conv_w"])
                fw.load("sp", dcr, dcr[4:5, :], I["d_conv_b"])
                fw.load("sp", dng, dng[:], I["d_norm_g"].partition_broadcast(128))
                fw.load("sp", dsk, dsk[:], I["d_skip"].partition_broadcast(128))
                fw.load("sp", dtb, dtb[:], I["d_dt_bias"].partition_broadcast(128))
                fw.load("sp", Aneg, Aneg[:], I["d_a_log"].partition_broadcast(128))
                A("act", [Aneg], [Aneg], lambda h: h.activation(out=Aneg[:], in_=Aneg[:], func=AF.Exp))
                A("dve", [Aneg], [Aneg], lambda h: h.tensor_scalar(out=Aneg[:], in0=Aneg[:], scalar1=-1.0, scalar2=None,
                                                                  op0=ALU.mult))
                A("dve", [], [ones], lambda h: h.memset(ones[:], 1.0))
                A("dve", [mn4], [bM4], lambda h: h.tensor_copy(out=bM4[:], in_=mn4[:]))
                A("dve", [], [Vc], lambda h: h.memset(Vc[:, :, :, 64:96], 1.0))
                rows_to_fm(dcr, 5, 8, dcw)
                bias_tiles("rel", bRel, 8, sreld, srk, [1, 129], False, crel, [mn0, None])

                xrot = Rot([fw.sb(f"xc{i}", [128, D], F32) for i in range(2)])
                scr = fw.sb("scrc", [128, D], BF16)
                st4 = fw.sb("st4c", [128, 4], F32)
                hT = fw.sb("hTc", [128, 8, 128], BF16)
                qc = fw.sb("qc", [128, 512], BF16)
                kc16 = fw.sb("kc16", [128, 512], BF16)
                kcf = Rot([fw.sb(f"kcf{i}", [128, 512], F32) for i in range(2)])
                vcf = Rot([fw.sb(f"vcf{i}", [128, 512], F32) for i in range(2)])
                qcT = fw.sb("qcT", [128, 4, 128], BF16)
                zs = fw.sb("zs", [128, 512], F32)
                ze = fw.sb("ze", [128, 512], F32)
                dts = fw.sb("dts", [128, 40], F32)
                xbuf = fw.sb("xbuf", [128, 8, 132], F32)
                xo3 = fw.sb("xo3", [128, 8, 3], F32)
                cvx = fw.sb("cvx", [128, 8, 128], F32)
                sle = fw.sb("sle", [128, 8, 128], F32)
                xbb = fw.sb("xbb", [128, 8, 128], BF16)
                xtok = fw.sb("xtok", [128, 768], BF16)
                xw = fw.sb("xw", [128, 512], BF16)
                Rm = fw.sb("Rm", [128, 8, 128], F32)
                Eh = fw.sb("Eh", [128, 8, 128], F32)
                cbm = fw.sb("cbm", [128, 2, 128], F32)
                WT = fw.sb("WT", [128, 8, 128], BF16)
                stT = fw.sb("stT", [128, 512], F32)
                stTb = fw.sb("stTb", [128, 512], BF16)
                sld = fw.sb("sld", [64, 8, 128], F32)
                yin = fw.sb("yin", [128, 512], F32)
                yt_ = fw.sb("ytc", [128, 512], F32)
                gsq = fw.sb("gsqc", [128, 512], F32)
                gst = fw.sb("gstc", [128, 8], F32)
                mix = fw.sb("mixc", [128, D], BF16)
                pTt = Rot([fw.sb(f"pTc{i}", [128, 4, 128], BF16) for i in range(3)])
                rc = fw.sb("rcc", [128, 4], F32)
                mT = fw.sb("mTc", [128, 8, 128], BF16)
                xo = Rot([fw.sb(f"xoc{i}", [128, D], F32) for i in range(2)])
                ot = fw.sb("otc", [128, D], F32)

                for s, (ntok, rp0, samp, row0) in enumerate(SEQS):
                    if str(s) not in SEQSEL:
                        continue
                    ntl = min(MAXT, (ntok + 127) // 128)
                    fw.load("sp", G1, G1[:], modd[s:s + 1, 1, 16 * 128:24 * 128].partition_broadcast(128), dk=mk)
                    if samp:
                        A("dve", [], [kcT], lambda h: h.memset(kcT[:, :, 1088:1152], 0.0))
                        A("dve", [], [Vc], lambda h: h.memset(Vc[64:128, 0, :, 0:64], 0.0))
                        for j in range(4):
                            kf = kcf.next()
                            fw.load("sp", kf, kf[:], I["ck_c"][j * 128:(j + 1) * 128, :])
                            A("dve", [kf], [kc16], lambda h, kf=kf: h.tensor_copy(out=kc16[:], in_=kf[:]))
                            pt = transpose_blocks(kc16, 128, 4, None)
                            A("act", [pt], [kcT], lambda h, pt=pt, j=j: h.activation(
                                out=kcT[:, :, (4 + j) * 128:(5 + j) * 128],
                                in_=pt[:, 0:512].rearrange("p (c t) -> p c t", c=4), func=AF.Copy))
                            vf = vcf.next()
                            fw.load("sp", vf, vf[:], I["cv_c"][j * 128:(j + 1) * 128, :])
                            A("dve", [vf], [Vc], lambda h, vf=vf, j=j: h.tensor_copy(
                                out=Vc[:, (4 + j) % 8, :, 0:64], in_=vf[:].rearrange("p (h e) -> p h e", h=8)))
                        for c in range(8):
                            fw.load("sp", xbuf, xbuf[:, c, 0:3],
                                    I["dconv_c"][:, c * 128:(c + 1) * 128].rearrange("t p -> p t"),
                                    allow_slow_non_contiguous=True)
                        fw.load("sp", sld, sld[:], I["dssm_c"].rearrange("h p n -> p h n"))
                        pp = psf.next()
                        def f(h, pp=pp):
                            for hh in range(8):
                                ins = h.matmul(pp[:, hh * 64:(hh + 1) * 64], sld[:, hh, :], idf[0:64, 0:64],
                                               start=True, stop=True)
                            return ins
                        A("pe", [sld, idf], [pp], f)
                        A("dve", [pp], [stT], lambda h, pp=pp: h.tensor_copy(out=stT[:], in_=pp[:]))
                    else:
                        A("dve", [], [stT], lambda h: h.memset(stT[:], 0.0))
                        A("dve", [], [xbuf], lambda h: h.memset(xbuf[:, :, 0:3], 0.0))
                    A("act", [stT], [stTb], lambda h: h.activation(out=stTb[:], in_=stT[:], func=AF.Copy))
                    for ti in range(ntl):
                        nt = min(128, ntok - ti * 128)
                        r0 = row0 + ti * 128
                        qi = 8 if samp else ti

                        def ldxc(tj):
                            ntj = min(128, ntok - tj * 128)
                            rj = row0 + tj * 128
                            xj = xrot.next()
                            fw.load("sp", xj, xj[:ntj, :], xs[rj:rj + ntj, :], dk=xtrk(rj))
                            return xj
                        if ti == 0:
                            xnext = ldxc(0)
                        xt = xnext
                        if ti + 1 < ntl:
                            xnext = ldxc(ti + 1)
                        rmsnorm_to_hT(xt, nt, hT, GS[1][0], modT[1], s, scr, st4)

                        def proj(c0, n, nrow=None):
                            pp = psf.next()
                            def f(h):
                                for c in range(8):
                                    ins = h.matmul(pp[:nt, 0:n], hT[:, c, :nt], wIn[:, c, c0:c0 + n],
                                                   start=(c == 0), stop=(c == 7))
                                return ins
                            A("pe", [hT, wIn], [pp], f)
                            return pp
                        pp = proj(0, 512)
                        A("act", [pp], [qc], lambda h, pp=pp: h.activation(out=qc[:nt, :], in_=pp[:nt, :], func=AF.Copy,
                                                                         scale=0.125))
                        pp = proj(512, 512)
                        kf = kcf.next()
                        A("act", [pp], [kf], lambda h, pp=pp, kf=kf: h.activation(out=kf[:nt, :], in_=pp[:nt, :], func=AF.Copy))
                        A("dve", [kf], [kc16], lambda h, kf=kf: h.tensor_copy(out=kc16[:nt, :], in_=kf[:nt, :]))
                        pp = proj(1024, 512)
                        vf = vcf.next()
                        A("act", [pp], [vf], lambda h, pp=pp, vf=vf: h.activation(out=vf[:nt, :], in_=pp[:nt, :], func=AF.Copy))
                        A("dve", [pp], [Vc], lambda h, pp=pp: h.tensor_copy(
                            out=Vc[:nt, qi % 8, :, 0:64], in_=pp[:nt, :].rearrange("p (h e) -> p h e", h=8)))
                        if samp:
                            fw.store("sp", kf, O["ck_s"][0:nt, :], kf[:nt, :])
                            fw.store("sp", vf, O["cv_s"][0:nt, :], vf[:nt, :])
                        elif ti >= 12:
                            fw.store("sp", kf, O["ck_p"][s, (ti - 12) * 128:(ti - 11) * 128, :], kf[:nt, :])
                            fw.store("sp", vf, O["cv_p"][s, (ti - 12) * 128:(ti - 11) * 128, :], vf[:nt, :])
                        pp = proj(1536, 512)
                        A("act", [pp], [ze], lambda h, pp=pp: h.activation(out=ze[:nt, :], in_=pp[:nt, :], func=AF.Sigmoid))
                        A("dve", [pp, ze], [zs], lambda h, pp=pp: h.tensor_tensor(out=zs[:nt, :], in0=pp[:nt, :], in1=ze[:nt, :],
                                                                                 op=ALU.mult))
                        pp = proj(3072, 8)
                        A("dve", [pp, dtb], [dts], lambda h, pp=pp: h.tensor_tensor(out=dts[:nt, 0:8], in0=pp[:nt, 0:8],
                                                                                   in1=dtb[:nt, :], op=ALU.add))
                        A("act", [dts], [dts], lambda h: h.activation(out=dts[:nt, 0:8], in_=dts[:nt, 0:8], func=AF.Exp))
                        A("act", [dts], [dts], lambda h: h.activation(out=dts[:nt, 0:8], in_=dts[:nt, 0:8], func=AF.Ln, bias=1.0))
                        A("dve", [dts, Aneg], [dts], lambda h: h.tensor_tensor(out=dts[:nt, 8:16], in0=dts[:nt, 0:8],
                                                                              in1=Aneg[:nt, :], op=ALU.mult))
                        for half in range(2):
                            pp = psf.next()
                            def f(h, pp=pp, half=half):
                                for cc in range(4):
                                    c = half * 4 + cc
                                    for kc in range(8):
                                        ins = h.matmul(pp[:, cc * 128:cc * 128 + nt],
                                                       wIn[:, kc, 2048 + c * 128:2048 + (c + 1) * 128], hT[:, kc, :nt],
                                                       start=(kc == 0), stop=(kc == 7))
                                return ins
                            A("pe", [wIn, hT], [pp], f)
                            A("act", [pp], [xbuf], lambda h, pp=pp, half=half: h.activation(
                                out=xbuf[:, half * 4:(half + 1) * 4, 3:3 + nt],
                                in_=pp[:, :].rearrange("p (c t) -> p c t", c=4)[:, :, :nt], func=AF.Copy))
                        def wb(col):
                            return dcw[:, :, col:col + 1].to_broadcast([128, 8, nt])
                        cvv, tAv = cvx[:, :, :nt], sle[:, :, :nt]
                        A("dve", [xbuf, dcw], [cvx], lambda h: h.tensor_tensor(out=cvv, in0=xbuf[:, :, 3:3 + nt], in1=wb(3),
                                                                            op=ALU.mult))
                        for tp in range(3):
                            A("dve", [xbuf, dcw], [sle], lambda h, tp=tp: h.tensor_tensor(out=tAv, in0=xbuf[:, :, tp:tp + nt],
                                                                                          in1=wb(tp), op=ALU.mult))
                            A("dve", [cvx, sle], [cvx], lambda h: h.tensor_tensor(out=cvv, in0=cvv, in1=tAv, op=ALU.add))
                        A("pool", [cvx, dcw], [cvx], lambda h: h.tensor_tensor(out=cvv, in0=cvv, in1=wb(4), op=ALU.add))
                        A("pool", [xbuf], [xo3], lambda h: h.tensor_copy(out=xo3[:], in_=xbuf[:, :, nt:nt + 3]))
                        A("pool", [xo3], [xbuf], lambda h: h.tensor_copy(out=xbuf[:, :, 0:3], in_=xo3[:]))
                        A("act", [cvx], [sle], lambda h: h.activation(out=sle[:, :, :nt], in_=cvx[:, :, :nt], func=AF.Sigmoid))
                        A("dve", [sle, cvx], [xbb], lambda h: h.tensor_tensor(out=xbb[:, :, :nt], in0=sle[:, :, :nt],
                                                                             in1=cvx[:, :, :nt], op=ALU.mult))
                        pt = ptb.next()
                        def f(h, pt=pt):
                            for c in range(6):
                                ins = h.transpose(out=pt[:nt, c * 128:(c + 1) * 128], in_=xbb[:, c, :nt], identity=idb[:, :])
                            return ins
                        A("pe", [xbb, idb], [pt], f)
                        A("act", [pt], [xtok], lambda h, pt=pt: h.activation(out=xtok[:nt, :], in_=pt[:nt, 0:768], func=AF.Copy))
                        pt = transpose_blocks(qc, nt, 4, None)
                        A("act", [pt], [qcT], lambda h, pt=pt: h.activation(
                            out=qcT[:, :, :nt], in_=pt[:, 0:512].rearrange("p (c t) -> p c t", c=4)[:, :, :nt], func=AF.Copy))
                        pt = transpose_blocks(kc16, nt, 4, None)
                        A("act", [pt], [kcT], lambda h, pt=pt: h.activation(
                            out=kcT[:, :, qi * 128:qi * 128 + nt],
                            in_=pt[:, 0:512].rearrange("p (c t) -> p c t", c=4)[:, :, :nt], func=AF.Copy))
                        pc = psf.next()
                        def f(h, pc=pc):
                            h.matmul(pc[:nt, 0:8], rmask[:nt, :nt], dts[:nt, 8:16], start=True, stop=True)
                            h.matmul(pc[:nt, 8:16], trigt[:nt, :nt], dts[:nt, 8:16], start=True, stop=True)
                            return h.matmul(pc[:, 16:24], ones[:nt, :], dts[:nt, 8:16], start=True, stop=True)
                        A("pe", [rmask, trigt, ones, dts], [pc], f)
                        A("act", [pc], [dts], lambda h, pc=pc: h.activation(out=dts[:nt, 16:32], in_=pc[:nt, 0:16], func=AF.Exp))
                        A("act", [pc], [dts], lambda h, pc=pc: h.activation(out=dts[:, 32:40], in_=pc[:, 16:24], func=AF.Exp))
                        A("dve", [dts], [dts], lambda h: h.tensor_tensor(out=dts[:nt, 24:32], in0=dts[:nt, 24:32],
                                                                        in1=dts[:nt, 0:8], op=ALU.mult))
                        pcb = psf.next()
                        def f(h, pcb=pcb):
                            for g in range(2):
                                ins = h.matmul(pcb[:nt, g * 128:g * 128 + nt], xbb[:, 4 + g, :nt], xbb[:, 6 + g, :nt],
                                               start=True, stop=True)
                            return ins
                        A("pe", [xbb], [pcb], f)
                        A("dve", [pcb, rmask], [cbm], lambda h, pcb=pcb: h.tensor_tensor(
                            out=cbm[:nt, :, :nt], in0=pcb[:nt, 0:256].rearrange("p (g t) -> p g t", g=2)[:, :, :nt],
                            in1=rmask[:nt, :nt].unsqueeze(1).to_broadcast([nt, 2, nt]), op=ALU.mult))
                        px = pacc.next()
                        def f(h, px=px):
                            for hh in range(8):
                                ins = h.matmul(px[:nt, hh * 64:(hh + 1) * 64], xbb[:, 6 + hh // 4, :nt],
                                               stTb[:, hh * 64:(hh + 1) * 64], start=True, stop=True)
                            return ins
                        A("pe", [xbb, stTb], [px], f)
                        A("dve", [rmask, dts], [Rm], lambda h: h.tensor_tensor(
                            out=Rm[:nt, :, :nt], in0=rmask[:nt, :nt].unsqueeze(1).to_broadcast([nt, 8, nt]),
                            in1=dts[:nt, 8:16].unsqueeze(2).to_broadcast([nt, 8, nt]), op=ALU.mult))
                        for half in range(2):
                            pg = psf.next()
                            def f(h, pg=pg, half=half):
                                for hh in range(4):
                                    ins = h.matmul(pg[:nt, hh * 128:hh * 128 + nt], trigt[:nt, :nt], Rm[:nt, half * 4 + hh, :nt],
                                                   start=True, stop=True)
                                return ins
                            A("pe", [trigt, Rm], [pg], f)
                            A("act", [pg], [Eh], lambda h, pg=pg, half=half: h.activation(
                                out=Eh[:nt, half * 4:(half + 1) * 4, :nt],
                                in_=pg[:nt, :].rearrange("p (c t) -> p c t", c=4)[:, :, :nt], func=AF.Exp))
                        A("dve", [Eh, dts], [Eh], lambda h: h.tensor_tensor(
                            out=Eh[:nt, :, :nt], in0=Eh[:nt, :, :nt],
                            in1=dts[:nt, 0:8].unsqueeze(2).to_broadcast([nt, 8, nt]), op=ALU.mult))
                        for g in range(2):
                            A("dve", [Eh, cbm], [WT], lambda h, g=g: h.tensor_tensor(
                                out=WT[:nt, g * 4:(g + 1) * 4, :nt], in0=Eh[:nt, g * 4:(g + 1) * 4, :nt],
                                in1=cbm[:nt, g:g + 1, :nt].to_broadcast([nt, 4, nt]), op=ALU.mult))
                        py = pacc.next()
                        def f(h, py=py):
                            for hh in range(8):
                                ins = h.matmul(py[:nt, hh * 64:(hh + 1) * 64], WT[:nt, hh, :nt], xtok[:nt, hh * 64:(hh + 1) * 64],
                                               start=True, stop=True)
                            return ins
                        A("pe", [WT, xtok], [py], f)
                        A("act", [py], [yin], lambda h, py=py: h.activation(out=yin[:nt, :], in_=py[:nt, :], func=AF.Copy))
                        A("dve", [px, dts], [yt_], lambda h, px=px: h.tensor_tensor(
                            out=yt_[:nt, :].rearrange("p (h e) -> p h e", h=8),
                            in0=px[:nt, :].rearrange("p (h e) -> p h e", h=8),
                            in1=dts[:nt, 16:24].unsqueeze(2).to_broadcast([nt, 8, 64]), op=ALU.mult))
                        A("dve", [yt_, yin], [yt_], lambda h: h.tensor_tensor(out=yt_[:nt, :], in0=yt_[:nt, :], in1=yin[:nt, :],
                                                                              op=ALU.add))
                        A("dve", [xtok, dsk], [yin], lambda h: h.tensor_tensor(
                            out=yin[:nt, :].rearrange("p (h e) -> p h e", h=8),
                            in0=xtok[:nt, 0:512].rearrange("p (h e) -> p h e", h=8),
                            in1=dsk[:nt, :].unsqueeze(2).to_broadcast([nt, 8, 64]), op=ALU.mult))
                        A("dve", [yt_, yin], [yt_], lambda h: h.tensor_tensor(out=yt_[:nt, :], in0=yt_[:nt, :], in1=yin[:nt, :],
                                                                              op=ALU.add))
                        A("dve", [xtok, dts], [xw], lambda h: h.tensor_tensor(
                            out=xw[:nt, :].rearrange("p (h e) -> p h e", h=8),
                            in0=xtok[:nt, 0:512].rearrange("p (h e) -> p h e", h=8),
                            in1=dts[:nt, 24:32].unsqueeze(2).to_broadcast([nt, 8, 64]), op=ALU.mult))
                        pd = psf.next()
                        def f(h, pd=pd):
                            for hh in range(8):
                                g = hh // 4
                                ins = h.matmul(pd[:, hh * 64:(hh + 1) * 64], xtok[:nt, 512 + g * 128:512 + (g + 1) * 128],
                                               xw[:nt, hh * 64:(hh + 1) * 64], start=True, stop=True)
                            return ins
                        A("pe", [xtok, xw], [pd], f)
                        A("dve", [stT, dts], [stT], lambda h: h.tensor_tensor(
                            out=stT[:].rearrange("p (h e) -> p h e", h=8), in0=stT[:].rearrange("p (h e) -> p h e", h=8),
                            in1=dts[:, 32:40].unsqueeze(2).to_broadcast([128, 8, 64]), op=ALU.mult))
                        A("dve", [pd, stT], [stT], lambda h, pd=pd: h.tensor_tensor(out=stT[:], in0=pd[:], in1=stT[:], op=ALU.add))
                        A("act", [stT], [stTb], lambda h: h.activation(out=stTb[:], in_=stT[:], func=AF.Copy))
                        A("dve", [yt_, zs], [yt_], lambda h: h.tensor_tensor(out=yt_[:nt, :], in0=yt_[:nt, :], in1=zs[:nt, :],
                                                                            op=ALU.mult))
                        A("act", [yt_], [gsq], lambda h: h.activation(out=gsq[:nt, :], in_=yt_[:nt, :], func=AF.Square))
                        A("dve", [gsq], [gst], lambda h: h.tensor_reduce(
                            out=gst[:nt, 0:2], in_=gsq[:nt, :].rearrange("p (g e) -> p g e", g=2), axis=AX.X, op=ALU.add))
                        A("act", [gst], [gst], lambda h: h.activation(out=gst[:nt, 0:2], in_=gst[:nt, 0:2], func=AF.Sqrt,
                                                                      scale=1.0 / 256, bias=EPS))
                        A("dve", [gst], [gst], lambda h: h.reciprocal(out=gst[:nt, 0:2], in_=gst[:nt, 0:2]))
                        A("dve", [yt_, gst], [yt_], lambda h: h.tensor_tensor(
                            out=yt_[:nt, :].rearrange("p (g e) -> p g e", g=2), in0=yt_[:nt, :].rearrange("p (g e) -> p g e", g=2),
                            in1=gst[:nt, 0:2].unsqueeze(2).to_broadcast([nt, 2, 256]), op=ALU.mult))
                        A("dve", [yt_, dng], [mix], lambda h: h.tensor_tensor(out=mix[:nt, 512:1024], in0=yt_[:nt, :],
                                                                              in1=dng[:nt, :], op=ALU.mult))
                        jlo = max(0, qi - 4)
                        if samp:
                            groups = [[4, 5, 6, 7], [8]]
                        else:
                            js = list(range(jlo, qi + 1))
                            groups = [js[:4], js[4:]] if len(js) > 4 else [js]
                        pas = {}
                        def qk_c(c4, i_, g):
                            u = 2 * c4 + i_
                            pS = psf.next()
                            def f(h):
                                for jj, j_ in enumerate(g):
                                    dl = j_ - qi
                                    extra = dl >= -1 or dl == -4
                                    ins = h.matmul(pS[:, jj * 128:jj * 128 + nt],
                                                   kcT[i_ * 64:(i_ + 1) * 64, c4, j_ * 128:(j_ + 1) * 128],
                                                   qcT[i_ * 64:(i_ + 1) * 64, c4, :nt], start=True, stop=not extra)
                                    if dl >= -1:
                                        ins = h.matmul(pS[:, jj * 128:jj * 128 + nt], idb[:, :],
                                                       bRel[:, u, -dl, :nt], start=False, stop=True)
                                    elif dl == -4:
                                        ins = h.matmul(pS[:, jj * 128:jj * 128 + nt], idb[:, :],
                                                       bM4[:, :nt], start=False, stop=True)
                                return ins
                            A("pe", [kcT, qcT, idb, bRel, bM4], [pS], f)
                            pt_ = pTt.next()
                            ng = len(g)
                            A("act", [pS, crel], [pt_], lambda h: h.activation(
                                out=pt_[:, 0:ng, :nt],
                                in_=pS[:, 0:ng * 128].rearrange("p (g t) -> p g t", g=ng)[:, :, :nt],
                                func=AF.Exp, bias=crel[:, u:u + 1]))
                            return pt_

                        def pv_c(c4, i_, g, pt_):
                            u = 2 * c4 + i_
                            if c4 not in pas:
                                pas[c4] = pacc.next()
                            pa = pas[c4]
                            def f(h):
                                for jj, j_ in enumerate(g):
                                    ins = h.matmul(pa[:nt, i_ * 256:i_ * 256 + 65], pt_[:, jj, :nt],
                                                   Vc[:, j_ % 8, u, 0:65], start=(j_ == groups[0][0]), stop=(j_ == qi))
                                return ins
                            A("pe", [pt_, Vc], [pa], f)
                            if i_ == 1 and g is groups[-1]:
                                pav = pa[:nt, 0:512].rearrange("p (i e) -> p i e", i=2)
                                A("dve", [pa], [rc], lambda h: h.reciprocal(out=rc[:nt, 0:2], in_=pav[:, :, 64]))
                                for ii in range(2):
                                    uu = 2 * c4 + ii
                                    A("dve", [pa, rc], [mix], lambda h, ii=ii, uu=uu: h.tensor_scalar(
                                        out=mix[:nt, uu * 64:(uu + 1) * 64], in0=pa[:nt, ii * 256:ii * 256 + 64],
                                        scalar1=rc[:nt, ii:ii + 1], scalar2=None, op0=ALU.mult))

                        items = [(c4, i_, g) for c4 in range(4) for i_ in range(2) for g in groups]
                        prev = None
                        for it in items:
                            cur = qk_c(*it)
                            if prev is not None:
                                pv_c(*prev)
                            prev = it + (cur,)
                        pv_c(*prev)
                        pt = transpose_blocks(mix, nt, 8, None)
                        A("act", [pt], [mT], lambda h, pt=pt: h.activation(
                            out=mT[:, :, :nt], in_=pt[:, :].rearrange("p (c t) -> p c t", c=8)[:, :, :nt], func=AF.Copy))
                        xn_ = xo.next()
                        for blk in range(2):
                            pp = psf.next()
                            def f(h, pp=pp, blk=blk):
                                for c in range(8):
                                    ins = h.matmul(pp[:nt, :], mT[:, c, :nt], wOut[:, c, blk * 512:(blk + 1) * 512],
                                                   start=(c == 0), stop=(c == 7))
                                return ins
                            A("pe", [mT, wOut], [pp], f)
                            A("dve", [pp, G1], [ot], lambda h, pp=pp, blk=blk: h.tensor_tensor(
                                out=ot[:nt, blk * 512:(blk + 1) * 512], in0=pp[:nt, :],
                                in1=G1[:nt, blk * 512:(blk + 1) * 512], op=ALU.mult))
                        A("dve", [ot, xt], [xn_], lambda h, xn_=xn_, xt=xt: h.tensor_tensor(
                            out=xn_[:nt, :], in0=ot[:nt, :], in1=xt[:nt, :], op=ALU.add))
                        fw.store("sp", xn_, xs[r0:r0 + nt, :], xn_[:nt, :], dk=xtrk(r0))
                    dst = O["dconv_s"] if samp else O["dconv_p"][s]
                    for c in range(8):
                        fw.store("sp", xo3, dst[:, c * 128:(c + 1) * 128].rearrange("t p -> p t"), xo3[:, c, :],
                                 allow_slow_non_contiguous=True)
                    dst = O["dssm_s"] if samp else O["dssm_p"][s]
                    for half in range(2):
                        pp = psf.next()
                        def f(h, pp=pp, half=half):
                            for hh in range(4):
                                ins = h.matmul(pp[0:64, hh * 128:(hh + 1) * 128],
                                               stT[:, (half * 4 + hh) * 64:(half * 4 + hh + 1) * 64], idf[:, :],
                                               start=True, stop=True)
                            return ins
                        A("pe", [stT, idf], [pp], f)
                        A("act", [pp], [sld], lambda h, pp=pp, half=half: h.activation(
                            out=sld[:, half * 4:(half + 1) * 4, :], in_=pp[0:64, :].rearrange("p (h n) -> p h n", h=4),
                            func=AF.Copy))
                    fw.store("sp", sld, dst.rearrange("h p n -> p h n"), sld[:])
                fw.barrier()
              fw.release_phase()

            if 'F' in PH:
                ffn_phase(0)
            if 'C' in PH:
                cd_phase()
            if 'G' in PH:
                ffn_phase(1)

            with ExitStack() as ph:
              if 'Z' in PH:
                fw.ts = ph
                fg = fw.sb("fg", [128, D], F32)
                fw.load("sp", fg, fg[:], I["final_g"].partition_broadcast(128))
                xrot = Rot([fw.sb(f"xz{i}", [128, D], F32) for i in range(2)])
                yo = Rot([fw.sb(f"yo{i}", [128, D], F32) for i in range(2)])
                scr = fw.sb("scrz", [128, D], F32)
                st4 = fw.sb("st4z", [128, 4], F32)
                for s, (ntok, rp0, samp, row0) in enumerate(SEQS):
                    if str(s) not in SEQSEL:
                        continue
                    ntl = min(MAXT, (ntok + 127) // 128)
                    for ti in range(ntl):
                        nt = min(128, ntok - ti * 128)
                        r0 = row0 + ti * 128
                        xt = xrot.next()
                        fw.load("sp", xt, xt[:nt, :], xs[r0:r0 + nt, :], dk=xtrk(r0))
                        A("act", [xt], [scr, st4], lambda h: h.activation(out=scr[:nt, :], in_=xt[:nt, :], func=AF.Square,
                                                                         accum_out=st4[:nt, 0:1]))
                        A("act", [st4], [st4], lambda h: h.activation(out=st4[:nt, 1:2], in_=st4[:nt, 0:1], func=AF.Sqrt,
                                                                      scale=1.0 / D, bias=EPS))
                        A("dve", [st4], [st4], lambda h: h.reciprocal(out=st4[:nt, 2:3], in_=st4[:nt, 1:2]))
                        yt = yo.next()
                        A("dve", [xt, st4, fg], [yt], lambda h, yt=yt, xt=xt: h.scalar_tensor_tensor(
                            out=yt[:nt, :], in0=xt[:nt, :], scalar=st4[:nt, 2:3], in1=fg[:nt, :], op0=ALU.mult, op1=ALU.mult))
                        dst = O["y_s"] if samp else O["y_p"]
                        sr0 = 0 if samp else r0
                        fw.store("sp", yt, dst[sr0:sr0 + nt, :], yt[:nt, :])
                fw.barrier()
            fw.release_phase()
    print('OPS', fw.cnt, flush=True)
    return nc


_NC = None


def kernel(**inp):
    global _NC
    f = lambda a: np.ascontiguousarray(np.asarray(a, dtype=np.float32))
    consts = host_consts()
    if _NC is None:
        _NC = build()
    nc = _NC
    shared = {
        "w_mod": f(inp["w_mod"]), "b_mod": f(inp["b_mod"]).reshape(1, -1), "norm_g": f(inp["norm_g"]).reshape(4, D),
        "final_g": f(inp["final_g"]).reshape(1, D), "t5_table": f(inp["t5_table"]),
        "w_in_ab": f(inp["w_in_ab"][0]), "w_out_ab": f(inp["w_out_ab"][0]), "ret_gn": f(inp["ret_gn"]).reshape(1, 512),
        "lam_q": f(inp["lam_q"]).reshape(1, 128), "lam_k": f(inp["lam_k"]).reshape(1, 128),
        "diff_gn": f(inp["diff_gn"]).reshape(1, 128), "w_in_cd": f(inp["w_in_cd"][0]), "w_out_cd": f(inp["w_out_cd"][0]),
        "rel_table": f(inp["rel_table"][0]), "d_conv_w": f(inp["d_conv_w"][0]), "d_conv_b": f(inp["d_conv_b"]).reshape(1, D),
        "d_dt_bias": f(inp["d_dt_bias"]).reshape(1, 8), "d_a_log": f(inp["d_a_log"]).reshape(1, 8),
        "d_skip": f(inp["d_skip"]).reshape(1, 8), "d_norm_g": f(inp["d_norm_g"]).reshape(1, 512),
        "w_up": f(inp["w_up"]), "ffn_cw": f(inp["ffn_conv_w"]).reshape(6, DFF), "ffn_cb": f(inp["ffn_conv_b"]),
        "w_down": f(inp["w_down"]),
    }
    shared.update(consts)
    in_maps = []
    for i in range(8):
        m = dict(shared)
        m["xp"] = f(inp["x_prompt"][2 * i:2 * i + 2]).reshape(4096, D)
        m["xsm"] = f(inp["x_sample"][i])
        m["cc"] = np.concatenate([f(inp["c_prompt"][2 * i:2 * i + 2]), f(inp["c_sample"][i:i + 1])], 0)
        m["ret0"] = f(inp["cache_ret_state"][0, i])
        m["bk_c"] = f(inp["cache_b_k"][0, i]).reshape(1024, 512)
        m["bv_c"] = f(inp["cache_b_v"][0, i]).reshape(1024, 512)
        m["ck_c"] = f(inp["cache_c_k"][0, i]).reshape(512, 512)
        m["cv_c"] = f(inp["cache_c_v"][0, i]).reshape(512, 512)
        m["dconv_c"] = f(inp["state_d_conv"][0, i])
        m["dssm_c"] = f(inp["state_d_ssm"][0, i])
        m["ffnc"] = f(inp["state_ffn_conv"][:, i])
        in_maps.append(m)
    res = run_bass_kernel_spmd(nc, in_maps[:NCORES], core_ids=list(range(NCORES)))
    R = list(res.results) * (8 // NCORES)
    cat = lambda k: np.concatenate([np.asarray(R[i][k]) for i in range(8)], 0)
    stk = lambda k: np.stack([np.asarray(R[i][k]) for i in range(8)], 0)
    y_p = cat("y_p").reshape(16, 2048, D)
    y_s = stk("y_s")
    ret_p = cat("ret_p")[None]
    ret_s = stk("ret_s")[None]
    bk_p = cat("bk_p").reshape(1, 16, 2048, 4, 128)
    bk_s = stk("bk_s").reshape(1, 8, 64, 4, 128)
    bv_p = cat("bv_p").reshape(1, 16, 2048, 4, 128)
    bv_s = stk("bv_s").reshape(1, 8, 64, 4, 128)
    ck_p = cat("ck_p").reshape(1, 16, 512, 8, 64)
    ck_s = stk("ck_s").reshape(1, 8, 64, 8, 64)
    cv_p = cat("cv_p").reshape(1, 16, 512, 8, 64)
    cv_s = stk("cv_s").reshape(1, 8, 64, 8, 64)
    dconv_p = cat("dconv_p")[None]
    dconv_s = stk("dconv_s")[None]
    dssm_p = cat("dssm_p")[None]
    dssm_s = stk("dssm_s")[None]
    ffn_p = np.concatenate([np.asarray(R[i]["ffn_p"]) for i in range(8)], 1)
    ffn_s = np.stack([np.asarray(R[i]["ffn_s"]) for i in range(8)], 1)
    return (y_p, y_s, ret_p, ret_s, bk_p, bk_s, bv_p, bv_s, ck_p, ck_s, cv_p, cv_s,
            dconv_p, dconv_s, dssm_p, dssm_s, ffn_p, ffn_s)
```

```python
import math
from contextlib import ExitStack
import numpy as np
import concourse.bass as bass
import concourse.mybir as mybir
from concourse.bass_utils import run_bass_kernel_spmd

F32 = mybir.dt.float32
BF16 = mybir.dt.bfloat16
AF = mybir.ActivationFunctionType
ALU = mybir.AluOpType
AX = mybir.AxisListType
EPOCH = 12000
EPS = 1e-6
D = 1024
DFF = 2816
NEG = -30000.0


class Trk:
    __slots__ = ("lastw", "readers", "name", "ldsem", "ldcnt", "stsem", "stcnt")

    def __init__(self, name=""):
        self.lastw = None
        self.readers = []
        self.name = name
        self.ldsem = None
        self.ldcnt = 0
        self.stsem = None
        self.stcnt = 0


class DSem:
    def __init__(self, sem):
        self.sem = sem
        self.cnt = 0


class Tl:
    def __init__(self, t, name):
        self.t = t
        self.k = Trk(name)

    def __getitem__(self, idx):
        return self.t[idx]


class PTl(Tl):
    pass


class CView(Tl):
    def __init__(self, tl, c0):
        self.t = tl.t
        self.k = tl.k
        self.c0 = c0

    def __getitem__(self, idx):
        p, c, t = idx
        if isinstance(c, slice):
            c = slice((c.start or 0) + self.c0, (c.stop if c.stop is not None else 4) + self.c0)
        else:
            c = c + self.c0
        return self.t[p, c, t]


class Eng:
    def __init__(self, fw, name, h):
        self.name = name
        self.h = h
        self.ep = 0
        self.n = 0
        self.sem = fw.newsem(f"e_{name}_0")
        self.sems = [self.sem]
        self.seen = {}


class FW:
    def __init__(self, nc):
        self.nc = nc
        self.es = None
        self.ts = None
        self.nsem = 0
        self.E = {}
        self.dtoks = {}
        self.free_ds = []
        self.phase_ds = []

    def start(self, es):
        self.es = es
        self.ts = es
        nc = self.nc
        for name, h in (("pe", nc.tensor), ("act", nc.scalar), ("dve", nc.vector),
                        ("pool", nc.gpsimd), ("sp", nc.sync)):
            self.E[name] = Eng(self, name, h)

    def newsem(self, name):
        self.nsem += 1
        return self.es.enter_context(self.nc.semaphore(f"{name}_{self.nsem}"))

    def sb(self, name, shape, dt):
        self.nsem += 1
        name = f"{name}_{self.nsem}"
        return Tl(self.ts.enter_context(self.nc.sbuf_tensor(name, list(shape), dt)), name)

    def ps(self, name, shape, dt=F32):
        self.nsem += 1
        name = f"{name}_{self.nsem}"
        return PTl(self.ts.enter_context(self.nc.psum_tensor(name, list(shape), dt)), name)

    def _wait(self, e, tok):
        if tok is None:
            return
        key, n = tok
        if key[0] == "e" and key[1] == e.name and e.name == "pe":
            return
        if e.seen.get(key, 0) >= n:
            return
        e.seen[key] = n
        if key[0] == "e":
            sem = self.E[key[1]].sems[key[2]]
        else:
            sem = key[1].sem
        e.h.wait_ge(sem, n)

    def _deps(self, e, reads, writes):
        need = {}

        def add(tok):
            if tok is not None:
                need[tok[0]] = max(need.get(tok[0], 0), tok[1])
        for r in reads:
            k = r.k if isinstance(r, Tl) else r
            add(k.lastw)
        for w in writes:
            k = w.k if isinstance(w, Tl) else w
            add(k.lastw)
            for t in k.readers:
                add(t)
        for key, n in need.items():
            self._wait(e, (key, n))

    def _mark(self, tok, reads, writes):
        for r in reads:
            k = r.k if isinstance(r, Tl) else r
            k.readers.append(tok)
            if len(k.readers) > 32:
                k.readers = k.readers[-32:]
        for w in writes:
            k = w.k if isinstance(w, Tl) else w
            k.lastw = tok
            k.readers = []

    def op(self, en, reads, writes, fn):
        self.cnt = getattr(self, 'cnt', 0) + 1
        if self.cnt > LIMIT:
            return None
        e = self.E[en]
        if en != "pe":
            pr = [r for r in reads if isinstance(r, PTl)]
            if pr:
                reads = [r for r in reads if not isinstance(r, PTl)]
                writes = list(writes) + pr
        self._deps(e, reads, writes)
        ins = fn(e.h)
        e.n += 1
        ins.then_inc(e.sem, 1)
        tok = (("e", en, e.ep), e.n)
        self._mark(tok, reads, writes)
        if e.n >= EPOCH:
            e.ep += 1
            e.n = 0
            e.sem = self.newsem(f"e_{en}_{e.ep}")
            e.sems.append(e.sem)
        return tok

    def load(self, q, st, out_ap, in_ap, dk=None, **kw):
        self.cnt = getattr(self, 'cnt', 0) + 1
        if self.cnt > LIMIT:
            return None
        e = self.E[q]
        k = st.k
        if k.ldsem is None:
            k.ldsem = DSem(self.newsem("p")) if q == "pool" else self.get_ds()
        self._deps(e, [dk] if dk is not None else [], [st])
        e.h.dma_start(out=out_ap, in_=in_ap, **kw).then_inc(k.ldsem.sem, 16)
        k.ldsem.cnt += 16
        tok = (("d", k.ldsem), k.ldsem.cnt)
        k.lastw = tok
        k.readers = []
        if dk is not None:
            dk.readers.append(tok)
        self.dtoks[tok[0]] = tok[1]
        return tok

    def store(self, q, st, out_ap, in_ap, dk=None, **kw):
        self.cnt = getattr(self, 'cnt', 0) + 1
        if self.cnt > LIMIT:
            return None
        e = self.E[q]
        k = st.k
        if k.stsem is None:
            k.stsem = self.get_ds()
        self._deps(e, [st], [dk] if dk is not None else [])
        e.h.dma_start(out=out_ap, in_=in_ap, **kw).then_inc(k.stsem.sem, 16)
        k.stsem.cnt += 16
        tok = (("d", k.stsem), k.stsem.cnt)
        k.readers.append(tok)
        if dk is not None:
            dk.lastw = tok
            dk.readers = []
        self.dtoks[tok[0]] = tok[1]
        return tok

    def get_ds(self):
        if self.free_ds:
            d = self.free_ds.pop()
        else:
            d = DSem(self.newsem("d"))
        if self.ts is not self.es:
            self.phase_ds.append(d)
        return d

    def release_phase(self):
        self.free_ds.extend(self.phase_ds)
        self.phase_ds = []

    def barrier(self, only=None):
        if os.environ.get('K_SPFIN') and getattr(self, 'cnt', 0) > LIMIT:
            only = "sp"
        for en, e in self.E.items():
            if only is not None and en != only:
                continue
            for key, n in list(self.dtoks.items()):
                self._wait(e, (key, n))
            for sn, src in self.E.items():
                if sn != en and src.n > 0:
                    self._wait(e, (("e", sn, src.ep), src.n))
                if sn != en and src.ep > 0 and src.n == 0:
                    self._wait(e, (("e", sn, src.ep - 1), EPOCH))


class Rot:
    def __init__(self, tiles):
        self.tiles = tiles
        self.i = 0

    def next(self):
        t = self.tiles[self.i % len(self.tiles)]
        self.i += 1
        return t


def t5_bucket(rel):
    half = 16
    max_exact = 8
    base = np.where(rel > 0, half, 0)
    n = np.abs(rel)
    nf = np.maximum(n, 1).astype(np.float32)
    large = max_exact + (np.log(nf / max_exact) / math.log(128 / max_exact) * (half - max_exact)).astype(np.int32)
    large = np.minimum(large, half - 1)
    return base + np.where(n < max_exact, n, large)


def host_consts():
    c = {}
    c["ident"] = np.eye(128, dtype=np.float32)
    J = np.zeros((128, 128), np.float32)
    for j in range(128):
        J[j, 127 - j] = 1.0
    c["exch"] = J
    lg = np.log1p(-np.exp2(-5.0 - np.arange(4, dtype=np.float64)))
    inv = np.power(10000.0, -np.arange(64, dtype=np.float32) / 64).astype(np.float32)
    rope = np.zeros((17, 128, 4, 4, 64), np.float32)
    for ti in range(17):
        if ti < 16:
            pos = ti * 128 + np.arange(128)
        else:
            pos = 1024 + np.arange(128)
        lc = np.arange(128, dtype=np.float64)
        ang = pos.astype(np.float32)[:, None] * inv[None, :]
        cs, sn = np.cos(ang), np.sin(ang)
        for h in range(4):
            fq = np.exp((lc + 1.0) * lg[h])[:, None]
            fk = np.exp(-(lc + 1.0) * lg[h])[:, None] * (128.0 ** -0.5)
            rope[ti, :, 0, h] = cs * fq
            rope[ti, :, 1, h] = sn * fq
            rope[ti, :, 2, h] = cs * fk
            rope[ti, :, 3, h] = sn * fk
    c["rope"] = rope
    m = np.zeros((128, 128), np.float32)
    for mm in range(128):
        m[mm, mm:] = 1.0
    c["retmask"] = m
    c["trigt"] = np.ascontiguousarray(1.0 - m)
    gd = np.zeros((2, 128, 512), np.float32)
    for h in range(4):
        gd[0, :, h * 128:(h + 1) * 128] = np.exp(128 * lg[h])
        gd[1, :, h * 128:(h + 1) * 128] = np.exp(64 * lg[h])
    c["gdec"] = gd
    oh = np.zeros((32, 384), np.float32)
    rel = np.arange(384) - 255
    b = t5_bucket(rel)
    oh[b, np.arange(384)] = 1.0
    c["t5oh"] = oh
    m0 = np.zeros((128, 128), np.float32)
    m0[64:, :64] = NEG
    m4 = np.zeros((128, 128), np.float32)
    m4[:64, 64:] = NEG
    c["mneg"] = np.stack([m0, m4])
    return c


import os
PH = os.environ.get('K_PH', '0AFCGZ')
MAXT = int(os.environ.get('K_MAXT', '99'))
LIMIT = int(os.environ.get('K_LIMIT', '100000000'))
NCORES = int(os.environ.get('K_NCORES', '8'))
SEQSEL = os.environ.get('K_SEQS', '012')


def build():
    nc = bass.Bass("TRN2", target_bir_lowering=False)

    def din(n, s):
        return nc.dram_tensor(n, list(s), F32, kind="ExternalInput").ap()

    def dout(n, s):
        return nc.dram_tensor(n, list(s), F32, kind="ExternalOutput").ap()

    def dscr(n, s):
        return nc.dram_tensor(n, list(s), F32, kind="Internal").ap()

    I = {}
    for n, s in (("xp", [4096, D]), ("xsm", [64, D]), ("cc", [3, D]), ("ret0", [4, 128, 128]),
                 ("bk_c", [1024, 512]), ("bv_c", [1024, 512]), ("ck_c", [512, 512]), ("cv_c", [512, 512]),
                 ("dconv_c", [3, 1024]), ("dssm_c", [8, 64, 128]), ("ffnc", [2, 2, DFF]),
                 ("w_mod", [2, D, 6 * D]), ("b_mod", [1, 12 * D]), ("norm_g", [4, D]), ("final_g", [1, D]),
                 ("t5_table", [32, 4]), ("w_in_ab", [D, 3584]), ("w_out_ab", [D, D]), ("ret_gn", [1, 512]),
                 ("lam_q", [1, 128]), ("lam_k", [1, 128]), ("diff_gn", [1, 128]),
                 ("w_in_cd", [D, 3080]), ("w_out_cd", [D, D]), ("rel_table", [257, 8]),
                 ("d_conv_w", [4, D]), ("d_conv_b", [1, D]), ("d_dt_bias", [1, 8]), ("d_a_log", [1, 8]),
                 ("d_skip", [1, 8]), ("d_norm_g", [1, 512]), ("w_up", [2, D, 2 * DFF]),
                 ("ffn_cw", [6, DFF]), ("ffn_cb", [2, DFF]), ("w_down", [2, DFF, D]),
                 ("ident", [128, 128]), ("exch", [128, 128]), ("rope", [17, 128, 4, 4, 64]),
                 ("retmask", [128, 128]), ("trigt", [128, 128]), ("gdec", [2, 128, 512]), ("t5oh", [32, 384]), ("mneg", [2, 128, 128])):
        I[n] = din(n, s)
    O = {}
    for n, s in (("y_p", [4096, D]), ("y_s", [64, D]), ("ret_p", [2, 4, 128, 128]), ("ret_s", [4, 128, 128]),
                 ("bk_p", [4096, 512]), ("bk_s", [64, 512]), ("bv_p", [4096, 512]), ("bv_s", [64, 512]),
                 ("ck_p", [2, 512, 512]), ("ck_s", [64, 512]), ("cv_p", [2, 512, 512]), ("cv_s", [64, 512]),
                 ("dconv_p", [2, 3, 1024]), ("dconv_s", [3, 1024]), ("dssm_p", [2, 8, 64, 128]),
                 ("dssm_s", [8, 64, 128]), ("ffn_p", [2, 2, 2, DFF]), ("ffn_s", [2, 2, DFF])):
        O[n] = dout(n, s)
    xs = dscr("xs", [4160, D])
    modd = dscr("modd", [3, 2, 6 * D])
    s5d = dscr("s5d", [4, 384])
    sreld = dscr("sreld", [8, 384])

    SEQS = [(2048, 0, False, 0), (2048, 0, False, 2048), (64, 16, True, 4096)]
    xk = {}

    def xtrk(r0):
        if r0 not in xk:
            xk[r0] = Trk(f"xs{r0}")
        return xk[r0]

    fw = FW(nc)
    with ExitStack() as es:
        fw.start(es)
        A = fw.op
        idf = fw.sb("idf", [128, 128], F32)
        idb = fw.sb("idb", [128, 128], BF16)
        modT = [fw.sb(f"modT{l}", [128, 48, 3], F32) for l in range(2)]
        GS = [[fw.sb(f"GS{l}{j}", [128, 8, 3], F32) for j in range(2)] for l in range(2)]
        fcw = fw.sb("fcw", [128, 22, 8], F32)
        lam = fw.sb("lam", [128, 4], F32)
        fw.load("sp", idf, idf[:], I["ident"])
        fw.load("pool", idb, idb[:], I["ident"])
        with ExitStack() as pst:
            fw.ts = pst
            ptb = Rot([fw.ps(f"ptb{i}", [128, 1024], BF16) for i in range(2)])
            pacc = Rot([fw.ps(f"pacc{i}", [128, 512], F32) for i in range(2)])
            psf = Rot([fw.ps(f"psf{i}", [128, 512], F32) for i in range(4)])

            def rows_to_fm(rows, R, n, out, c0=0):
                pp = psf.next()
                def f(h):
                    for c in range(n):
                        ins = h.matmul(pp[:, c * R:(c + 1) * R], rows[0:R, (c0 + c) * 128:(c0 + c + 1) * 128],
                                       idf[0:R, 0:R], start=True, stop=True)
                    return ins
                A("pe", [rows, idf], [pp], f)
                A("dve", [pp], [out], lambda h: h.tensor_copy(
                    out=out[:, 0:n, :], in_=pp[:, 0:n * R].rearrange("p (c r) -> p c r", r=R)))

            with ExitStack() as ph:
                fw.ts = ph
                cr = fw.sb("cr", [3, D], F32)
                ce = fw.sb("ce", [3, D], F32)
                cT = fw.sb("cT", [128, 8, 3], F32)
                mr = fw.sb("mr", [3, 2, 6 * D], F32)
                bmr = fw.sb("bmr", [1, 12 * D], F32)
                one = fw.sb("one", [1, 4], F32)
                ngr = fw.sb("ngr", [4, D], F32)
                ngT = fw.sb("ngT", [128, 8, 4], F32)
                fcr = fw.sb("fcr", [8, DFF], F32)
                wrot = Rot([fw.sb(f"wmb{i}", [128, 8, 512], F32) for i in range(2)])
                fw.load("sp", cr, cr[:], I["cc"])
                fw.load("sp", bmr, bmr[:], I["b_mod"])
                fw.load("sp", ngr, ngr[:], I["norm_g"])
                fw.load("sp", fcr, fcr[0:6, :], I["ffn_cw"])
                fw.load("sp", fcr, fcr[6:8, :], I["ffn_cb"])
                A("dve", [], [one], lambda h: h.memset(one[:], 1.0))
                A("act", [cr], [ce], lambda h: h.activation(out=ce[:], in_=cr[:], func=AF.Exp, scale=-1.0))
                A("dve", [ce], [ce], lambda h: h.tensor_scalar_add(out=ce[:], in0=ce[:], scalar1=1.0))
                A("dve", [ce], [ce], lambda h: h.reciprocal(out=ce[:], in_=ce[:]))
                A("dve", [ce, cr], [cr], lambda h: h.tensor_mul(out=cr[:], in0=cr[:], in1=ce[:]))
                rows_to_fm(cr, 3, 8, cT)
                rows_to_fm(ngr, 4, 8, ngT)
                rows_to_fm(fcr, 8, 22, fcw)
                for l in range(2):
                    for nb in range(12):
                        wb = wrot.next()
                        src = I["w_mod"][l, :, nb * 512:(nb + 1) * 512].rearrange("(kc p) n -> p kc n", p=128)
                        fw.load("sp", wb, wb[:, 0:4, :], src[:, 0:4, :])
                        fw.load("sp", wb, wb[:, 4:8, :], src[:, 4:8, :])
                        pp = psf.next()
                        def f(h, wb=wb, pp=pp, l=l, nb=nb):
                            for kc in range(8):
                                h.matmul(pp[0:3, :], cT[:, kc, :], wb[:, kc, :], start=(kc == 0), stop=False)
                            return h.matmul(pp[0:3, :], one[0:1, 0:3],
                                            bmr[0:1, l * 6 * D + nb * 512: l * 6 * D + (nb + 1) * 512],
                                            start=False, stop=True)
                        A("pe", [wb, cT, one, bmr], [pp], f)
                        A("act", [pp], [mr], lambda h, pp=pp, l=l, nb=nb: h.activation(
                            out=mr[0:3, l, nb * 512:(nb + 1) * 512], in_=pp[0:3, :], func=AF.Copy))
                mk = Trk("modd")
                fw.store("sp", mr, modd, mr[:], dk=mk)
                for l in range(2):
                    for half in range(2):
                        pp = psf.next()
                        def f(h, pp=pp, l=l, half=half):
                            for c in range(24):
                                cc_ = half * 24 + c
                                ins = h.matmul(pp[:, c * 3:(c + 1) * 3], mr[0:3, l, cc_ * 128:(cc_ + 1) * 128],
                                               idf[0:3, 0:3], start=True, stop=True)
                            return ins
                        A("pe", [mr, idf], [pp], f)
                        A("dve", [pp], [modT[l]], lambda h, pp=pp, l=l, half=half: h.tensor_copy(
                            out=modT[l][:, half * 24:(half + 1) * 24, :],
                            in_=pp[:, 0:72].rearrange("p (c r) -> p c r", r=3)))
                    for j in range(2):
                        sc0 = 8 + 24 * j
                        A("dve", [modT[l]], [GS[l][j]], lambda h, l=l, j=j, sc0=sc0: h.tensor_scalar_add(
                            out=GS[l][j][:], in0=modT[l][:, sc0:sc0 + 8, :], scalar1=1.0))
                        A("dve", [GS[l][j], ngT], [GS[l][j]], lambda h, l=l, j=j: h.tensor_tensor(
                            out=GS[l][j][:], in0=GS[l][j][:],
                            in1=ngT[:, :, l * 2 + j:l * 2 + j + 1].to_broadcast([128, 8, 3]), op=ALU.mult))
                lq = fw.sb("lq", [128, 128], F32)
                lk = fw.sb("lk", [128, 128], F32)
                fw.load("sp", lq, lq[:], I["lam_q"].partition_broadcast(128))
                fw.load("sp", lk, lk[:], I["lam_k"].partition_broadcast(128))
                A("dve", [lq, lk], [lq], lambda h: h.tensor_mul(out=lq[:], in0=lq[:], in1=lk[:]))
                A("dve", [lq], [lam], lambda h: h.tensor_reduce(
                    out=lam[:, 1:3], in_=lq[:].rearrange("p (a b) -> p a b", a=2), axis=AX.X, op=ALU.add))
                A("act", [lam], [lam], lambda h: h.activation(out=lam[:, 1:3], in_=lam[:, 1:3], func=AF.Exp))
                lam_init0 = 0.8 - 0.6 * math.exp(-0.3 * 0)
                A("dve", [lam], [lam], lambda h: h.tensor_sub(out=lam[:, 0:1], in0=lam[:, 2:3], in1=lam[:, 1:2]))
                A("dve", [lam], [lam], lambda h: h.tensor_scalar_add(out=lam[:, 0:1], in0=lam[:, 0:1],
                                                                     scalar1=-lam_init0))
                t5t = fw.sb("t5t", [32, 4], F32)
                t5o = fw.sb("t5o", [32, 384], F32)
                t5r = fw.sb("t5r", [4, 384], F32)
                fw.load("sp", t5t, t5t[:], I["t5_table"])
                fw.load("sp", t5o, t5o[:], I["t5oh"])
                pp = psf.next()
                A("pe", [t5t, t5o], [pp], lambda h, pp=pp: h.matmul(pp[0:4, 0:384], t5t[:], t5o[:],
                                                                      start=True, stop=True))
                A("act", [pp], [t5r], lambda h, pp=pp: h.activation(out=t5r[:], in_=pp[0:4, 0:384], func=AF.Copy))
                s5k = Trk("s5d")
                fw.store("sp", t5r, s5d, t5r[:], dk=s5k)
                rlt = fw.sb("rlt", [128, 3, 8], F32)
                rlr = fw.sb("rlr", [8, 384], F32)
                fw.load("sp", rlt, rlt[:, 0:2, :], I["rel_table"][0:256, :].rearrange("(c p) h -> p c h", p=128))
                fw.load("sp", rlt, rlt[0:1, 2, :], I["rel_table"][256:257, :])
                pp = psf.next()
                def f(h, pp=pp):
                    h.matmul(pp[0:8, 0:128], rlt[:, 0, :], idf[:, :], start=True, stop=True)
                    h.matmul(pp[0:8, 128:256], rlt[:, 1, :], idf[:, :], start=True, stop=True)
                    return h.matmul(pp[0:8, 256:257], rlt[0:1, 2, :], idf[0:1, 0:1], start=True, stop=True)
                A("pe", [rlt, idf], [pp], f)
                A("act", [pp], [rlr], lambda h, pp=pp: h.activation(out=rlr[:, 0:257], in_=pp[0:8, 0:257],
                                                                     func=AF.Copy))
                A("dve", [rlr], [rlr], lambda h: h.tensor_copy(out=rlr[:, 257:384],
                                                               in_=rlr[:, 256:257].to_broadcast([8, 127])))
                srk = Trk("sreld")
                fw.store("sp", rlr, sreld, rlr[:], dk=srk)
                fw.barrier()
            fw.release_phase()

            def bias_tiles(ph_name, dst, nh, rows_ap, rk, offs, first, cst, mn):
                hk = Rot([fw.sb(f"hk{ph_name}{i}", [128, 128], F32) for i in range(2)])
                ex = fw.sb(f"ex{ph_name}", [128, 128], F32)
                fw.load("sp", ex, ex[:], I["exch"])
                for h_ in range(nh):
                    for d_, off in enumerate(offs):
                        hkt = hk.next()
                        src = bass.AP(rows_ap.tensor, h_ * 384 + off, [[1, 128], [1, 128]])
                        fw.load("sp", hkt, hkt[:], src, dk=rk)
                        pp = psf.next()
                        if first:
                            A("pe", [hkt, ex], [pp], lambda h, pp=pp, hkt=hkt: h.matmul(
                                pp[:, 0:128], hkt[:], ex[:], start=True, stop=True))
                        else:
                            A("pe", [hkt, ex], [pp], lambda h, pp=pp, hkt=hkt: h.matmul(
                                pp[:, 0:128], ex[:], hkt[:], start=True, stop=True))
                        m_ = mn[d_]
                        if m_ is None:
                            A("dve", [pp, cst], [dst], lambda h, pp=pp, h_=h_, d_=d_: h.tensor_scalar(
                                out=dst[:, h_, d_, :], in0=pp[:, 0:128], scalar1=cst[:, h_:h_ + 1], scalar2=None,
                                op0=ALU.subtract))
                        else:
                            A("dve", [pp, cst, m_], [dst], lambda h, pp=pp, h_=h_, d_=d_, m_=m_: h.scalar_tensor_tensor(
                                out=dst[:, h_, d_, :], in0=pp[:, 0:128], scalar=cst[:, h_:h_ + 1], in1=m_[:],
                                op0=ALU.subtract, op1=ALU.add))

            def rmsnorm_to_hT(xt, nt, hT, gs, sh, s, scr, st4):
                A("act", [xt], [scr, st4], lambda h: h.activation(out=scr[:nt, :], in_=xt[:nt, :], func=AF.Square,
                                                                 accum_out=st4[:nt, 0:1]))
                A("act", [st4], [st4], lambda h: h.activation(out=st4[:nt, 1:2], in_=st4[:nt, 0:1], func=AF.Sqrt,
                                                              scale=1.0 / D, bias=EPS))
                A("dve", [st4], [st4], lambda h: h.reciprocal(out=st4[:nt, 2:3], in_=st4[:nt, 1:2]))
                A("dve", [xt, st4], [scr], lambda h: h.tensor_scalar(out=scr[:nt, :], in0=xt[:nt, :],
                                                                    scalar1=st4[:nt, 2:3], scalar2=None, op0=ALU.mult))
                pt = ptb.next()
                def f(h):
                    for c in range(8):
                        ins = h.transpose(out=pt[:, c * 128:c * 128 + nt], in_=scr[:nt, c * 128:(c + 1) * 128],
                                          identity=idb[:nt, :nt])
                    return ins
                A("pe", [scr, idb], [pt], f)
                for c in range(8):
                    A("act", [pt, gs, sh], [hT], lambda h, c=c: h.activation(
                        out=hT[:, c, :nt], in_=pt[:, c * 128:c * 128 + nt], func=AF.Identity,
                        scale=gs[:, c, s:s + 1], bias=sh[:, c, s:s + 1]))

            def transpose_blocks(src, nt, nblk, dst_fn, rd_extra=()):
                pt = ptb.next()
                def f(h):
                    for c in range(nblk):
                        ins = h.transpose(out=pt[:, c * 128:c * 128 + nt], in_=src[:nt, c * 128:(c + 1) * 128],
                                          identity=idb[:nt, :nt])
                    return ins
                A("pe", [src, idb], [pt], f)
                return pt

            with ExitStack() as ph:
              if 'A' in PH:
                fw.ts = ph
                wIn = fw.sb("wIn", [128, 8, 2560], BF16)
                w32 = fw.sb("w32", [128, 8, 1024], F32)
                wOut = fw.sb("wOut", [128, 8, D], BF16)
                for kc in range(8):
                    fw.load("sp", w32, w32[:, kc, :], I["w_in_ab"][kc * 128:(kc + 1) * 128, 0:1024])
                for kc in range(8):
                    fw.load("pool", wIn, wIn[:, kc, :], I["w_in_ab"][kc * 128:(kc + 1) * 128, 1024:3584])
                for kc in range(8):
                    fw.load("pool", wOut, wOut[:, kc, :], I["w_out_ab"][kc * 128:(kc + 1) * 128, :])
                kbT = fw.sb("kbT", [128, 4, 2176], BF16)
                Vc = fw.sb("Vc", [128, 17, 4, 130], BF16)
                bT5 = fw.sb("bT5", [128, 4, 2, 128], BF16)
                c15 = fw.sb("c15", [128, 4], F32)
                mn0 = fw.sb("mn0", [128, 128], F32)
                rmask = fw.sb("rmask", [128, 128], F32)
                gdec = fw.sb("gdecs", [128, 2, 512], F32)
                rgn = fw.sb("rgn", [128, 512], F32)
                dgn = fw.sb("dgn", [128, 128], F32)
                G1 = fw.sb("G1", [128, D], F32)
                fw.load("sp", c15, c15[:], I["t5_table"][15:16, :].partition_broadcast(128))
                fw.load("sp", mn0, mn0[:], I["mneg"][0])
                fw.load("sp", rmask, rmask[:], I["retmask"])
                fw.load("sp", gdec, gdec[:], I["gdec"].rearrange("a p n -> p a n"))
                fw.load("sp", rgn, rgn[:], I["ret_gn"].partition_broadcast(128))
                fw.load("sp", dgn, dgn[:], I["diff_gn"].partition_broadcast(128))
                A("dve", [dgn], [dgn], lambda h: h.tensor_scalar(out=dgn[:], in0=dgn[:], scalar1=1.0 - lam_init0,
                                                                scalar2=None, op0=ALU.mult))
                A("dve", [], [Vc], lambda h: h.memset(Vc[:, :, :, 128:130], 1.0))
                bias_tiles("t5", bT5, 4, s5d, s5k, [128, 0], True, c15, [mn0, None])

                xrot = Rot([fw.sb(f"xa{i}", [128, D], F32) for i in range(2)])
                rrot = Rot([fw.sb(f"rp{i}", [128, 4, 4, 64], F32) for i in range(1)])
                st4 = fw.sb("st4", [128, 4], F32)
                hT = fw.sb("hT", [128, 8, 128], BF16)
                hT32 = fw.sb("hT32", [128, 8, 128], F32)
                qaT32 = CView(hT32, 0)
                kaT32 = CView(hT32, 4)
                ev = fw.sb("ev", [128, 512], F32)
                rt = [fw.sb(f"rt{i}", [128, 4, 64], F32) for i in range(4)]
                qrot = fw.sb("qrot", [128, 512], BF16)
                krot = fw.sb("krot", [128, 512], BF16)
                va = fw.sb("va", [128, 512], BF16)
                sg = fw.sb("sg", [128, 512], F32)
                sge = fw.sb("sge", [128, 512], F32)
                qb = fw.sb("qb", [128, 512], BF16)
                kbf = Rot([fw.sb(f"kbf{i}", [128, 512], F32) for i in range(1)])
                vbf = Rot([fw.sb(f"vbf{i}", [128, 512], F32) for i in range(1)])
                kb16 = fw.sb("kb16", [128, 512], BF16)
                qaT = fw.sb("qaT", [128, 4, 128], BF16)
                qbT = fw.sb("qbT", [128, 4, 128], BF16)
                PT = fw.sb("PT", [128, 4, 128], BF16)
                stf = fw.sb("stf", [128, 512], F32)
                stb = fw.sb("stb", [128, 512], BF16)
                gst = fw.sb("gst", [128, 16], F32)
                gtmp = fw.sb("gtmp", [128, 512], F32)
                gsq = fw.sb("gsq", [128, 512], F32)
                mix = fw.sb("mix", [128, D], BF16)
                pTt = Rot([fw.sb(f"pTt{i}", [128, 4, 128], BF16) for i in range(2)])
                ob = fw.sb("ob", [128, 4, 128], F32)
                rc = fw.sb("rc", [128, 4], F32)
                t1 = fw.sb("t1", [128, 128], F32)
                mT = fw.sb("mT", [128, 8, 128], BF16)
                xo = Rot([fw.sb(f"xo{i}", [128, D], F32) for i in range(1)])
                ot = fw.sb("ot", [128, D], F32)
                xn32 = ot
                q32 = gtmp
                k32 = gsq
                scr = mix

                for s, (ntok, rp0, samp, row0) in enumerate(SEQS):
                    if str(s) not in SEQSEL:
                        continue
                    ntl = min(MAXT, (ntok + 127) // 128)
                    fw.load("sp", G1, G1[:], modd[s:s + 1, 0, 16 * 128:24 * 128].partition_broadcast(128), dk=mk)
                    if samp:
                        fw.load("sp", stf, stf[:].rearrange("p (h e) -> p h e", h=4),
                                I["ret0"].rearrange("h d e -> d h e"))
                        A("dve", [], [kbT], lambda h: h.memset(kbT[:, :, 1088:1152], 0.0))
                        A("dve", [], [Vc], lambda h: h.memset(Vc[64:128, 8, :, 0:128], 0.0))
                        for j in range(8):
                            kf = kbf.next()
                            fw.load("sp", kf, kf[:], I["bk_c"][j * 128:(j + 1) * 128, :])
                            A("dve", [kf], [kb16], lambda h, kf=kf: h.tensor_copy(out=kb16[:], in_=kf[:]))
                            pt = transpose_blocks(kb16, 128, 4, None)
                            A("act", [pt], [kbT], lambda h, pt=pt, j=j: h.activation(
                                out=kbT[:, :, j * 128:(j + 1) * 128],
                                in_=pt[:, 0:512].rearrange("p (c t) -> p c t", c=4), func=AF.Copy))
                            vf = vbf.next()
                            fw.load("sp", vf, vf[:], I["bv_c"][j * 128:(j + 1) * 128, :])
                            A("dve", [vf], [Vc], lambda h, vf=vf, j=j: h.tensor_copy(
                                out=Vc[:, j, :, 0:128], in_=vf[:].rearrange("p (h e) -> p h e", h=4)))
                    else:
                        A("dve", [], [stf], lambda h: h.memset(stf[:], 0.0))
                    A("act", [stf], [stb], lambda h: h.activation(out=stb[:], in_=stf[:], func=AF.Copy))
                    for ti in range(ntl):
                        nt = min(128, ntok - ti * 128)
                        r0 = row0 + ti * 128
                        qi = 8 if samp else ti
                        gdi = 1 if samp else 0
                        src = I["xsm"] if samp else I["xp"]
                        sr0 = 0 if samp else r0

                        def ldxa(tj):
                            ntj = min(128, ntok - tj * 128)
                            sj = 0 if samp else row0 + tj * 128
                            xj = xrot.next()
                            fw.load("sp", xj, xj[:ntj, :], src[sj:sj + ntj, :])
                            return xj
                        if ti == 0:
                            xnext = ldxa(0)
                        xt = xnext
                        if ti + 1 < ntl:
                            xnext = ldxa(ti + 1)
                        rp = rrot.next()
                        fw.load("sp", rp, rp[:], I["rope"][rp0 + ti])
                        A("act", [xt], [scr, st4], lambda h: h.activation(out=scr[:nt, :], in_=xt[:nt, :], func=AF.Square,
                                                                         accum_out=st4[:nt, 0:1]))
                        A("act", [st4], [st4], lambda h: h.activation(out=st4[:nt, 1:2], in_=st4[:nt, 0:1], func=AF.Sqrt,
                                                                      scale=1.0 / D, bias=EPS))
                        A("dve", [st4], [st4], lambda h: h.reciprocal(out=st4[:nt, 2:3], in_=st4[:nt, 1:2]))
                        A("dve", [xt, st4], [xn32], lambda h: h.tensor_scalar(out=xn32[:nt, :], in0=xt[:nt, :],
                                                                             scalar1=st4[:nt, 2:3], scalar2=None, op0=ALU.mult))
                        for half in range(2):
                            pp = psf.next()
                            def f(h, pp=pp, half=half):
                                for cc in range(4):
                                    c = half * 4 + cc
                                    ins = h.matmul(pp[:, cc * 128:cc * 128 + nt], xn32[:nt, c * 128:(c + 1) * 128],
                                                   idf[:nt, :nt], start=True, stop=True)
                                return ins
                            A("pe", [xn32, idf], [pp], f)
                            for cc in range(4):
                                c = half * 4 + cc
                                A("act", [pp, GS[0][0], modT[0]], [hT32], lambda h, pp=pp, c=c, cc=cc: h.activation(
                                    out=hT32[:, c, :nt], in_=pp[:, cc * 128:cc * 128 + nt], func=AF.Identity,
                                    scale=GS[0][0][:, c, s:s + 1], bias=modT[0][:, c, s:s + 1]))
                        A("dve", [hT32], [hT], lambda h: h.tensor_copy(out=hT[:, :, :nt], in_=hT32[:, :, :nt]))
                        def proj(blk):
                            pp = psf.next()
                            def f(h):
                                for c in range(8):
                                    if blk < 2:
                                        ins = h.matmul(pp[:nt, :], hT32[:, c, :nt], w32[:, c, blk * 512:(blk + 1) * 512],
                                                       start=(c == 0), stop=(c == 7))
                                    else:
                                        ins = h.matmul(pp[:nt, :], hT[:, c, :nt], wIn[:, c, (blk - 2) * 512:(blk - 1) * 512],
                                                       start=(c == 0), stop=(c == 7))
                                return ins
                            A("pe", [hT, hT32, wIn, w32], [pp], f)
                            return pp

                        def rotary(pp, ci, out, eng):
                            A("act", [pp], [ev], lambda h: h.activation(out=ev[:nt, :], in_=pp[:nt, :], func=AF.Copy))
                            x1 = ev[:nt, :].rearrange("p (h e) -> p h e", h=4)[:, :, 0:64]
                            x2 = ev[:nt, :].rearrange("p (h e) -> p h e", h=4)[:, :, 64:128]
                            cs_ = rp[:nt, ci]
                            sn_ = rp[:nt, ci + 1]
                            out32 = q32 if ci == 0 else k32
                            ov = out32[:nt, :].rearrange("p (h e) -> p h e", h=4)
                            A(eng, [ev, rp], [rt[0]], lambda h: h.tensor_tensor(out=rt[0][:nt], in0=x1, in1=cs_, op=ALU.mult))
                            A(eng, [ev, rp], [rt[1]], lambda h: h.tensor_tensor(out=rt[1][:nt], in0=x2, in1=sn_, op=ALU.mult))
                            A(eng, [ev, rp], [rt[2]], lambda h: h.tensor_tensor(out=rt[2][:nt], in0=x1, in1=sn_, op=ALU.mult))
                            A(eng, [ev, rp], [rt[3]], lambda h: h.tensor_tensor(out=rt[3][:nt], in0=x2, in1=cs_, op=ALU.mult))
                            A(eng, [rt[0], rt[1]], [out32], lambda h: h.tensor_tensor(
                                out=ov[:, :, 0:64], in0=rt[0][:nt], in1=rt[1][:nt], op=ALU.subtract))
                            A(eng, [rt[2], rt[3]], [out32], lambda h: h.tensor_tensor(
                                out=ov[:, :, 64:128], in0=rt[2][:nt], in1=rt[3][:nt], op=ALU.add))
                            A("dve", [out32], [out], lambda h: h.tensor_copy(out=out[:nt, :], in_=out32[:nt, :]))

                        pp = proj(0)
                        rotary(pp, 0, qrot, "dve")
                        pp = proj(1)
                        rotary(pp, 2, krot, "dve")
                        pp = proj(2)
                        A("act", [pp], [va], lambda h, pp=pp: h.activation(out=va[:nt, :], in_=pp[:nt, :], func=AF.Copy))
                        pp = proj(3)
                        A("act", [pp], [sge], lambda h, pp=pp: h.activation(out=sge[:nt, :], in_=pp[:nt, :], func=AF.Sigmoid))
                        A("dve", [pp, sge], [sg], lambda h, pp=pp: h.tensor_tensor(out=sg[:nt, :], in0=pp[:nt, :],
                                                                                 in1=sge[:nt, :], op=ALU.mult))
                        pp = proj(4)
                        A("act", [pp], [qb], lambda h, pp=pp: h.activation(out=qb[:nt, :], in_=pp[:nt, :], func=AF.Copy,
                                                                         scale=0.125))
                        pp = proj(5)
                        kf = kbf.next()
                        A("act", [pp], [kf], lambda h, pp=pp: h.activation(out=kf[:nt, :], in_=pp[:nt, :], func=AF.Copy))
                        A("dve", [pp], [kb16], lambda h, pp=pp: h.tensor_copy(out=kb16[:nt, :], in_=pp[:nt, :]))
                        fw.store("sp", kf, (O["bk_s"] if samp else O["bk_p"])[sr0:sr0 + nt, :], kf[:nt, :])
                        pp = proj(6)
                        vf = vbf.next()
                        A("act", [pp], [vf], lambda h, pp=pp: h.activation(out=vf[:nt, :], in_=pp[:nt, :], func=AF.Copy))
                        A("dve", [pp], [Vc], lambda h, pp=pp: h.tensor_copy(
                            out=Vc[:nt, qi, :, 0:128], in_=pp[:nt, :].rearrange("p (h e) -> p h e", h=4)))
                        fw.store("sp", vf, (O["bv_s"] if samp else O["bv_p"])[sr0:sr0 + nt, :], vf[:nt, :])
                        for src32, dst32 in ((q32, qaT32), (k32, kaT32)):
                            pp = psf.next()
                            def f(h, pp=pp, src32=src32):
                                for hh in range(4):
                                    ins = h.matmul(pp[:, hh * 128:hh * 128 + nt], src32[:nt, hh * 128:(hh + 1) * 128],
                                                   idf[:nt, :nt], start=True, stop=True)
                                return ins
                            A("pe", [src32, idf], [pp], f)
                            A("act", [pp], [dst32], lambda h, pp=pp, dst32=dst32: h.activation(
                                out=dst32[:, :, :nt], in_=pp[:, :].rearrange("p (c t) -> p c t", c=4)[:, :, :nt],
                                func=AF.Copy))
                        for srcT, dstT in ((qrot, qaT), (qb, qbT)):
                            pt = transpose_blocks(srcT, nt, 4, None)
                            A("act", [pt], [dstT], lambda h, pt=pt, dstT=dstT: h.activation(
                                out=dstT[:, :, :nt], in_=pt[:, 0:512].rearrange("p (c t) -> p c t", c=4)[:, :, :nt],
                                func=AF.Copy))
                        pt = transpose_blocks(kb16, nt, 4, None)
                        A("act", [pt], [kbT], lambda h, pt=pt: h.activation(
                            out=kbT[:, :, qi * 128:qi * 128 + nt],
                            in_=pt[:, 0:512].rearrange("p (c t) -> p c t", c=4)[:, :, :nt], func=AF.Copy))
                        pS = psf.next()
                        def f(h):
                            for hh in range(4):
                                ins = h.matmul(pS[:nt, hh * 128:hh * 128 + nt], kaT32[:, hh, :nt], qaT32[:, hh, :nt],
                                               start=True, stop=True)
                            return ins
                        A("pe", [kaT32, qaT32], [pS], f)
                        A("dve", [pS, rmask], [PT], lambda h: h.tensor_tensor(
                            out=PT[:nt, :, :nt], in0=pS[:nt, :].rearrange("p (h t) -> p h t", h=4)[:, :, :nt],
                            in1=rmask[:nt, :nt].unsqueeze(1).to_broadcast([nt, 4, nt]), op=ALU.mult))
                        pD = psf.next()
                        def f(h):
                            for hh in range(4):
                                ins = h.matmul(pD[:, hh * 128:(hh + 1) * 128], krot[:nt, hh * 128:(hh + 1) * 128],
                                               va[:nt, hh * 128:(hh + 1) * 128], start=True, stop=True)
                            return ins
                        A("pe", [krot, va], [pD], f)
                        pO = psf.next()
                        def f(h):
                            for hh in range(4):
                                h.matmul(pO[:nt, hh * 128:(hh + 1) * 128], PT[:nt, hh, :nt], va[:nt, hh * 128:(hh + 1) * 128],
                                         start=True, stop=False)
                                ins = h.matmul(pO[:nt, hh * 128:(hh + 1) * 128], qaT[:, hh, :nt],
                                               stb[:, hh * 128:(hh + 1) * 128], start=False, stop=True)
                            return ins
                        A("pe", [PT, va, qaT, stb], [pO], f)
                        A("dve", [pD, stf], [stf], lambda h: h.tensor_tensor(out=stf[:], in0=pD[:], in1=stf[:], op=ALU.add))
                        A("dve", [stf, gdec], [stf], lambda h: h.tensor_tensor(out=stf[:], in0=stf[:], in1=gdec[:, gdi, :],
                                                                               op=ALU.mult))
                        A("act", [stf], [stb], lambda h: h.activation(out=stb[:], in_=stf[:], func=AF.Copy))
                        pOv = pO[:nt, :].rearrange("p (h e) -> p h e", h=4)
                        A("dve", [pO], [gst], lambda h: h.tensor_reduce(out=gst[:nt, 0:4], in_=pOv, axis=AX.X, op=ALU.add))
                        A("act", [pO], [gsq], lambda h: h.activation(out=gsq[:nt, :], in_=pO[:nt, :], func=AF.Square))
                        A("dve", [gsq], [gst], lambda h: h.tensor_reduce(
                            out=gst[:nt, 4:8], in_=gsq[:nt, :].rearrange("p (h e) -> p h e", h=4), axis=AX.X, op=ALU.add))
                        A("dve", [gst], [gst], lambda h: h.tensor_scalar(out=gst[:nt, 0:4], in0=gst[:nt, 0:4],
                                                                        scalar1=1.0 / 128, scalar2=None, op0=ALU.mult))
                        A("dve", [gst], [gst], lambda h: h.tensor_tensor(out=gst[:nt, 8:12], in0=gst[:nt, 0:4],
                                                                        in1=gst[:nt, 0:4], op=ALU.mult))
                        A("dve", [gst], [gst], lambda h: h.scalar_tensor_tensor(
                            out=gst[:nt, 4:8], in0=gst[:nt, 4:8], scalar=1.0 / 128, in1=gst[:nt, 8:12],
                            op0=ALU.mult, op1=ALU.subtract))
                        A("act", [gst], [gst], lambda h: h.activation(out=gst[:nt, 4:8], in_=gst[:nt, 4:8], func=AF.Sqrt,
                                                                      bias=EPS))
                        A("dve", [gst], [gst], lambda h: h.reciprocal(out=gst[:nt, 4:8], in_=gst[:nt, 4:8]))
                        gv = gtmp[:nt, :].rearrange("p (h e) -> p h e", h=4)
                        A("dve", [pO, gst], [gtmp], lambda h: h.tensor_tensor(
                            out=gv, in0=pOv, in1=gst[:nt, 0:4].unsqueeze(2).to_broadcast([nt, 4, 128]), op=ALU.subtract))
                        A("dve", [gtmp, gst], [gtmp], lambda h: h.tensor_tensor(
                            out=gv, in0=gv, in1=gst[:nt, 4:8].unsqueeze(2).to_broadcast([nt, 4, 128]), op=ALU.mult))
                        A("dve", [gtmp, rgn], [gtmp], lambda h: h.tensor_tensor(out=gtmp[:nt, :], in0=gtmp[:nt, :],
                                                                                in1=rgn[:nt, :], op=ALU.mult))
                        A("dve", [gtmp, sg], [mix], lambda h: h.tensor_tensor(out=mix[:nt, 0:512], in0=gtmp[:nt, :],
                                                                             in1=sg[:nt, :], op=ALU.mult))
                        groups = []
                        j = 0
                        while j <= qi:
                            if samp and j == 8:
                                groups.append([8])
                                j += 1
                            else:
                                hi = min(j + 4, (8 if samp else qi + 1))
                                groups.append(list(range(j, hi)))
                                j = hi
                        pas = {}
                        def qk_a(hh, i_, g):
                            nk = 128
                            pS = psf.next()
                            def f(h):
                                for jj, j_ in enumerate(g):
                                    dl = j_ - qi
                                    near = dl >= -1
                                    ins = h.matmul(pS[:nk, jj * 128:jj * 128 + nt],
                                                   kbT[i_ * 64:(i_ + 1) * 64, hh, j_ * 128:j_ * 128 + nk],
                                                   qbT[i_ * 64:(i_ + 1) * 64, hh, :nt], start=True, stop=not near)
                                    if near:
                                        ins = h.matmul(pS[:nk, jj * 128:jj * 128 + nt], idb[:nk, :nk],
                                                       bT5[:nk, hh, -dl, :nt], start=False, stop=True)
                                return ins
                            A("pe", [kbT, qbT, idb, bT5], [pS], f)
                            pt_ = pTt.next()
                            ng = len(g)
                            A("act", [pS, c15], [pt_], lambda h: h.activation(
                                out=pt_[:nk, 0:ng, :nt],
                                in_=pS[:nk, 0:ng * 128].rearrange("p (g t) -> p g t", g=ng)[:, :, :nt],
                                func=AF.Exp, bias=c15[:nk, hh:hh + 1]))
                            return pt_

                        def pv_a(hh, i_, g, pt_):
                            nk = 128
                            if hh not in pas:
                                pas[hh] = pacc.next()
                            pa = pas[hh]
                            def f(h):
                                for jj, j_ in enumerate(g):
                                    ins = h.matmul(pa[:nt, i_ * 256:i_ * 256 + 129], pt_[:nk, jj, :nt],
                                                   Vc[:nk, j_, hh, 0:129], start=(j_ == 0), stop=(j_ == qi))
                                return ins
                            A("pe", [pt_, Vc], [pa], f)
                            if i_ == 1 and g is groups[-1]:
                                pav = pa[:nt, 0:512].rearrange("p (i e) -> p i e", i=2)
                                A("dve", [pa], [rc], lambda h: h.reciprocal(out=rc[:nt, 0:2], in_=pav[:, :, 128]))
                                A("dve", [rc, lam], [rc], lambda h: h.tensor_tensor(out=rc[:nt, 2:3], in0=rc[:nt, 1:2],
                                                                                   in1=lam[:nt, 0:1], op=ALU.mult))
                                A("dve", [pa, rc], [t1], lambda h: h.tensor_scalar(
                                    out=t1[:nt, :], in0=pa[:nt, 256:384], scalar1=rc[:nt, 2:3], scalar2=None, op0=ALU.mult))
                                A("dve", [pa, rc, t1], [ob], lambda h: h.scalar_tensor_tensor(
                                    out=ob[:nt, hh, :], in0=pa[:nt, 0:128], scalar=rc[:nt, 0:1], in1=t1[:nt, :],
                                    op0=ALU.mult, op1=ALU.add))

                        items = [(hh, i_, g) for hh in range(4) for i_ in range(2) for g in groups]
                        prev = None
                        for it in items:
                            cur = qk_a(*it)
                            if prev is not None:
                                pv_a(*prev)
                            prev = it + (cur,)
                        pv_a(*prev)
                        obf = ob[:nt].rearrange("p h e -> p (h e)")
                        A("act", [ob], [gsq], lambda h: h.activation(out=gsq[:nt, :], in_=obf, func=AF.Square))
                        A("dve", [gsq], [gst], lambda h: h.tensor_reduce(
                            out=gst[:nt, 12:16], in_=gsq[:nt, :].rearrange("p (h e) -> p h e", h=4), axis=AX.X, op=ALU.add))
                        A("act", [gst], [gst], lambda h: h.activation(out=gst[:nt, 12:16], in_=gst[:nt, 12:16], func=AF.Sqrt,
                                                                      scale=1.0 / 128, bias=EPS))
                        A("dve", [gst], [gst], lambda h: h.reciprocal(out=gst[:nt, 12:16], in_=gst[:nt, 12:16]))
                        A("dve", [ob, gst], [gtmp], lambda h: h.tensor_tensor(
                            out=gv, in0=ob[:nt], in1=gst[:nt, 12:16].unsqueeze(2).to_broadcast([nt, 4, 128]), op=ALU.mult))
                        A("dve", [gtmp, dgn], [mix], lambda h: h.tensor_tensor(
                            out=mix[:nt, 512:1024].rearrange("p (h e) -> p h e", h=4), in0=gv,
                            in1=dgn[:nt, :].unsqueeze(1).to_broadcast([nt, 4, 128]), op=ALU.mult))
                        pt = transpose_blocks(mix, nt, 8, None)
                        A("act", [pt], [mT], lambda h, pt=pt: h.activation(
                            out=mT[:, :, :nt], in_=pt[:, :].rearrange("p (c t) -> p c t", c=8)[:, :, :nt], func=AF.Copy))
                        xn_ = xo.next()
                        for blk in range(2):
                            pp = psf.next()
                            def f(h, pp=pp, blk=blk):
                                for c in range(8):
                                    ins = h.matmul(pp[:nt, :], mT[:, c, :nt], wOut[:, c, blk * 512:(blk + 1) * 512],
                                                   start=(c == 0), stop=(c == 7))
                                return ins
                            A("pe", [mT, wOut], [pp], f)
                            A("dve", [pp, G1], [ot], lambda h, pp=pp, blk=blk: h.tensor_tensor(
                                out=ot[:nt, blk * 512:(blk + 1) * 512], in0=pp[:nt, :],
                                in1=G1[:nt, blk * 512:(blk + 1) * 512], op=ALU.mult))
                        A("dve", [ot, xt], [xn_], lambda h, xn_=xn_, xt=xt: h.tensor_tensor(
                            out=xn_[:nt, :], in0=ot[:nt, :], in1=xt[:nt, :], op=ALU.add))
                        fw.store("sp", xn_, xs[r0:r0 + nt, :], xn_[:nt, :], dk=xtrk(r0))
                    dst = O["ret_s"] if samp else O["ret_p"][s]
                    fw.store("sp", stf, dst.rearrange("h d e -> d h e"), stf[:].rearrange("p (h e) -> p h e", h=4))
                fw.barrier()
            fw.release_phase()

            def ffn_phase(l):
                with ExitStack() as ph:
                    fw.ts = ph
                    wUp = fw.sb("wUp", [128, 8, 2 * DFF], BF16)
                    wDn = fw.sb("wDn", [128, 22, D], BF16)
                    for kc in range(8):
                        fw.load("pool", wUp, wUp[:, kc, :], I["w_up"][l, kc * 128:(kc + 1) * 128, :])
                    for kc in range(22):
                        fw.load("pool", wDn, wDn[:, kc, :], I["w_down"][l, kc * 128:(kc + 1) * 128, :])
                    G2 = fw.sb("G2", [128, D], F32)
                    xrot = Rot([fw.sb(f"xf{i}", [128, D], F32) for i in range(3)])
                    scr = fw.sb("scrf", [128, D], BF16)
                    st4 = fw.sb("st4f", [128, 4], F32)
                    hT = fw.sb("hTf", [128, 8, 128], BF16)
                    gbs = [fw.sb(f"gb{i}", [128, 22, 130], F32) for i in range(2)]
                    cv = fw.sb("cv", [128, 6, 128], F32)
                    tA = fw.sb("tA", [128, 6, 128], F32)
                    tB = fw.sb("tB", [128, 6, 128], F32)
                    abs_ = [fw.sb(f"ab{i}", [128, 22, 128], BF16) for i in range(2)]
                    act = fw.sb("actT", [128, 22, 128], BF16)
                    xo = Rot([fw.sb(f"xof{i}", [128, D], F32) for i in range(1)])
                    fo = fw.sb("fo", [128, 22, 2], F32)
                    K0 = math.sqrt(2.0 / math.pi)
                    QB = [(0, 6), (6, 12), (12, 18), (18, 22)]
                    for s, (ntok, rp0, samp, row0) in enumerate(SEQS):
                        if str(s) not in SEQSEL:
                            continue
                        ntl = min(MAXT, (ntok + 127) // 128)
                        fw.load("sp", G2, G2[:], modd[s:s + 1, l, 40 * 128:48 * 128].partition_broadcast(128), dk=mk)
                        gb0 = gbs[0]
                        if samp:
                            for c in range(22):
                                fw.load("sp", gb0, gb0[:, c, 0:2],
                                        I["ffnc"][l, :, c * 128:(c + 1) * 128].rearrange("t p -> p t"),
                                        allow_slow_non_contiguous=True)
                        else:
                            A("dve", [], [gb0], lambda h: h.memset(gb0[:, :, 0:2], 0.0))
                        xts = {}

                        def ntk(ti):
                            return min(128, ntok - ti * 128)

                        def ldx(ti):
                            xt = xrot.next()
                            xts[ti] = xt
                            r0 = row0 + ti * 128
                            fw.load("sp", xt, xt[:ntk(ti), :], xs[r0:r0 + ntk(ti), :], dk=xtrk(r0))

                        def up_pairs(ti, cps):
                            nt = ntk(ti)
                            gb, ab = gbs[ti % 2], abs_[ti % 2]
                            for cp in cps:
                                pg = psf.next()
                                def f(h, pg=pg, cp=cp):
                                    for q_ in range(4):
                                        c = 2 * cp + (q_ % 2)
                                        col = (DFF if q_ >= 2 else 0) + c * 128
                                        for kc in range(8):
                                            ins = h.matmul(pg[:, q_ * 128:q_ * 128 + nt], wUp[:, kc, col:col + 128],
                                                           hT[:, kc, :nt], start=(kc == 0), stop=(kc == 7))
                                    return ins
                                A("pe", [wUp, hT], [pg], f)
                                A("act", [pg], [gb], lambda h, pg=pg, cp=cp: h.activation(
                                    out=gb[:, 2 * cp:2 * cp + 2, 2:2 + nt],
                                    in_=pg[:, 256:512].rearrange("p (c t) -> p c t", c=2)[:, :, :nt], func=AF.Copy))
                                A("dve", [pg], [ab], lambda h, pg=pg, cp=cp: h.tensor_copy(
                                    out=ab[:, 2 * cp:2 * cp + 2, :nt],
                                    in_=pg[:, 0:256].rearrange("p (c t) -> p c t", c=2)[:, :, :nt]))

                        def carry(ti):
                            nt = ntk(ti)
                            gb, gbn = gbs[ti % 2], gbs[(ti + 1) % 2]
                            A("pool", [gb], [fo], lambda h: h.tensor_copy(out=fo[:, :, :], in_=gb[:, :, nt:nt + 2]))
                            A("pool", [fo], [gbn], lambda h: h.tensor_copy(out=gbn[:, :, 0:2], in_=fo[:, :, :]))

                        def ew(ti, q):
                            nt = ntk(ti)
                            gb, ab = gbs[ti % 2], abs_[ti % 2]
                            c0, c1 = QB[q]
                            n = c1 - c0
                            def wb(col):
                                return fcw[:, c0:c1, col:col + 1].to_broadcast([128, n, nt])
                            cvv, tAv, tBv = cv[:, 0:n, :nt], tA[:, 0:n, :nt], tB[:, 0:n, :nt]
                            A("dve", [gb, fcw], [cv], lambda h: h.tensor_tensor(out=cvv, in0=gb[:, c0:c1, 2:2 + nt],
                                                                              in1=wb(l * 3 + 2), op=ALU.mult))
                            A("dve", [gb, fcw], [tA], lambda h: h.tensor_tensor(out=tAv, in0=gb[:, c0:c1, 1:1 + nt],
                                                                              in1=wb(l * 3 + 1), op=ALU.mult))
                            A("dve", [cv, tA], [cv], lambda h: h.tensor_tensor(out=cvv, in0=cvv, in1=tAv, op=ALU.add))
                            A("dve", [gb, fcw], [tA], lambda h: h.tensor_tensor(out=tAv, in0=gb[:, c0:c1, 0:nt],
                                                                              in1=wb(l * 3 + 0), op=ALU.mult))
                            A("dve", [cv, tA], [cv], lambda h: h.tensor_tensor(out=cvv, in0=cvv, in1=tAv, op=ALU.add))
                            A("dve", [cv, fcw], [cv], lambda h: h.tensor_tensor(out=cvv, in0=cvv, in1=wb(6 + l), op=ALU.add))
                            A("act", [cv], [tA], lambda h: h.activation(out=tAv, in_=cvv, func=AF.Square))
                            A("act", [tA], [tA], lambda h: h.activation(out=tAv, in_=tAv, func=AF.Identity, scale=0.044715,
                                                                        bias=1.0))
                            A("dve", [tA, cv], [tA], lambda h: h.tensor_tensor(out=tAv, in0=tAv, in1=cvv, op=ALU.mult))
                            A("act", [tA], [tB], lambda h: h.activation(out=tBv, in_=tAv, func=AF.Sigmoid, scale=2.0 * K0))
                            A("dve", [tB, cv], [tB], lambda h: h.tensor_tensor(out=tBv, in0=tBv, in1=cvv, op=ALU.mult))
                            A("dve", [ab, tB], [act], lambda h: h.tensor_tensor(out=act[:, c0:c1, :nt], in0=ab[:, c0:c1, :nt],
                                                                               in1=tBv, op=ALU.mult))

                        def down(ti):
                            nt = ntk(ti)
                            r0 = row0 + ti * 128
                            xt = xts.pop(ti)
                            xn_ = xo.next()
                            for blk in range(2):
                                pp = psf.next()
                                def f(h, pp=pp, blk=blk):
                                    for c in range(22):
                                        ins = h.matmul(pp[:nt, :], act[:, c, :nt], wDn[:, c, blk * 512:(blk + 1) * 512],
                                                       start=(c == 0), stop=(c == 21))
                                    return ins
                                A("pe", [act, wDn], [pp], f)
                                A("dve", [pp, G2], [xn_], lambda h, pp=pp, blk=blk: h.tensor_tensor(
                                    out=xn_[:nt, blk * 512:(blk + 1) * 512], in0=pp[:nt, :],
                                    in1=G2[:nt, blk * 512:(blk + 1) * 512], op=ALU.mult))
                            A("dve", [xn_, xt], [xn_], lambda h, xn_=xn_, xt=xt: h.tensor_tensor(
                                out=xn_[:nt, :], in0=xn_[:nt, :], in1=xt[:nt, :], op=ALU.add))
                            fw.store("sp", xn_, xs[r0:r0 + nt, :], xn_[:nt, :], dk=xtrk(r0))

                        PAIRS = [(0, 1, 2), (3, 4, 5), (6, 7, 8), (9, 10)]
                        ldx(0)
                        if ntl > 1:
                            ldx(1)
                        _rms_ffn(xts[0], ntk(0), hT, l, s, scr, st4)
                        for q in range(4):
                            up_pairs(0, PAIRS[q])
                        carry(0)
                        for ti in range(ntl):
                            if ti + 2 < ntl:
                                ldx(ti + 2)
                            if ti + 1 < ntl:
                                _rms_ffn(xts[ti + 1], ntk(ti + 1), hT, l, s, scr, st4)
                            for q in range(4):
                                ew(ti, q)
                                if ti + 1 < ntl:
                                    up_pairs(ti + 1, PAIRS[q])
                            if ti + 1 < ntl:
                                carry(ti + 1)
                            down(ti)
                        dst = O["ffn_s"][l] if samp else O["ffn_p"][l, s]
                        for c in range(22):
                            fw.store("sp", fo, dst[:, c * 128:(c + 1) * 128].rearrange("t p -> p t"), fo[:, c, :],
                                     allow_slow_non_contiguous=True)
                    fw.barrier()
                fw.release_phase()

            def _rms_ffn(xt, nt, hT, l, s, scr, st4):
                class _V:
                    def __init__(self, t, c0):
                        self.t, self.c0, self.k = t, c0, t.k
                    def __getitem__(self, idx):
                        p, c, ss = idx
                        return self.t[p, self.c0 + c, ss]
                sh = _V(modT[l], 24)
                A("act", [xt], [scr, st4], lambda h: h.activation(out=scr[:nt, :], in_=xt[:nt, :], func=AF.Square,
                                                                 accum_out=st4[:nt, 0:1]))
                A("act", [st4], [st4], lambda h: h.activation(out=st4[:nt, 1:2], in_=st4[:nt, 0:1], func=AF.Sqrt,
                                                              scale=1.0 / D, bias=EPS))
                A("dve", [st4], [st4], lambda h: h.reciprocal(out=st4[:nt, 2:3], in_=st4[:nt, 1:2]))
                A("dve", [xt, st4], [scr], lambda h: h.tensor_scalar(out=scr[:nt, :], in0=xt[:nt, :],
                                                                    scalar1=st4[:nt, 2:3], scalar2=None, op0=ALU.mult))
                pt = ptb.next()
                def f(h):
                    for c in range(8):
                        ins = h.transpose(out=pt[:, c * 128:c * 128 + nt], in_=scr[:nt, c * 128:(c + 1) * 128],
                                          identity=idb[:nt, :nt])
                    return ins
                A("pe", [scr, idb], [pt], f)
                for c in range(8):
                    A("act", [pt, GS[l][1], modT[l]], [hT], lambda h, c=c: h.activation(
                        out=hT[:, c, :nt], in_=pt[:, c * 128:c * 128 + nt], func=AF.Identity,
                        scale=GS[l][1][:, c, s:s + 1], bias=modT[l][:, 24 + c, s:s + 1]))


            def cd_phase():
              with ExitStack() as ph:
                fw.ts = ph
                wIn = fw.sb("wIn2", [128, 8, 3200], BF16)
                wOut = fw.sb("wOut2", [128, 8, D], BF16)
                for kc in range(8):
                    fw.load("pool", wIn, wIn[:, kc, 0:3080], I["w_in_cd"][kc * 128:(kc + 1) * 128, :])
                for kc in range(8):
                    fw.load("pool", wOut, wOut[:, kc, :], I["w_out_cd"][kc * 128:(kc + 1) * 128, :])
                kcT = fw.sb("kcT", [128, 4, 2176], BF16)
                Vc = fw.sb("Vc2", [128, 8, 8, 96], BF16)
                bRel = fw.sb("bRel", [128, 8, 2, 128], BF16)
                bM4 = fw.sb("bM4", [128, 128], BF16)
                crel = fw.sb("crel", [128, 8], F32)
                mn0 = fw.sb("mn0c", [128, 128], F32)
                mn4 = fw.sb("mn4c", [128, 128], F32)
                rmask = fw.sb("rmaskc", [128, 128], F32)
                trigt = fw.sb("trigt", [128, 128], F32)
                ones = fw.sb("onesc", [128, 128], F32)
                dcr = fw.sb("dcr", [5, D], F32)
                dcw = fw.sb("dcw", [128, 8, 5], F32)
                dng = fw.sb("dng", [128, 512], F32)
                dsk = fw.sb("dsk", [128, 8], F32)
                dtb = fw.sb("dtb", [128, 8], F32)
                Aneg = fw.sb("Aneg", [128, 8], F32)
                G1 = fw.sb("G1c", [128, D], F32)
                fw.load("sp", crel, crel[:], I["rel_table"][256:257, :].partition_broadcast(128))
                fw.load("sp", mn0, mn0[:], I["mneg"][0])
                fw.load("sp", mn4, mn4[:], I["mneg"][1])
                fw.load("sp", rmask, rmask[:], I["retmask"])
                fw.load("sp", trigt, trigt[:], I["trigt"])
                fw.load("sp", dcr, dcr[0:4, :], I["d_conv_w"])
                fw.load("sp", dcr, dcr[4:5, :], I["d_conv_b"])
                fw.load("sp", dng, dng[:], I["d_norm_g"].partition_broadcast(128))
                fw.load("sp", dsk, dsk[:], I["d_skip"].partition_broadcast(128))
                fw.load("sp", dtb, dtb[:], I["d_dt_bias"].partition_broadcast(128))
                fw.load("sp", Aneg, Aneg[:], I["d_a_log"].partition_broadcast(128))
                A("act", [Aneg], [Aneg], lambda h: h.activation(out=Aneg[:], in_=Aneg[:], func=AF.Exp))
                A("dve", [Aneg], [Aneg], lambda h: h.tensor_scalar(out=Aneg[:], in0=Aneg[:], scalar1=-1.0, scalar2=None,
                                                                  op0=ALU.mult))
                A("dve", [], [ones], lambda h: h.memset(ones[:], 1.0))
                A("dve", [mn4], [bM4], lambda h: h.tensor_copy(out=bM4[:], in_=mn4[:]))
                A("dve", [], [Vc], lambda h: h.memset(Vc[:, :, :, 64:96], 1.0))
                rows_to_fm(dcr, 5, 8, dcw)
                bias_tiles("rel", bRel, 8, sreld, srk, [1, 129], False, crel, [mn0, None])

                xrot = Rot([fw.sb(f"xc{i}", [128, D], F32) for i in range(2)])
                scr = fw.sb("scrc", [128, D], BF16)
                st4 = fw.sb("st4c", [128, 4], F32)
                hT = fw.sb("hTc", [128, 8, 128], BF16)
                qc = fw.sb("qc", [128, 512], BF16)
                kc16 = fw.sb("kc16", [128, 512], BF16)
                kcf = Rot([fw.sb(f"kcf{i}", [128, 512], F32) for i in range(2)])
                vcf = Rot([fw.sb(f"vcf{i}", [128, 512], F32) for i in range(2)])
                qcT = fw.sb("qcT", [128, 4, 128], BF16)
                zs = fw.sb("zs", [128, 512], F32)
                ze = fw.sb("ze", [128, 512], F32)
                dts = fw.sb("dts", [128, 40], F32)
                xbuf = fw.sb("xbuf", [128, 8, 132], F32)
                xo3 = fw.sb("xo3", [128, 8, 3], F32)
                cvx = fw.sb("cvx", [128, 8, 128], F32)
                sle = fw.sb("sle", [128, 8, 128], F32)
                xbb = fw.sb("xbb", [128, 8, 128], BF16)
                xtok = fw.sb("xtok", [128, 768], BF16)
                xw = fw.sb("xw", [128, 512], BF16)
                Rm = fw.sb("Rm", [128, 8, 128], F32)
                Eh = fw.sb("Eh", [128, 8, 128], F32)
                cbm = fw.sb("cbm", [128, 2, 128], F32)
                WT = fw.sb("WT", [128, 8, 128], BF16)
                stT = fw.sb("stT", [128, 512], F32)
                stTb = fw.sb("stTb", [128, 512], BF16)
                sld = fw.sb("sld", [64, 8, 128], F32)
                yin = fw.sb("yin", [128, 512], F32)
                yt_ = fw.sb("ytc", [128, 512], F32)
                gsq = fw.sb("gsqc", [128, 512], F32)
                gst = fw.sb("gstc", [128, 8], F32)
                mix = fw.sb("mixc", [128, D], BF16)
                pTt = Rot([fw.sb(f"pTc{i}", [128, 4, 128], BF16) for i in range(3)])
                rc = fw.sb("rcc", [128, 4], F32)
                mT = fw.sb("mTc", [128, 8, 128], BF16)
                xo = Rot([fw.sb(f"xoc{i}", [128, D], F32) for i in range(2)])
                ot = fw.sb("otc", [128, D], F32)

                for s, (ntok, rp0, samp, row0) in enumerate(SEQS):
                    if str(s) not in SEQSEL:
                        continue
                    ntl = min(MAXT, (ntok + 127) // 128)
                    fw.load("sp", G1, G1[:], modd[s:s + 1, 1, 16 * 128:24 * 128].partition_broadcast(128), dk=mk)
                    if samp:
                        A("dve", [], [kcT], lambda h: h.memset(kcT[:, :, 1088:1152], 0.0))
                        A("dve", [], [Vc], lambda h: h.memset(Vc[64:128, 0, :, 0:64], 0.0))
                        for j in range(4):
                            kf = kcf.next()
                            fw.load("sp", kf, kf[:], I["ck_c"][j * 128:(j + 1) * 128, :])
                            A("dve", [kf], [kc16], lambda h, kf=kf: h.tensor_copy(out=kc16[:], in_=kf[:]))
                            pt = transpose_blocks(kc16, 128, 4, None)
                            A("act", [pt], [kcT], lambda h, pt=pt, j=j: h.activation(
                                out=kcT[:, :, (4 + j) * 128:(5 + j) * 128],
                                in_=pt[:, 0:512].rearrange("p (c t) -> p c t", c=4), func=AF.Copy))
                            vf = vcf.next()
                            fw.load("sp", vf, vf[:], I["cv_c"][j * 128:(j + 1) * 128, :])
                            A("dve", [vf], [Vc], lambda h, vf=vf, j=j: h.tensor_copy(
                                out=Vc[:, (4 + j) % 8, :, 0:64], in_=vf[:].rearrange("p (h e) -> p h e", h=8)))
                        for c in range(8):
                            fw.load("sp", xbuf, xbuf[:, c, 0:3],
                                    I["dconv_c"][:, c * 128:(c + 1) * 128].rearrange("t p -> p t"),
                                    allow_slow_non_contiguous=True)
                        fw.load("sp", sld, sld[:], I["dssm_c"].rearrange("h p n -> p h n"))
                        pp = psf.next()
                        def f(h, pp=pp):
                            for hh in range(8):
                                ins = h.matmul(pp[:, hh * 64:(hh + 1) * 64], sld[:, hh, :], idf[0:64, 0:64],
                                               start=True, stop=True)
                            return ins
                        A("pe", [sld, idf], [pp], f)
                        A("dve", [pp], [stT], lambda h, pp=pp: h.tensor_copy(out=stT[:], in_=pp[:]))
                    else:
                        A("dve", [], [stT], lambda h: h.memset(stT[:], 0.0))
                        A("dve", [], [xbuf], lambda h: h.memset(xbuf[:, :, 0:3], 0.0))
                    A("act", [stT], [stTb], lambda h: h.activation(out=stTb[:], in_=stT[:], func=AF.Copy))
                    for ti in range(ntl):
                        nt = min(128, ntok - ti * 128)
                        r0 = row0 + ti * 128
                        qi = 8 if samp else ti

                        def ldxc(tj):
                            ntj = min(128, ntok - tj * 128)
                            rj = row0 + tj * 128
                            xj = xrot.next()
                            fw.load("sp", xj, xj[:ntj, :], xs[rj:rj + ntj, :], dk=xtrk(rj))
                            return xj
                        if ti == 0:
                            xnext = ldxc(0)
                        xt = xnext
                        if ti + 1 < ntl:
                            xnext = ldxc(ti + 1)
                        rmsnorm_to_hT(xt, nt, hT, GS[1][0], modT[1], s, scr, st4)

                        def proj(c0, n, nrow=None):
                            pp = psf.next()
                            def f(h):
                                for c in range(8):
                                    ins = h.matmul(pp[:nt, 0:n], hT[:, c, :nt], wIn[:, c, c0:c0 + n],
                                                   start=(c == 0), stop=(c == 7))
                                return ins
                            A("pe", [hT, wIn], [pp], f)
                            return pp
                        pp = proj(0, 512)
                        A("act", [pp], [qc], lambda h, pp=pp: h.activation(out=qc[:nt, :], in_=pp[:nt, :], func=AF.Copy,
                                                                         scale=0.125))
                        pp = proj(512, 512)
                        kf = kcf.next()
                        A("act", [pp], [kf], lambda h, pp=pp, kf=kf: h.activation(out=kf[:nt, :], in_=pp[:nt, :], func=AF.Copy))
                        A("dve", [kf], [kc16], lambda h, kf=kf: h.tensor_copy(out=kc16[:nt, :], in_=kf[:nt, :]))
                        pp = proj(1024, 512)
                        vf = vcf.next()
                        A("act", [pp], [vf], lambda h, pp=pp, vf=vf: h.activation(out=vf[:nt, :], in_=pp[:nt, :], func=AF.Copy))
                        A("dve", [pp], [Vc], lambda h, pp=pp: h.tensor_copy(
                            out=Vc[:nt, qi % 8, :, 0:64], in_=pp[:nt, :].rearrange("p (h e) -> p h e", h=8)))
                        if samp:
                            fw.store("sp", kf, O["ck_s"][0:nt, :], kf[:nt, :])
                            fw.store("sp", vf, O["cv_s"][0:nt, :], vf[:nt, :])
                        elif ti >= 12:
                            fw.store("sp", kf, O["ck_p"][s, (ti - 12) * 128:(ti - 11) * 128, :], kf[:nt, :])
                            fw.store("sp", vf, O["cv_p"][s, (ti - 12) * 128:(ti - 11) * 128, :], vf[:nt, :])
                        pp = proj(1536, 512)
                        A("act", [pp], [ze], lambda h, pp=pp: h.activation(out=ze[:nt, :], in_=pp[:nt, :], func=AF.Sigmoid))
                        A("dve", [pp, ze], [zs], lambda h, pp=pp: h.tensor_tensor(out=zs[:nt, :], in0=pp[:nt, :], in1=ze[:nt, :],
                                                                                 op=ALU.mult))
                        pp = proj(3072, 8)
                        A("dve", [pp, dtb], [dts], lambda h, pp=pp: h.tensor_tensor(out=dts[:nt, 0:8], in0=pp[:nt, 0:8],
                                                                                   in1=dtb[:nt, :], op=ALU.add))
                        A("act", [dts], [dts], lambda h: h.activation(out=dts[:nt, 0:8], in_=dts[:nt, 0:8], func=AF.Exp))
                        A("act", [dts], [dts], lambda h: h.activation(out=dts[:nt, 0:8], in_=dts[:nt, 0:8], func=AF.Ln, bias=1.0))
                        A("dve", [dts, Aneg], [dts], lambda h: h.tensor_tensor(out=dts[:nt, 8:16], in0=dts[:nt, 0:8],
                                                                              in1=Aneg[:nt, :], op=ALU.mult))
                        for half in range(2):
                            pp = psf.next()
                            def f(h, pp=pp, half=half):
                                for cc in range(4):
                                    c = half * 4 + cc
                                    for kc in range(8):
                                        ins = h.matmul(pp[:, cc * 128:cc * 128 + nt],
                                                       wIn[:, kc, 2048 + c * 128:2048 + (c + 1) * 128], hT[:, kc, :nt],
                                                       start=(kc == 0), stop=(kc == 7))
                                return ins
                            A("pe", [wIn, hT], [pp], f)
                            A("act", [pp], [xbuf], lambda h, pp=pp, half=half: h.activation(
                                out=xbuf[:, half * 4:(half + 1) * 4, 3:3 + nt],
                                in_=pp[:, :].rearrange("p (c t) -> p c t", c=4)[:, :, :nt], func=AF.Copy))
                        def wb(col):
                            return dcw[:, :, col:col + 1].to_broadcast([128, 8, nt])
                        cvv, tAv = cvx[:, :, :nt], sle[:, :, :nt]
                        A("dve", [xbuf, dcw], [cvx], lambda h: h.tensor_tensor(out=cvv, in0=xbuf[:, :, 3:3 + nt], in1=wb(3),
                                                                            op=ALU.mult))
                        for tp in range(3):
                            A("dve", [xbuf, dcw], [sle], lambda h, tp=tp: h.tensor_tensor(out=tAv, in0=xbuf[:, :, tp:tp + nt],
                                                                                          in1=wb(tp), op=ALU.mult))
                            A("dve", [cvx, sle], [cvx], lambda h: h.tensor_tensor(out=cvv, in0=cvv, in1=tAv, op=ALU.add))
                        A("pool", [cvx, dcw], [cvx], lambda h: h.tensor_tensor(out=cvv, in0=cvv, in1=wb(4), op=ALU.add))
                        A("pool", [xbuf], [xo3], lambda h: h.tensor_copy(out=xo3[:], in_=xbuf[:, :, nt:nt + 3]))
                        A("pool", [xo3], [xbuf], lambda h: h.tensor_copy(out=xbuf[:, :, 0:3], in_=xo3[:]))
                        A("act", [cvx], [sle], lambda h: h.activation(out=sle[:, :, :nt], in_=cvx[:, :, :nt], func=AF.Sigmoid))
                        A("dve", [sle, cvx], [xbb], lambda h: h.tensor_tensor(out=xbb[:, :, :nt], in0=sle[:, :, :nt],
                                                                             in1=cvx[:, :, :nt], op=ALU.mult))
                        pt = ptb.next()
                        def f(h, pt=pt):
                            for c in range(6):
                                ins = h.transpose(out=pt[:nt, c * 128:(c + 1) * 128], in_=xbb[:, c, :nt], identity=idb[:, :])
                            return ins
                        A("pe", [xbb, idb], [pt], f)
                        A("act", [pt], [xtok], lambda h, pt=pt: h.activation(out=xtok[:nt, :], in_=pt[:nt, 0:768], func=AF.Copy))
                        pt = transpose_blocks(qc, nt, 4, None)
                        A("act", [pt], [qcT], lambda h, pt=pt: h.activation(
                            out=qcT[:, :, :nt], in_=pt[:, 0:512].rearrange("p (c t) -> p c t", c=4)[:, :, :nt], func=AF.Copy))
                        pt = transpose_blocks(kc16, nt, 4, None)
                        A("act", [pt], [kcT], lambda h, pt=pt: h.activation(
                            out=kcT[:, :, qi * 128:qi * 128 + nt],
                            in_=pt[:, 0:512].rearrange("p (c t) -> p c t", c=4)[:, :, :nt], func=AF.Copy))
                        pc = psf.next()
                        def f(h, pc=pc):
                            h.matmul(pc[:nt, 0:8], rmask[:nt, :nt], dts[:nt, 8:16], start=True, stop=True)
                            h.matmul(pc[:nt, 8:16], trigt[:nt, :nt], dts[:nt, 8:16], start=True, stop=True)
                            return h.matmul(pc[:, 16:24], ones[:nt, :], dts[:nt, 8:16], start=True, stop=True)
                        A("pe", [rmask, trigt, ones, dts], [pc], f)
                        A("act", [pc], [dts], lambda h, pc=pc: h.activation(out=dts[:nt, 16:32], in_=pc[:nt, 0:16], func=AF.Exp))
                        A("act", [pc], [dts], lambda h, pc=pc: h.activation(out=dts[:, 32:40], in_=pc[:, 16:24], func=AF.Exp))
                        A("dve", [dts], [dts], lambda h: h.tensor_tensor(out=dts[:nt, 24:32], in0=dts[:nt, 24:32],
                                                                        in1=dts[:nt, 0:8], op=ALU.mult))
                        pcb = psf.next()
                        def f(h, pcb=pcb):
                            for g in range(2):
                                ins = h.matmul(pcb[:nt, g * 128:g * 128 + nt], xbb[:, 4 + g, :nt], xbb[:, 6 + g, :nt],
                                               start=True, stop=True)
                            return ins
                        A("pe", [xbb], [pcb], f)
                        A("dve", [pcb, rmask], [cbm], lambda h, pcb=pcb: h.tensor_tensor(
                            out=cbm[:nt, :, :nt], in0=pcb[:nt, 0:256].rearrange("p (g t) -> p g t", g=2)[:, :, :nt],
                            in1=rmask[:nt, :nt].unsqueeze(1).to_broadcast([nt, 2, nt]), op=ALU.mult))
                        px = pacc.next()
                        def f(h, px=px):
                            for hh in range(8):
                                ins = h.matmul(px[:nt, hh * 64:(hh + 1) * 64], xbb[:, 6 + hh // 4, :nt],
                                               stTb[:, hh * 64:(hh + 1) * 64], start=True, stop=True)
                            return ins
                        A("pe", [xbb, stTb], [px], f)
                        A("dve", [rmask, dts], [Rm], lambda h: h.tensor_tensor(
                            out=Rm[:nt, :, :nt], in0=rmask[:nt, :nt].unsqueeze(1).to_broadcast([nt, 8, nt]),
                            in1=dts[:nt, 8:16].unsqueeze(2).to_broadcast([nt, 8, nt]), op=ALU.mult))
                        for half in range(2):
                            pg = psf.next()
                            def f(h, pg=pg, half=half):
                                for hh in range(4):
                                    ins = h.matmul(pg[:nt, hh * 128:hh * 128 + nt], trigt[:nt, :nt], Rm[:nt, half * 4 + hh, :nt],
                                                   start=True, stop=True)
                                return ins
                            A("pe", [trigt, Rm], [pg], f)
                            A("act", [pg], [Eh], lambda h, pg=pg, half=half: h.activation(
                                out=Eh[:nt, half * 4:(half + 1) * 4, :nt],
                                in_=pg[:nt, :].rearrange("p (c t) -> p c t", c=4)[:, :, :nt], func=AF.Exp))
                        A("dve", [Eh, dts], [Eh], lambda h: h.tensor_tensor(
                            out=Eh[:nt, :, :nt], in0=Eh[:nt, :, :nt],
                            in1=dts[:nt, 0:8].unsqueeze(2).to_broadcast([nt, 8, nt]), op=ALU.mult))
                        for g in range(2):
                            A("dve", [Eh, cbm], [WT], lambda h, g=g: h.tensor_tensor(
                                out=WT[:nt, g * 4:(g + 1) * 4, :nt], in0=Eh[:nt, g * 4:(g + 1) * 4, :nt],
                                in1=cbm[:nt, g:g + 1, :nt].to_broadcast([nt, 4, nt]), op=ALU.mult))
                        py = pacc.next()
                        def f(h, py=py):
                            for hh in range(8):
                                ins = h.matmul(py[:nt, hh * 64:(hh + 1) * 64], WT[:nt, hh, :nt], xtok[:nt, hh * 64:(hh + 1) * 64],
                                               start=True, stop=True)
                            return ins
                        A("pe", [WT, xtok], [py], f)
                        A("act", [py], [yin], lambda h, py=py: h.activation(out=yin[:nt, :], in_=py[:nt, :], func=AF.Copy))
                        A("dve", [px, dts], [yt_], lambda h, px=px: h.tensor_tensor(
                            out=yt_[:nt, :].rearrange("p (h e) -> p h e", h=8),
                            in0=px[:nt, :].rearrange("p (h e) -> p h e", h=8),
                            in1=dts[:nt, 16:24].unsqueeze(2).to_broadcast([nt, 8, 64]), op=ALU.mult))
                        A("dve", [yt_, yin], [yt_], lambda h: h.tensor_tensor(out=yt_[:nt, :], in0=yt_[:nt, :], in1=yin[:nt, :],
                                                                              op=ALU.add))
                        A("dve", [xtok, dsk], [yin], lambda h: h.tensor_tensor(
                            out=yin[:nt, :].rearrange("p (h e) -> p h e", h=8),
                            in0=xtok[:nt, 0:512].rearrange("p (h e) -> p h e", h=8),
                            in1=dsk[:nt, :].unsqueeze(2).to_broadcast([nt, 8, 64]), op=ALU.mult))
                        A("dve", [yt_, yin], [yt_], lambda h: h.tensor_tensor(out=yt_[:nt, :], in0=yt_[:nt, :], in1=yin[:nt, :],
                                                                              op=ALU.add))
                        A("dve", [xtok, dts], [xw], lambda h: h.tensor_tensor(
                            out=xw[:nt, :].rearrange("p (h e) -> p h e", h=8),
                            in0=xtok[:nt, 0:512].rearrange("p (h e) -> p h e", h=8),
                            in1=dts[:nt, 24:32].unsqueeze(2).to_broadcast([nt, 8, 64]), op=ALU.mult))
                        pd = psf.next()
                        def f(h, pd=pd):
                            for hh in range(8):
                                g = hh // 4
                                ins = h.matmul(pd[:, hh * 64:(hh + 1) * 64], xtok[:nt, 512 + g * 128:512 + (g + 1) * 128],
                                               xw[:nt, hh * 64:(hh + 1) * 64], start=True, stop=True)
                            return ins
                        A("pe", [xtok, xw], [pd], f)
                        A("dve", [stT, dts], [stT], lambda h: h.tensor_tensor(
                            out=stT[:].rearrange("p (h e) -> p h e", h=8), in0=stT[:].rearrange("p (h e) -> p h e", h=8),
                            in1=dts[:, 32:40].unsqueeze(2).to_broadcast([128, 8, 64]), op=ALU.mult))
                        A("dve", [pd, stT], [stT], lambda h, pd=pd: h.tensor_tensor(out=stT[:], in0=pd[:], in1=stT[:], op=ALU.add))
                        A("act", [stT], [stTb], lambda h: h.activation(out=stTb[:], in_=stT[:], func=AF.Copy))
                        A("dve", [yt_, zs], [yt_], lambda h: h.tensor_tensor(out=yt_[:nt, :], in0=yt_[:nt, :], in1=zs[:nt, :],
                                                                            op=ALU.mult))
                        A("act", [yt_], [gsq], lambda h: h.activation(out=gsq[:nt, :], in_=yt_[:nt, :], func=AF.Square))
                        A("dve", [gsq], [gst], lambda h: h.tensor_reduce(
                            out=gst[:nt, 0:2], in_=gsq[:nt, :].rearrange("p (g e) -> p g e", g=2), axis=AX.X, op=ALU.add))
                        A("act", [gst], [gst], lambda h: h.activation(out=gst[:nt, 0:2], in_=gst[:nt, 0:2], func=AF.Sqrt,
                                                                      scale=1.0 / 256, bias=EPS))
                        A("dve", [gst], [gst], lambda h: h.reciprocal(out=gst[:nt, 0:2], in_=gst[:nt, 0:2]))
                        A("dve", [yt_, gst], [yt_], lambda h: h.tensor_tensor(
                            out=yt_[:nt, :].rearrange("p (g e) -> p g e", g=2), in0=yt_[:nt, :].rearrange("p (g e) -> p g e", g=2),
                            in1=gst[:nt, 0:2].unsqueeze(2).to_broadcast([nt, 2, 256]), op=ALU.mult))
                        A("dve", [yt_, dng], [mix], lambda h: h.tensor_tensor(out=mix[:nt, 512:1024], in0=yt_[:nt, :],
                                                                              in1=dng[:nt, :], op=ALU.mult))
                        jlo = max(0, qi - 4)
                        if samp:
                            groups = [[4, 5, 6, 7], [8]]
                        else:
                            js = list(range(jlo, qi + 1))
                            groups = [js[:4], js[4:]] if len(js) > 4 else [js]
                        pas = {}
                        def qk_c(c4, i_, g):
                            u = 2 * c4 + i_
                            pS = psf.next()
                            def f(h):
                                for jj, j_ in enumerate(g):
                                    dl = j_ - qi
                                    extra = dl >= -1 or dl == -4
                                    ins = h.matmul(pS[:, jj * 128:jj * 128 + nt],
                                                   kcT[i_ * 64:(i_ + 1) * 64, c4, j_ * 128:(j_ + 1) * 128],
                                                   qcT[i_ * 64:(i_ + 1) * 64, c4, :nt], start=True, stop=not extra)
                                    if dl >= -1:
                                        ins = h.matmul(pS[:, jj * 128:jj * 128 + nt], idb[:, :],
                                                       bRel[:, u, -dl, :nt], start=False, stop=True)
                                    elif dl == -4:
                                        ins = h.matmul(pS[:, jj * 128:jj * 128 + nt], idb[:, :],
                                                       bM4[:, :nt], start=False, stop=True)
                                return ins
                            A("pe", [kcT, qcT, idb, bRel, bM4], [pS], f)
                            pt_ = pTt.next()
                            ng = len(g)
                            A("act", [pS, crel], [pt_], lambda h: h.activation(
                                out=pt_[:, 0:ng, :nt],
                                in_=pS[:, 0:ng * 128].rearrange("p (g t) -> p g t", g=ng)[:, :, :nt],
                                func=AF.Exp, bias=crel[:, u:u + 1]))
                            return pt_

                        def pv_c(c4, i_, g, pt_):
                            u = 2 * c4 + i_
                            if c4 not in pas:
                                pas[c4] = pacc.next()
                            pa = pas[c4]
                            def f(h):
                                for jj, j_ in enumerate(g):
                                    ins = h.matmul(pa[:nt, i_ * 256:i_ * 256 + 65], pt_[:, jj, :nt],
                                                   Vc[:, j_ % 8, u, 0:65], start=(j_ == groups[0][0]), stop=(j_ == qi))
                                return ins
                            A("pe", [pt_, Vc], [pa], f)
                            if i_ == 1 and g is groups[-1]:
                                pav = pa[:nt, 0:512].rearrange("p (i e) -> p i e", i=2)
                                A("dve", [pa], [rc], lambda h: h.reciprocal(out=rc[:nt, 0:2], in_=pav[:, :, 64]))
                                for ii in range(2):
                                    uu = 2 * c4 + ii
                                    A("dve", [pa, rc], [mix], lambda h, ii=ii, uu=uu: h.tensor_scalar(
                                        out=mix[:nt, uu * 64:(uu + 1) * 64], in0=pa[:nt, ii * 256:ii * 256 + 64],
                                        scalar1=rc[:nt, ii:ii + 1], scalar2=None, op0=ALU.mult))

                        items = [(c4, i_, g) for c4 in range(4) for i_ in range(2) for g in groups]
                        prev = None
                        for it in items:
                            cur = qk_c(*it)
                            if prev is not None:
                                pv_c(*prev)
                            prev = it + (cur,)
                        pv_c(*prev)
                        pt = transpose_blocks(mix, nt, 8, None)
                        A("act", [pt], [mT], lambda h, pt=pt: h.activation(
                            out=mT[:, :, :nt], in_=pt[:, :].rearrange("p (c t) -> p c t", c=8)[:, :, :nt], func=AF.Copy))
                        xn_ = xo.next()
                        for blk in range(2):
                            pp = psf.next()
                            def f(h, pp=pp, blk=blk):
                                for c in range(8):
                                    ins = h.matmul(pp[:nt, :], mT[:, c, :nt], wOut[:, c, blk * 512:(blk + 1) * 512],
                                                   start=(c == 0), stop=(c == 7))
                                return ins
                            A("pe", [mT, wOut], [pp], f)
                            A("dve", [pp, G1], [ot], lambda h, pp=pp, blk=blk: h.tensor_tensor(
                                out=ot[:nt, blk * 512:(blk + 1) * 512], in0=pp[:nt, :],
                                in1=G1[:nt, blk * 512:(blk + 1) * 512], op=ALU.mult))
                        A("dve", [ot, xt], [xn_], lambda h, xn_=xn_, xt=xt: h.tensor_tensor(
                            out=xn_[:nt, :], in0=ot[:nt, :], in1=xt[:nt, :], op=ALU.add))
                        fw.store("sp", xn_, xs[r0:r0 + nt, :], xn_[:nt, :], dk=xtrk(r0))
                    dst = O["dconv_s"] if samp else O["dconv_p"][s]
                    for c in range(8):
                        fw.store("sp", xo3, dst[:, c * 128:(c + 1) * 128].rearrange("t p -> p t"), xo3[:, c, :],
                                 allow_slow_non_contiguous=True)
                    dst = O["dssm_s"] if samp else O["dssm_p"][s]
                    for half in range(2):
                        pp = psf.next()
                        def f(h, pp=pp, half=half):
                            for hh in range(4):
                                ins = h.matmul(pp[0:64, hh * 128:(hh + 1) * 128],
                                               stT[:, (half * 4 + hh) * 64:(half * 4 + hh + 1) * 64], idf[:, :],
                                               start=True, stop=True)
                            return ins
                        A("pe", [stT, idf], [pp], f)
                        A("act", [pp], [sld], lambda h, pp=pp, half=half: h.activation(
                            out=sld[:, half * 4:(half + 1) * 4, :], in_=pp[0:64, :].rearrange("p (h n) -> p h n", h=4),
                            func=AF.Copy))
                    fw.store("sp", sld, dst.rearrange("h p n -> p h n"), sld[:])
                fw.barrier()
              fw.release_phase()

            if 'F' in PH:
                ffn_phase(0)
            if 'C' in PH:
                cd_phase()
            if 'G' in PH:
                ffn_phase(1)

            with ExitStack() as ph:
              if 'Z' in PH:
                fw.ts = ph
                fg = fw.sb("fg", [128, D], F32)
                fw.load("sp", fg, fg[:], I["final_g"].partition_broadcast(128))
                xrot = Rot([fw.sb(f"xz{i}", [128, D], F32) for i in range(2)])
                yo = Rot([fw.sb(f"yo{i}", [128, D], F32) for i in range(2)])
                scr = fw.sb("scrz", [128, D], F32)
                st4 = fw.sb("st4z", [128, 4], F32)
                for s, (ntok, rp0, samp, row0) in enumerate(SEQS):
                    if str(s) not in SEQSEL:
                        continue
                    ntl = min(MAXT, (ntok + 127) // 128)
                    for ti in range(ntl):
                        nt = min(128, ntok - ti * 128)
                        r0 = row0 + ti * 128
                        def ldxz(tj):
                            ntj = min(128, ntok - tj * 128)
                            rj = row0 + tj * 128
                            xj = xrot.next()
                            fw.load("sp", xj, xj[:ntj, :], xs[rj:rj + ntj, :], dk=xtrk(rj))
                            return xj
                        if ti == 0:
                            xnext = ldxz(0)
                        xt = xnext
                        if ti + 1 < ntl:
                            xnext = ldxz(ti + 1)
                        A("act", [xt], [scr, st4], lambda h: h.activation(out=scr[:nt, :], in_=xt[:nt, :], func=AF.Square,
                                                                         accum_out=st4[:nt, 0:1]))
                        A("act", [st4], [st4], lambda h: h.activation(out=st4[:nt, 1:2], in_=st4[:nt, 0:1], func=AF.Sqrt,
                                                                      scale=1.0 / D, bias=EPS))
                        A("dve", [st4], [st4], lambda h: h.reciprocal(out=st4[:nt, 2:3], in_=st4[:nt, 1:2]))
                        yt = yo.next()
                        A("dve", [xt, st4, fg], [yt], lambda h, yt=yt, xt=xt: h.scalar_tensor_tensor(
                            out=yt[:nt, :], in0=xt[:nt, :], scalar=st4[:nt, 2:3], in1=fg[:nt, :], op0=ALU.mult, op1=ALU.mult))
                        dst = O["y_s"] if samp else O["y_p"]
                        sr0 = 0 if samp else r0
                        fw.store("sp", yt, dst[sr0:sr0 + nt, :], yt[:nt, :])
                fw.barrier()
            fw.release_phase()
    print('OPS', fw.cnt, flush=True)
    return nc


_NC = None


def kernel(**inp):
    global _NC
    f = lambda a: np.ascontiguousarray(np.asarray(a, dtype=np.float32))
    consts = host_consts()
    if _NC is None:
        _NC = build()
    nc = _NC
    shared = {
        "w_mod": f(inp["w_mod"]), "b_mod": f(inp["b_mod"]).reshape(1, -1), "norm_g": f(inp["norm_g"]).reshape(4, D),
        "final_g": f(inp["final_g"]).reshape(1, D), "t5_table": f(inp["t5_table"]),
        "w_in_ab": f(inp["w_in_ab"][0]), "w_out_ab": f(inp["w_out_ab"][0]), "ret_gn": f(inp["ret_gn"]).reshape(1, 512),
        "lam_q": f(inp["lam_q"]).reshape(1, 128), "lam_k": f(inp["lam_k"]).reshape(1, 128),
        "diff_gn": f(inp["diff_gn"]).reshape(1, 128), "w_in_cd": f(inp["w_in_cd"][0]), "w_out_cd": f(inp["w_out_cd"][0]),
        "rel_table": f(inp["rel_table"][0]), "d_conv_w": f(inp["d_conv_w"][0]), "d_conv_b": f(inp["d_conv_b"]).reshape(1, D),
        "d_dt_bias": f(inp["d_dt_bias"]).reshape(1, 8), "d_a_log": f(inp["d_a_log"]).reshape(1, 8),
        "d_skip": f(inp["d_skip"]).reshape(1, 8), "d_norm_g": f(inp["d_norm_g"]).reshape(1, 512),
        "w_up": f(inp["w_up"]), "ffn_cw": f(inp["ffn_conv_w"]).reshape(6, DFF), "ffn_cb": f(inp["ffn_conv_b"]),
        "w_down": f(inp["w_down"]),
    }
    shared.update(consts)
    in_maps = []
    for i in range(8):
        m = dict(shared)
        m["xp"] = f(inp["x_prompt"][2 * i:2 * i + 2]).reshape(4096, D)
        m["xsm"] = f(inp["x_sample"][i])
        m["cc"] = np.concatenate([f(inp["c_prompt"][2 * i:2 * i + 2]), f(inp["c_sample"][i:i + 1])], 0)
        m["ret0"] = f(inp["cache_ret_state"][0, i])
        m["bk_c"] = f(inp["cache_b_k"][0, i]).reshape(1024, 512)
        m["bv_c"] = f(inp["cache_b_v"][0, i]).reshape(1024, 512)
        m["ck_c"] = f(inp["cache_c_k"][0, i]).reshape(512, 512)
        m["cv_c"] = f(inp["cache_c_v"][0, i]).reshape(512, 512)
        m["dconv_c"] = f(inp["state_d_conv"][0, i])
        m["dssm_c"] = f(inp["state_d_ssm"][0, i])
        m["ffnc"] = f(inp["state_ffn_conv"][:, i])
        in_maps.append(m)
    res = run_bass_kernel_spmd(nc, in_maps[:NCORES], core_ids=list(range(NCORES)))
    R = list(res.results) * (8 // NCORES)
    cat = lambda k: np.concatenate([np.asarray(R[i][k]) for i in range(8)], 0)
    stk = lambda k: np.stack([np.asarray(R[i][k]) for i in range(8)], 0)
    y_p = cat("y_p").reshape(16, 2048, D)
    y_s = stk("y_s")
    ret_p = cat("ret_p")[None]
    ret_s = stk("ret_s")[None]
    bk_p = cat("bk_p").reshape(1, 16, 2048, 4, 128)
    bk_s = stk("bk_s").reshape(1, 8, 64, 4, 128)
    bv_p = cat("bv_p").reshape(1, 16, 2048, 4, 128)
    bv_s = stk("bv_s").reshape(1, 8, 64, 4, 128)
    ck_p = cat("ck_p").reshape(1, 16, 512, 8, 64)
    ck_s = stk("ck_s").reshape(1, 8, 64, 8, 64)
    cv_p = cat("cv_p").reshape(1, 16, 512, 8, 64)
    cv_s = stk("cv_s").reshape(1, 8, 64, 8, 64)
    dconv_p = cat("dconv_p")[None]
    dconv_s = stk("dconv_s")[None]
    dssm_p = cat("dssm_p")[None]
    dssm_s = stk("dssm_s")[None]
    ffn_p = np.concatenate([np.asarray(R[i]["ffn_p"]) for i in range(8)], 1)
    ffn_s = np.stack([np.asarray(R[i]["ffn_s"]) for i in range(8)], 1)
    return (y_p, y_s, ret_p, ret_s, bk_p, bk_s, bv_p, bv_s, ck_p, ck_s, cv_p, cv_s,
            dconv_p, dconv_s, dssm_p, dssm_s, ffn_p, ffn_s)
```
